# Optimizing a Trainium2 kernel written in Bass

```python
import math
import jax, jax.numpy as jnp
from jax import lax
import numpy as np

D_MODEL = 2048
BATCH = 16
SEQ = 2048
DEPTH = 4

CHUNK = 64
Q_BLOCK = 128
N_ATT_HEADS = 8
HEAD_DIM = 64
V_HEAD_DIM = 2 * HEAD_DIM
D_ATT_QK = N_ATT_HEADS * 2 * HEAD_DIM
D_ATT_V = N_ATT_HEADS * V_HEAD_DIM
D_SSM = D_MODEL // 2
SSM_GROUP = 16
N_SSM_GROUPS = D_SSM // SSM_GROUP
SSM_STATE = 64
DT_MIN = 1e-3
DT_MAX = 1e-1
D_FF = 5504
CONV_W = 3
N_BRANCH = 2
IN_COLS = 2 * D_ATT_QK + D_ATT_V + D_SSM + N_BRANCH * D_MODEL
EPS = 1e-6

kernel_name = "hybrid_diffattn_s5_convffn_streaming"


def rmsnorm(x, g):
    xf = x.astype(jnp.float32)
    y = xf * lax.rsqrt(jnp.mean(xf * xf, axis=-1, keepdims=True) + EPS)
    return (y * g.astype(jnp.float32)).astype(x.dtype)


def alibi_slopes():
    h = jnp.arange(1, N_ATT_HEADS + 1, dtype=jnp.float32)
    return 2.0 ** (-8.0 * h / N_ATT_HEADS)


def diff_attention(q, k, v, lam):
    b_, s_ = q.shape[0], q.shape[1]
    nblk = s_ // Q_BLOCK
    scale = HEAD_DIM ** -0.5
    kt = k.transpose(0, 2, 3, 1, 4)
    vt = v.transpose(0, 2, 1, 3)
    qb = (q * scale).reshape(b_, nblk, Q_BLOCK, N_ATT_HEADS, 2, HEAD_DIM)
    qb = qb.transpose(1, 0, 3, 4, 2, 5)
    slopes = alibi_slopes()
    kpos = jnp.arange(s_, dtype=jnp.int32)

    def block(args):
        qblk, bi = args
        qpos = bi * Q_BLOCK + jnp.arange(Q_BLOCK, dtype=jnp.int32)
        chunk_end = (qpos // CHUNK + 1) * CHUNK
        visible = kpos[None, :] < chunk_end[:, None]
        dist = jnp.abs(qpos[:, None] - kpos[None, :]).astype(jnp.float32)
        bias = jnp.where(visible[None], -slopes[:, None, None] * dist[None], -jnp.inf)
        s = jnp.einsum('bhcqd,bhcsd->bhcqs', qblk, kt).astype(jnp.float32) + bias[None, :, None]
        p = jax.nn.softmax(s, axis=-1)
        w = (p[:, :, 0] - lam * p[:, :, 1]).astype(vt.dtype)
        return jnp.einsum('bhqs,bhsd->bhqd', w, vt)

    o = lax.map(block, (qb, jnp.arange(nblk, dtype=jnp.int32)))
    return o.transpose(1, 0, 3, 2, 4).reshape(b_, s_, N_ATT_HEADS, V_HEAD_DIM)


def _complex_affine_combine(earlier, later):
    ar1, ai1, br1, bi1 = earlier
    ar2, ai2, br2, bi2 = later
    return (ar2 * ar1 - ai2 * ai1,
            ar2 * ai1 + ai2 * ar1,
            ar2 * br1 - ai2 * bi1 + br2,
            ar2 * bi1 + ai2 * br1 + bi2)


def s5_ssm(u, a_re, a_im, log_dt, b_re, b_im, c_re, c_im, d_skip):
    b_, s_, _ = u.shape
    ug = u.reshape(b_, s_, N_SSM_GROUPS, SSM_GROUP)
    dt = jnp.exp(log_dt)[:, None]
    mag = jnp.exp(dt * a_re)
    ab_re = mag * jnp.cos(dt * a_im)
    ab_im = mag * jnp.sin(dt * a_im)
    den = a_re * a_re + a_im * a_im
    zr = ab_re - 1.0
    zi = ab_im
    f_re = (zr * a_re + zi * a_im) / den
    f_im = (zi * a_re - zr * a_im) / den
    bb_re = f_re[..., None] * b_re - f_im[..., None] * b_im
    bb_im = f_re[..., None] * b_im + f_im[..., None] * b_re
    x_re = jnp.einsum('bsgi,gpi->sbgp', ug, bb_re)
    x_im = jnp.einsum('bsgi,gpi->sbgp', ug, bb_im)
    a_re_s = jnp.broadcast_to(ab_re[None, None], (s_, 1) + ab_re.shape)
    a_im_s = jnp.broadcast_to(ab_im[None, None], (s_, 1) + ab_im.shape)
    _, _, h_re, h_im = lax.associative_scan(
        _complex_affine_combine, (a_re_s, a_im_s, x_re, x_im), axis=0)
    y = (jnp.einsum('sbgp,gop->bsgo', h_re, c_re)
         - jnp.einsum('sbgp,gop->bsgo', h_im, c_im))
    y = y + d_skip.reshape(N_SSM_GROUPS, SSM_GROUP) * ug
    return y.reshape(b_, s_, D_SSM)


def causal_dwconv(x, w, b):
    s_ = x.shape[1]
    xp = jnp.pad(x, ((0, 0), (CONV_W - 1, 0), (0, 0)))
    y = b + xp[:, 0:s_] * w[0]
    for j in range(1, CONV_W):
        y = y + xp[:, j:j + s_] * w[j]
    return y


def setup_inputs(seed: int = 0) -> dict:
    key = jax.random.key(seed)
    ks = jax.random.split(key, 32)
    L, D = DEPTH, D_MODEL
    G, P, I = N_SSM_GROUPS, SSM_STATE, SSM_GROUP
    f32 = jnp.float32
    nrm = lambda k, shape, s: (jax.random.normal(k, shape, f32) * s)
    a_im = jnp.broadcast_to(jnp.pi * jnp.arange(P, dtype=f32), (L, G, P))
    log_dt = jax.random.uniform(ks[10], (L, G), f32, math.log(DT_MIN), math.log(DT_MAX))
    return {
        "x": nrm(ks[0], (BATCH, SEQ, D), 1.0),
        "norm1_g": 1.0 + nrm(ks[1], (L, D), 0.02),
        "w_in": nrm(ks[2], (L, D, IN_COLS), D ** -0.5),
        "q_norm_g": 1.0 + nrm(ks[3], (L, HEAD_DIM), 0.02),
        "k_norm_g": 1.0 + nrm(ks[4], (L, HEAD_DIM), 0.02),
        "lambda_q1": nrm(ks[5], (L, HEAD_DIM), 0.1),
        "lambda_k1": nrm(ks[6], (L, HEAD_DIM), 0.1),
        "lambda_q2": nrm(ks[7], (L, HEAD_DIM), 0.1),
        "lambda_k2": nrm(ks[8], (L, HEAD_DIM), 0.1),
        "subln_g": 1.0 + nrm(ks[9], (L, V_HEAD_DIM), 0.02),
        "ssm_a_re": -0.5 + nrm(ks[11], (L, G, P), 0.01),
        "ssm_a_im": a_im + 0.0,
        "ssm_log_dt": log_dt,
        "ssm_b_re": nrm(ks[12], (L, G, P, I), (2 * I) ** -0.5),
        "ssm_b_im": nrm(ks[13], (L, G, P, I), (2 * I) ** -0.5),
        "ssm_c_re": nrm(ks[14], (L, G, I, P), (2 * P) ** -0.5),
        "ssm_c_im": nrm(ks[15], (L, G, I, P), (2 * P) ** -0.5),
        "ssm_d": nrm(ks[16], (L, D_SSM), 1.0),
        "ssm_glu_w": nrm(ks[17], (L, D_SSM, D_SSM), D_SSM ** -0.5),
        "ssm_glu_b": nrm(ks[18], (L, D_SSM), 0.01),
        "w_branch_attn": nrm(ks[19], (L, D_ATT_V, D), D_ATT_V ** -0.5),
        "w_branch_ssm": nrm(ks[20], (L, D_SSM, D), D_SSM ** -0.5),
        "w_out": nrm(ks[21], (L, D, D), D ** -0.5),
        "norm2_g": 1.0 + nrm(ks[22], (L, D), 0.02),
        "ffn_w_up": nrm(ks[23], (L, D, 2 * D_FF), D ** -0.5),
        "ffn_conv_w": nrm(ks[24], (L, CONV_W, 2 * D_FF), CONV_W ** -0.5),
        "ffn_conv_b": nrm(ks[25], (L, 2 * D_FF), 0.01),
        "ffn_w_down": nrm(ks[26], (L, D_FF, D), D_FF ** -0.5),
    }


def reference(x, norm1_g, w_in, q_norm_g, k_norm_g, lambda_q1, lambda_k1, lambda_q2,
              lambda_k2, subln_g, ssm_a_re, ssm_a_im, ssm_log_dt, ssm_b_re, ssm_b_im,
              ssm_c_re, ssm_c_im, ssm_d, ssm_glu_w, ssm_glu_b, w_branch_attn,
              w_branch_ssm, w_out, norm2_g, ffn_w_up, ffn_conv_w, ffn_conv_b, ffn_w_down):
    b_, s_, _ = x.shape
    splits = [D_ATT_QK, 2 * D_ATT_QK, 2 * D_ATT_QK + D_ATT_V, 2 * D_ATT_QK + D_ATT_V + D_SSM,
              2 * D_ATT_QK + D_ATT_V + D_SSM + D_MODEL]
    for l in range(DEPTH):
        lam_init = 0.8 - 0.6 * math.exp(-0.3 * l)
        h = rmsnorm(x, norm1_g[l])
        proj = h @ w_in[l]
        q, k, v, u, g_att, g_ssm = jnp.split(proj, splits, axis=-1)
        q = rmsnorm(q.reshape(b_, s_, N_ATT_HEADS, 2, HEAD_DIM), q_norm_g[l])
        k = rmsnorm(k.reshape(b_, s_, N_ATT_HEADS, 2, HEAD_DIM), k_norm_g[l])
        v = v.reshape(b_, s_, N_ATT_HEADS, V_HEAD_DIM)
        lam = (jnp.exp(jnp.sum(lambda_q1[l].astype(jnp.float32) * lambda_k1[l].astype(jnp.float32)))
               - jnp.exp(jnp.sum(lambda_q2[l].astype(jnp.float32) * lambda_k2[l].astype(jnp.float32)))
               + lam_init)
        o = diff_attention(q, k, v, lam)
        o = rmsnorm(o, subln_g[l]) * (1.0 - lam_init)
        o_att = o.reshape(b_, s_, D_ATT_V) @ w_branch_attn[l]

        y = s5_ssm(u, ssm_a_re[l], ssm_a_im[l], ssm_log_dt[l], ssm_b_re[l], ssm_b_im[l],
                   ssm_c_re[l], ssm_c_im[l], ssm_d[l])
        y = jax.nn.gelu(y)
        y = y * jax.nn.sigmoid(y @ ssm_glu_w[l] + ssm_glu_b[l])
        o_ssm = y @ w_branch_ssm[l]

        mixed = jax.nn.sigmoid(g_att) * o_att + jax.nn.sigmoid(g_ssm) * o_ssm
        x = x + mixed @ w_out[l]

        h = rmsnorm(x, norm2_g[l])
        up = causal_dwconv(h @ ffn_w_up[l], ffn_conv_w[l], ffn_conv_b[l])
        a, val = jnp.split(up, 2, axis=-1)
        x = x + (jax.nn.gelu(a) * val) @ ffn_w_down[l]
    return x
```

```python
import math
from contextlib import ExitStack
import numpy as np
import concourse.bass as bass
import concourse.mybir as mybir
from concourse.bass_utils import run_bass_kernel_spmd

F32 = mybir.dt.float32
BF16 = mybir.dt.bfloat16
AF = mybir.ActivationFunctionType
ALU = mybir.AluOpType
AX = mybir.AxisListType

NCORES = 8
D = 2048
NT = 4096
SEQ = 2048
DEPTH = 4
DFF = 5504
EPS = 1e-6
NFT = DFF // 128
KGROUPS = [(0, 15), (15, 29), (29, 43)]


class Res:
    __slots__ = ("name", "w", "rs")

    def __init__(self, name):
        self.name = name
        self.w = None
        self.rs = []


class Sched:
    def __init__(self, nc, stack):
        self.nc = nc
        self.stack = stack
        self.eng = {"pe": nc.tensor, "act": nc.scalar, "dve": nc.vector, "pool": nc.gpsimd, "sp": nc.sync}
        self.sem = {}
        self.cnt = {}
        self.waited = {e: {} for e in self.eng}
        self.pending = {e: [] for e in self.eng}
        self.dma_sems = []
        self.free_dsems = []
        self.ndsem = 0
        self.cur = {}
        self.gen = 0
        self.rotate()
        self.nres = 0

    def rotate(self):
        self.gen += 1
        for e in ("pe", "act", "dve", "pool"):
            n = f"c_{e}_{self.gen}"
            self._newsem(n)
            self.cur[e] = n

    def _newsem(self, name):
        self.sem[name] = self.stack.enter_context(self.nc.semaphore(name))
        self.cnt[name] = 0

    def res(self, name=None):
        self.nres += 1
        return Res(name or f"r{self.nres}")

    def dsem(self, name=None):
        if self.free_dsems:
            n = self.free_dsems.pop()
        else:
            self.ndsem += 1
            n = f"d{self.ndsem}"
            self._newsem(n)
        self.dma_sems.append(n)
        return n

    def _wait(self, e, ev):
        if ev is None:
            return
        s, v = ev
        if e == "pe" and s.startswith("c_pe"):
            return
        if self.waited[e].get(s, 0) >= v:
            return
        self.eng[e].wait_ge(self.sem[s], v)
        self.waited[e][s] = v

    def _deps(self, e, reads, writes):
        for r in reads:
            self._wait(e, r.w)
        for w in writes:
            self._wait(e, w.w)
            for ev in w.rs:
                self._wait(e, ev)

    def op(self, e, fn, reads=(), writes=(), sig=True):
        self._deps(e, reads, writes)
        ins = fn(self.eng[e])
        if not sig:
            self.pending[e].append((tuple(reads), tuple(writes)))
            return None
        s = self.cur[e]
        self.cnt[s] += 1
        ins.then_inc(self.sem[s], 1)
        ev = (s, self.cnt[s])
        for (prs, pws) in self.pending[e]:
            for r in prs:
                r.rs.append(ev)
            for w in pws:
                w.w = ev
                w.rs = []
        self.pending[e] = []
        for r in reads:
            r.rs.append(ev)
        for w in writes:
            w.w = ev
            w.rs = []
        return ev

    def dma(self, q, out, in_, sem, reads=(), writes=(), **kw):
        self._deps(q, reads, writes)
        ins = self.eng[q].dma_start(out=out, in_=in_, **kw)
        self.cnt[sem] += 16
        ins.then_inc(self.sem[sem], 16)
        ev = (sem, self.cnt[sem])
        for r in reads:
            r.rs.append(ev)
        for w in writes:
            w.w = ev
            w.rs = []
        return ev

    def barrier(self):
        for s in self.dma_sems:
            if self.cnt[s] > self.waited["sp"].get(s, 0):
                self.eng["sp"].wait_ge(self.sem[s], self.cnt[s])
        self.nc.all_engine_barrier()
        for e in self.eng:
            for s in self.cnt:
                self.waited[e][s] = self.cnt[s]
        self.free_dsems.extend(self.dma_sems)
        self.dma_sems = []


def dap(t, off, pat):
    return bass.AP(t, off, [list(p) for p in pat])


def build_program(nlayers=DEPTH, dbg=None, PH=("n1", "ip", "at", "ss", "mx", "n2", "fu", "fd")):
    nc = bass.Bass("TRN2", target_bir_lowering=False)
    stack = ExitStack()
    S = Sched(nc, stack)
    L = DEPTH

    def din(name, shape, dt=F32):
        return nc.dram_tensor(name, list(shape), dt, kind="ExternalInput")

    def dscr(name, shape, dt, out=False):
        return nc.dram_tensor(name, list(shape), dt, kind=("ExternalOutput" if out else "Internal"))

    x_in = din("x", [NT, D])
    y_out = nc.dram_tensor("y", [NT, D], F32, kind="ExternalOutput")
    norm1_g = din("norm1_g", [L, D]); norm2_g = din("norm2_g", [L, D])
    w_in = din("w_in", [L, D, 8192])
    q_norm_g = din("q_norm_g", [L, 64]); k_norm_g = din("k_norm_g", [L, 64])
    lam_q1 = din("lambda_q1", [L, 64]); lam_k1 = din("lambda_k1", [L, 64])
    lam_q2 = din("lambda_q2", [L, 64]); lam_k2 = din("lambda_k2", [L, 64])
    subln_g = din("subln_g", [L, 128])
    a_re = din("ssm_a_re", [L, 64, 64]); a_im = din("ssm_a_im", [L, 64, 64])
    log_dt = din("ssm_log_dt", [L, 64])
    b_re = din("ssm_b_re", [L, 64, 64, 16]); b_im = din("ssm_b_im", [L, 64, 64, 16])
    c_re = din("ssm_c_re", [L, 64, 16, 64]); c_im = din("ssm_c_im", [L, 64, 16, 64])
    ssm_d = din("ssm_d", [L, 1024])
    glu_w = din("ssm_glu_w", [L, 1024, 1024]); glu_b = din("ssm_glu_b", [L, 1024])
    w_ba = din("w_branch_attn", [L, 1024, D]); w_bs = din("w_branch_ssm", [L, 1024, D])
    w_out = din("w_out", [L, D, D])
    w_up = din("ffn_w_up", [L, D, 2 * DFF]); conv_w = din("ffn_conv_w", [L, 3, 2 * DFF])
    conv_b = din("ffn_conv_b", [L, 2 * DFF]); w_down = din("ffn_w_down", [L, DFF, D])
    c_ident = din("c_ident", [128, 128])
    c_alibi = din("c_alibi", [128, 5, 512])
    c_blk = din("c_blk", [128, 128])
    c_pf = din("c_pf", [128, 513])
    c_aug = din("c_aug", [8, 2, SEQ], BF16)

    isdbg = lambda n: dbg is not None and n in dbg
    XT = dscr("XT", [16, 128, NT], F32, isdbg("XT"))
    H = dscr("H", [16, 128, NT], BF16, isdbg("H"))
    QT = dscr("QT", [8, 128, NT], BF16, isdbg("QT"))
    KT = dscr("KT", [8, 128, NT], BF16, isdbg("KT"))
    V = dscr("V", [NT, 1024], BF16, isdbg("V"))
    U2 = dscr("U2", [64, 8, 16, 512], BF16, isdbg("U2"))
    GA = dscr("GA", [16, 128, NT], BF16, isdbg("GA"))
    GS = dscr("GS", [16, 128, NT], BF16, isdbg("GS"))
    OT = dscr("OT", [8, 128, NT], BF16, isdbg("OT"))
    Y2 = dscr("Y2", [64, 8, 16, 512], BF16, isdbg("Y2"))
    AT = dscr("AT", [NFT, 128, NT], BF16, isdbg("AT"))

    PS = [nc.alloc_psum_tensor(f"ps{i}", [128, 512], F32) for i in range(8)]
    PSR = [S.res(f"ps{i}") for i in range(8)]

    _uid = [0]

    def sb(st, name, shape, dt):
        _uid[0] += 1
        return st.enter_context(nc.sbuf_tensor(f"{name}_{_uid[0]}", list(shape), dt))

    ident = sb(stack, "ident", [128, 128], F32)
    identb = sb(stack, "identb", [128, 128], BF16)
    onesb = sb(stack, "onesb", [128, 128], BF16)
    blkb = sb(stack, "blkb", [128, 128], BF16)
    blkf = sb(stack, "blkf", [128, 128], F32)
    epsc = sb(stack, "epsc", [128, 1], F32)
    r_const = S.res("const")
    s_const = S.dsem("d_const")
    S.dma("sp", ident[:], c_ident.ap(), s_const, writes=[r_const])
    S.dma("sp", blkf[:], c_blk.ap(), s_const, writes=[r_const])
    S.op("dve", lambda e: e.tensor_copy(out=identb[:], in_=ident[:]), reads=[r_const], writes=[r_const])
    S.op("dve", lambda e: e.tensor_copy(out=blkb[:], in_=blkf[:]), reads=[r_const], writes=[r_const])
    S.op("dve", lambda e: e.memset(onesb[:], 1.0), writes=[r_const])
    S.op("dve", lambda e: e.memset(epsc[:], EPS), writes=[r_const])
    S.barrier()

    def phase_transpose_in():
        with ExitStack() as st:
            xin = [sb(st, f"p0x{i}", [128, 4, D], F32) for i in range(2)]
            xo = [sb(st, f"p0o{i}", [128, 16, 512], F32) for i in range(2)]
            r_in = [S.res() for _ in range(2)]; r_o = [S.res() for _ in range(2)]
            s_in = [S.dsem(f"p0si{i}") for i in range(2)]; s_o = [S.dsem(f"p0so{i}") for i in range(2)]
            xv = x_in.ap().rearrange("(n s p) d -> n p s d", s=4, p=128)
            XTv = XT.ap().rearrange("c p n -> p c n")
            for j in range(8):
                b = j % 2
                S.dma("sp", xin[b][:], xv[j], s_in[b], writes=[r_in[b]])
                for c in range(16):
                    pb = c % 8
                    for s4 in range(4):
                        last = s4 == 3
                        S.op("pe", lambda e, s4=s4, c=c, pb=pb: e.transpose(
                            out=PS[pb][:, s4 * 128:(s4 + 1) * 128], in_=xin[b][:, s4, c * 128:(c + 1) * 128],
                            identity=ident[:]), reads=[r_in[b]], writes=[PSR[pb]], sig=last)
                    ee = "act" if c % 2 == 0 else "dve"
                    if ee == "act":
                        S.op("act", lambda e, c=c, pb=pb: e.activation(out=xo[b][:, c, :], in_=PS[pb][:], func=AF.Copy),
                             reads=[PSR[pb]], writes=[r_o[b]])
                    else:
                        S.op("dve", lambda e, c=c, pb=pb: e.tensor_copy(out=xo[b][:, c, :], in_=PS[pb][:]),
                             reads=[PSR[pb]], writes=[r_o[b]])
                S.dma("act", XTv[:, :, j * 512:(j + 1) * 512], xo[b][:], s_o[b], reads=[r_o[b]])
            S.barrier()

    def phase_norm(gain_dram, l, tag, Hs, r_H):
        with ExitStack() as st:
            g = sb(st, tag + "g", [128, 16], F32)
            xs = [sb(st, f"{tag}x{i}", [128, 16, 256], F32) for i in range(2)]
            sq = sb(st, tag + "sq", [128, 16, 256], BF16)
            rs = [sb(st, f"{tag}rs{i}", [128, 256], F32) for i in range(2)]
            r_g = S.res(); r_x = [S.res() for _ in range(2)]; r_sq = S.res(); r_rs = [S.res() for _ in range(2)]
            s_g = S.dsem(); s_x = [S.dsem() for i in range(2)]
            S.dma("sp", g[:], gain_dram.ap()[l].rearrange("(c p) -> p c", p=128), s_g, writes=[r_g],
                  allow_slow_non_contiguous=True)
            XTv = XT.ap().rearrange("c p n -> p c n")
            for j in range(16):
                b = j % 2
                tsl = slice(j * 256, (j + 1) * 256)
                S.dma("sp", xs[b][:], XTv[:, :, tsl], s_x[b], writes=[r_x[b]])
                S.op("act", lambda e: e.activation(out=sq[:], in_=xs[b][:], func=AF.Square), reads=[r_x[b]], writes=[r_sq])
                pb = j % 8
                for c in range(16):
                    S.op("pe", lambda e, c=c: e.matmul(PS[pb][:, 0:256], onesb[:], sq[:, c, :], start=(c == 0), stop=(c == 15)),
                         reads=[r_sq], writes=[PSR[pb]], sig=(c == 15))
                S.op("act", lambda e: e.activation(out=rs[b][:], in_=PS[pb][:, 0:256], func=AF.Ln, bias=epsc[:], scale=1.0 / D),
                     reads=[PSR[pb]], writes=[r_rs[b]])
                S.op("act", lambda e: e.activation(out=rs[b][:], in_=rs[b][:], func=AF.Exp, scale=-0.5), reads=[r_rs[b]], writes=[r_rs[b]])
                for c in range(16):
                    S.op("dve", lambda e, c=c: e.scalar_tensor_tensor(out=Hs[:, c, tsl], in0=xs[b][:, c, :], scalar=g[:, c:c + 1],
                                                                 in1=rs[b][:], op0=ALU.mult, op1=ALU.mult),
                         reads=[r_x[b], r_rs[b], r_g], writes=[r_H])
            S.barrier()

    def phase_inproj(l):
        with ExitStack() as st:
            Hs = sb(st, "p2H", [128, 16, NT], BF16)
            r_H = S.res()
            phase_norm(norm1_g, l, f"n1{l}", Hs, r_H)
            wv = w_in.ap()[l].rearrange("(c p) n -> p c n", p=128)
            ws = WStream(st, "p2", 16, [(wv[:, :, m * 128:(m + 1) * 128], 16) for m in range(64)])
            qg = sb(st, "p2qg", [128, 2], F32)
            sqb = [sb(st, f"p2sq{i}", [128, 512], BF16) for i in range(2)]
            rsb = [sb(st, f"p2rs{i}", [128, 512], F32) for i in range(2)]
            NO = 4
            ob = [sb(st, f"p2o{i}", [128, 512], BF16) for i in range(NO)]
            vb = [sb(st, f"p2v{i}", [128, 512], BF16) for i in range(2)]
            r_qg = S.res(); r_sq = [S.res() for _ in range(2)]; r_rs = [S.res() for _ in range(2)]
            r_o = [S.res() for _ in range(NO)]; r_v = [S.res() for _ in range(2)]
            s_H = S.dsem("p2sH")
            s_qg = S.dsem("p2sqg"); s_o = [S.dsem(f"p2so{i}") for i in range(NO)]
            for hh in range(2):
                S.dma("sp", qg[hh * 64:(hh + 1) * 64, 0:1], dap(q_norm_g, l * 64, [[1, 64], [1, 1]]), s_qg, writes=[r_qg])
                S.dma("sp", qg[hh * 64:(hh + 1) * 64, 1:2], dap(k_norm_g, l * 64, [[1, 64], [1, 1]]), s_qg, writes=[r_qg])
            Vv = V.ap().rearrange("(n s p) f -> n p s f", s=4, p=128)
            U2v = U2.ap()
            cnt = 0
            oc = 0
            for m in range(64):
                wbm, r_wbm = ws.next()
                for j in range(8):
                    pb = cnt % 6
                    cnt += 1
                    for c in range(16):
                        S.op("pe", lambda e, c=c: e.matmul(PS[pb][:], wbm[:, c, :], Hs[:, c, j * 512:(j + 1) * 512],
                                                           start=(c == 0), stop=(c == 15)),
                             reads=[r_wbm, r_H], writes=[PSR[pb]], sig=(c == 15))
                    oi = oc % NO
                    oc += 1
                    tsl = slice(j * 512, (j + 1) * 512)
                    if m < 16:
                        kq = 0 if m < 8 else 1
                        hh = m % 8
                        qi = cnt % 2
                        pb2 = 6 + (cnt % 2)
                        S.op("act", lambda e: e.activation(out=sqb[qi][:], in_=PS[pb][:], func=AF.Square),
                             reads=[PSR[pb]], writes=[r_sq[qi]])
                        S.op("pe", lambda e: e.matmul(PS[pb2][:], blkb[:], sqb[qi][:], start=True, stop=True),
                             reads=[r_sq[qi]], writes=[PSR[pb2]])
                        S.op("act", lambda e: e.activation(out=rsb[qi][:], in_=PS[pb2][:], func=AF.Ln, bias=epsc[:], scale=1.0 / 64),
                             reads=[PSR[pb2]], writes=[r_rs[qi]])
                        S.op("act", lambda e: e.activation(out=rsb[qi][:], in_=rsb[qi][:], func=AF.Exp, scale=-0.5),
                             reads=[r_rs[qi]], writes=[r_rs[qi]])
                        S.op("dve", lambda e: e.scalar_tensor_tensor(out=ob[oi][:], in0=PS[pb][:], scalar=qg[:, kq:kq + 1],
                                                                     in1=rsb[qi][:], op0=ALU.mult, op1=ALU.mult),
                             reads=[PSR[pb], r_rs[qi], r_qg], writes=[r_o[oi]])
                        dst = (QT if kq == 0 else KT).ap()[hh][:, tsl]
                        S.dma("act", dst, ob[oi][:], s_o[oi], reads=[r_o[oi]])
                    elif m < 24:
                        hh = m - 16
                        vi = cnt % 2
                        pb2 = 6 + (cnt % 2)
                        S.op("act", lambda e: e.activation(out=vb[vi][:], in_=PS[pb][:], func=AF.Copy),
                             reads=[PSR[pb]], writes=[r_v[vi]])
                        pv = PS[pb2].bitcast(BF16)
                        for s4 in range(4):
                            S.op("pe", lambda e, s4=s4: e.transpose(out=pv[:, s4 * 128:(s4 + 1) * 128],
                                                                    in_=vb[vi][:, s4 * 128:(s4 + 1) * 128], identity=identb[:]),
                                 reads=[r_v[vi]], writes=[PSR[pb2]], sig=(s4 == 3))
                        S.op("dve", lambda e: e.tensor_copy(out=ob[oi][:], in_=pv[:, 0:512]), reads=[PSR[pb2]], writes=[r_o[oi]])
                        S.dma("act", Vv[j][:, :, hh * 128:(hh + 1) * 128], ob[oi][:].rearrange("p (s f) -> p s f", s=4),
                              s_o[oi], reads=[r_o[oi]])
                    elif m < 32:
                        mu = m - 24
                        S.op("act", lambda e: e.activation(out=ob[oi][:].rearrange("p (s c) -> p c s", s=8),
                                                           in_=PS[pb][:].rearrange("p (c s) -> p c s", s=8), func=AF.Copy),
                             reads=[PSR[pb]], writes=[r_o[oi]])
                        for gl in range(8):
                            gidx = mu * 8 + gl
                            S.dma("act", U2v[gidx].rearrange("s i n -> i s n")[:, :, j * 64:(j + 1) * 64],
                                  ob[oi][gl * 16:(gl + 1) * 16, :].rearrange("p (s c) -> p s c", s=8),
                                  s_o[oi], reads=[r_o[oi]])
                    else:
                        mg = m - 32
                        S.op("act", lambda e: e.activation(out=ob[oi][:], in_=PS[pb][:], func=AF.Sigmoid),
                             reads=[PSR[pb]], writes=[r_o[oi]])
                        dst = (GA if mg < 16 else GS).ap()[mg % 16][:, tsl]
                        S.dma("act", dst, ob[oi][:], s_o[oi], reads=[r_o[oi]])
            S.barrier()


    def phase_attn(l, tick=None):
        lam_init = 0.8 - 0.6 * math.exp(-0.3 * l)
        with ExitStack() as st:
            alib = sb(st, "atal", [128, 5, 512], F32)
            lamt = sb(st, "atlam", [128, 4, 64], F32)
            ltmp = sb(st, "atlt", [128, 64], F32)
            lsc = sb(st, "atls", [128, 8], F32)
            sg = sb(st, "atsg", [128, 1], F32)
            btab = sb(st, "atbt", [128, 8, 13], F32)
            btab2 = sb(st, "atbt2", [128, 8, 13], F32)
            pcol = sb(st, "atpc", [128, 8], F32)
            cpf = sb(st, "atcpf", [128, 513], F32)
            qT = [sb(st, f"atq{i}", [128, 2, SEQ], BF16) for i in range(2)]
            kT = [sb(st, f"atk{i}", [128, 2, SEQ], BF16) for i in range(2)]
            vt = [sb(st, f"atv{i}", [128, 16, 128], BF16) for i in range(2)]
            NB = 6
            sbias = [sb(st, f"atsb{i}", [128, 512], F32) for i in range(NB)]
            Eb = [sb(st, f"atE{i}", [128, 512], BF16) for i in range(NB)]
            fr2 = sb(st, "atfr2", [128, 512], F32); r_fr2 = S.res()
            o1s = sb(st, "ato1s", [128, 512], F32); r_o1s = S.res()
            o2s = sb(st, "ato2s", [128, 512], F32); r_o2s = S.res()
            fr = sb(st, "atfr", [128, 512], F32)
            ft1 = sb(st, "atft1", [128, 512], F32)
            ft2 = sb(st, "atft2", [128, 512], F32)
            fo = sb(st, "atfo", [128, 512], F32)
            fsq = sb(st, "atfsq", [128, 512], BF16)
            frs = sb(st, "atfrs", [128, 512], F32)
            fon = [sb(st, f"atfon{i}", [128, 512], BF16) for i in range(2)]
            r_c = S.res(); r_q = [S.res() for _ in range(2)]
            r_sb = [S.res() for _ in range(NB)]; r_E = [S.res() for _ in range(NB)]
            r_fr = S.res(); r_ft1 = S.res(); r_ft2 = S.res(); r_fo = S.res(); r_fsq = S.res(); r_frs = S.res()
            r_fon = [S.res() for _ in range(2)]
            s_c = S.dsem("atsc"); s_q = [S.dsem(f"atsq{i}") for i in range(2)]
            s_on = [S.dsem(f"atson{i}") for i in range(2)]
            S.dma("sp", alib[:], c_alibi.ap(), s_c, writes=[r_c])
            for idx, t in enumerate([lam_q1, lam_k1, lam_q2, lam_k2]):
                S.dma("sp", lamt[:, idx, :], dap(t, l * 64, [[0, 128], [1, 64]]), s_c, writes=[r_c])
            S.dma("sp", sg[:], dap(subln_g, l * 128, [[1, 128], [1, 1]]), s_c, writes=[r_c])
            for k2 in range(2):
                S.op("dve", lambda e: e.tensor_tensor(out=ltmp[:], in0=lamt[:, 2 * k2, :], in1=lamt[:, 2 * k2 + 1, :], op=ALU.mult),
                     reads=[r_c], writes=[r_c])
                S.op("dve", lambda e: e.tensor_reduce(out=lsc[:, k2:k2 + 1], in_=ltmp[:], axis=AX.X, op=ALU.add),
                     reads=[r_c], writes=[r_c])
            S.op("act", lambda e: e.activation(out=lsc[:, 2:4], in_=lsc[:, 0:2], func=AF.Exp), reads=[r_c], writes=[r_c])
            S.op("dve", lambda e: e.tensor_tensor(out=lsc[:, 4:5], in0=lsc[:, 3:4], in1=lsc[:, 2:3], op=ALU.subtract),
                 reads=[r_c], writes=[r_c])
            S.op("dve", lambda e: e.tensor_scalar(out=lsc[:, 5:6], in0=lsc[:, 4:5], scalar1=-lam_init, scalar2=None, op0=ALU.add),
                 reads=[r_c], writes=[r_c])
            S.op("dve", lambda e: e.tensor_scalar(out=sg[:], in0=sg[:], scalar1=(1.0 - lam_init), scalar2=None, op0=ALU.mult),
                 reads=[r_c], writes=[r_c])
            for hh in range(8):
                for rel in range(13):
                    S.op("pool", lambda e: e.memset(btab[:, hh, rel:rel + 1], -(2.0 ** -(hh + 1)) * 128.0 * rel), writes=[r_c])
            S.dma("sp", cpf[:], c_pf.ap(), s_c, writes=[r_c])
            for hh in range(8):
                sl_ = 2.0 ** -(hh + 1)
                S.op("dve", lambda e: e.tensor_scalar(out=pcol[:, hh:hh + 1], in0=cpf[:, 0:1], scalar1=sl_, scalar2=None, op0=ALU.mult),
                     reads=[r_c], writes=[r_c])
                S.op("dve", lambda e: e.tensor_scalar(out=btab2[:, hh, :], in0=btab[:, hh, :], scalar1=pcol[:, hh:hh + 1], scalar2=None,
                                                      op0=ALU.add), reads=[r_c], writes=[r_c])
            nlam = lsc[:, 5:6]
            for qi in range(2):
                S.op("dve", lambda e: e.memset(kT[qi][64:66, :, :], 1.0), writes=[r_q[qi]])
            it = 0
            bh = 0
            pending = [None]
            for b in range(2):
                for hh in range(8):
                    qi = bh % 2
                    bh += 1
                    tok = slice(b * SEQ, (b + 1) * SEQ)
                    for c2 in range(2):
                        S.dma("sp", qT[qi][0:64, c2, :], QT.ap()[hh][c2 * 64:(c2 + 1) * 64, tok], s_q[qi], writes=[r_q[qi]])
                        S.dma("sp", kT[qi][0:64, c2, :], KT.ap()[hh][c2 * 64:(c2 + 1) * 64, tok], s_q[qi], writes=[r_q[qi]])
                        S.dma("sp", qT[qi][64:66, c2, :], c_aug.ap()[hh], s_q[qi], writes=[r_q[qi]])
                    S.dma("sp", vt[qi][:], V.ap()[tok, hh * 128:(hh + 1) * 128].rearrange("(t p) f -> p t f", p=128),
                          s_q[qi], writes=[r_q[qi]])
                    slope = 2.0 ** -(hh + 1)
                    for j in range(4):
                        nk = 4 * (j + 1)
                        slots = {}

                        def keep(i):
                            rel = 4 * j - i
                            return rel <= 0 or slope * (128 * rel - 127) <= 40.0
                        tiles = [i for i in range(nk) if keep(i)]
                        nkk = len(tiles)

                        def emit_S(n):
                            nonlocal it
                            i = tiles[n]
                            a = 2 * (n % 2)
                            ksl = slice(i * 128, (i + 1) * 128)
                            rel = 4 * j - i
                            c0 = 0 if rel >= 1 else -rel * 128
                            pidx = 0 if rel >= 1 else 1 - rel
                            cs_ = slice(c0, 512)
                            qs_ = slice(j * 512 + c0, (j + 1) * 512)
                            kk = 66 if rel >= 1 else 64
                            xs2 = []
                            for c2 in range(2):
                                S.op("pe", lambda e: e.matmul(PS[a + c2][:, cs_], kT[qi][0:kk, c2, ksl], qT[qi][0:kk, c2, qs_], start=True, stop=True),
                                     reads=[r_q[qi]], writes=[PSR[a + c2]])
                            for c2 in range(2):
                                x = it % NB
                                it += 1
                                xs2.append(x)
                                if rel >= 1:
                                    S.op("act", lambda e: e.activation(out=Eb[x][:], in_=PS[a + c2][:], func=AF.Exp,
                                                                       bias=btab2[:, hh, rel:rel + 1], scale=0.125),
                                         reads=[PSR[a + c2], r_c], writes=[r_E[x]])
                                else:
                                    S.op("dve", lambda e: e.scalar_tensor_tensor(out=sbias[x][:, cs_], in0=alib[:, pidx, cs_], scalar=-8.0 * slope,
                                                                                 in1=PS[a + c2][:, cs_], op0=ALU.mult, op1=ALU.add),
                                         reads=[r_c, PSR[a + c2]], writes=[r_sb[x]])
                                    S.op("act", lambda e: e.activation(out=Eb[x][:, cs_], in_=sbias[x][:, cs_], func=AF.Exp, scale=0.125),
                                         reads=[r_sb[x]], writes=[r_E[x]])
                            if tick is not None:
                                tick()
                            slots[n] = (xs2, cs_)

                        def emit_OZ(n):
                            i = tiles[n]
                            (x1, x2), cs_ = slots[n]
                            st_, sp_ = (n == 0), (n == nkk - 1)
                            S.op("pe", lambda e: e.matmul(PS[4][:, cs_], vt[qi][:, i, :], Eb[x1][:, cs_], start=st_, stop=sp_),
                                 reads=[r_q[qi], r_E[x1]], writes=[PSR[4]])
                            S.op("pe", lambda e: e.matmul(PS[5][:, cs_], onesb[:], Eb[x1][:, cs_], start=st_, stop=sp_),
                                 reads=[r_E[x1]], writes=[PSR[5]])
                            S.op("pe", lambda e: e.matmul(PS[6][:, cs_], vt[qi][:, i, :], Eb[x2][:, cs_], start=st_, stop=sp_),
                                 reads=[r_q[qi], r_E[x2]], writes=[PSR[6]])
                            S.op("pe", lambda e: e.matmul(PS[7][:, cs_], onesb[:], Eb[x2][:, cs_], start=st_, stop=sp_),
                                 reads=[r_E[x2]], writes=[PSR[7]])

                        for n in range(nkk + 2):
                            if n < nkk:
                                emit_S(n)
                            if n == 1 and pending[0] is not None:
                                pending[0]()
                                pending[0] = None
                            if n >= 2:
                                emit_OZ(n - 2)

                        S.op("act", lambda e: e.activation(out=fr[:], in_=PS[5][:], func=AF.Ln), reads=[PSR[5]], writes=[r_fr])
                        S.op("act", lambda e: e.activation(out=fr2[:], in_=PS[7][:], func=AF.Ln), reads=[PSR[7]], writes=[r_fr2])
                        S.op("dve", lambda e: e.tensor_copy(out=o1s[:], in_=PS[4][:]), reads=[PSR[4]], writes=[r_o1s])
                        S.op("dve", lambda e: e.tensor_copy(out=o2s[:], in_=PS[6][:]), reads=[PSR[6]], writes=[r_o2s])

                        def finalize(b=b, hh=hh, j=j, oi=(bh * 4 + j) % 2):
                            S.op("act", lambda e: e.activation(out=fr[:], in_=fr[:], func=AF.Exp, scale=-1.0), reads=[r_fr], writes=[r_fr])
                            S.op("pool", lambda e: e.tensor_tensor(out=ft1[:], in0=o1s[:], in1=fr[:], op=ALU.mult),
                                 reads=[r_o1s, r_fr], writes=[r_ft1])
                            S.op("act", lambda e: e.activation(out=fr2[:], in_=fr2[:], func=AF.Exp, scale=-1.0), reads=[r_fr2], writes=[r_fr2])
                            S.op("dve", lambda e: e.tensor_tensor(out=ft2[:], in0=o2s[:], in1=fr2[:], op=ALU.mult),
                                 reads=[r_o2s, r_fr2], writes=[r_ft2])
                            S.op("dve", lambda e: e.scalar_tensor_tensor(out=fo[:], in0=ft2[:], scalar=nlam, in1=ft1[:],
                                                                         op0=ALU.mult, op1=ALU.add),
                                 reads=[r_ft1, r_ft2, r_c], writes=[r_fo])
                            S.op("pool", lambda e: e.tensor_tensor(out=fsq[:], in0=fo[:], in1=fo[:], op=ALU.mult), reads=[r_fo], writes=[r_fsq])
                            S.op("pe", lambda e: e.matmul(PS[5][:], onesb[:], fsq[:], start=True, stop=True), reads=[r_fsq], writes=[PSR[5]])
                            S.op("act", lambda e: e.activation(out=frs[:], in_=PS[5][:], func=AF.Ln, bias=epsc[:], scale=1.0 / 128),
                                 reads=[PSR[5]], writes=[r_frs])
                            S.op("act", lambda e: e.activation(out=frs[:], in_=frs[:], func=AF.Exp, scale=-0.5), reads=[r_frs], writes=[r_frs])
                            S.op("dve", lambda e: e.scalar_tensor_tensor(out=fon[oi][:], in0=fo[:], scalar=sg[:, 0:1], in1=frs[:],
                                                                         op0=ALU.mult, op1=ALU.mult),
                                 reads=[r_fo, r_frs, r_c], writes=[r_fon[oi]])
                            S.dma("act", OT.ap()[hh][:, b * SEQ + j * 512:b * SEQ + (j + 1) * 512], fon[oi][:], s_on[oi], reads=[r_fon[oi]])
                        pending[0] = finalize
            pending[0]()
            S.barrier()

    def phase_ssm(l, attn_fn):
        def bc(t, off, rowlen, n1, n2):
            return bass.AP(t, off, [[rowlen, 128], [1, n1], [0, n2]])
        with ExitStack() as st:
            W1b = sb(st, "ssW1", [128, 64, 128], BF16)
            W4re = sb(st, "ssW4r", [128, 32, 128], BF16); W4im = sb(st, "ssW4i", [128, 32, 128], BF16)
            AR2 = sb(st, "ssAR2", [128, 2, 32], F32); NAI = sb(st, "ssNAI", [128, 2, 32], F32)
            W2re = sb(st, "ssW2r", [128, 64, 64], BF16); W2im = sb(st, "ssW2i", [128, 64, 64], BF16)
            r_W = S.res()
            r_SH = [S.res() for _ in range(2)]; r_H = [S.res() for _ in range(2)]
            with ExitStack() as st2:
                An = sb(st2, "ssAn", [32, 2, 128], F32)
                Are = sb(st2, "ssAre", [128, 32], F32); Aim = sb(st2, "ssAim", [128, 32], F32)
                Ldt = sb(st2, "ssLdt", [128, 32], F32)
                dre = sb(st2, "ssdre", [128, 32], F32); dim = sb(st2, "ssdim", [128, 32], F32)
                mag = sb(st2, "ssmag", [128, 32], F32)
                cs = sb(st2, "sscs", [128, 2, 32], F32)
                t1 = sb(st2, "sst1", [128, 32], F32); t2 = sb(st2, "sst2", [128, 32], F32)
                PW = sb(st2, "ssPW", [128, 9, 2, 32], F32)
                FF = sb(st2, "ssFF", [128, 2, 32], F32)
                FP = sb(st2, "ssFP", [128, 8, 2, 32], F32)
                hpi = sb(st2, "sshpi", [128, 1], F32)
                Bre = sb(st2, "ssBre", [128, 32, 16], F32); Bim = sb(st2, "ssBim", [128, 32, 16], F32)
                T1 = sb(st2, "ssT1", [128, 32, 16], F32); T2 = sb(st2, "ssT2", [128, 32, 16], F32)
                ABr = [sb(st2, f"ssAB{i}", [128, 32, 240], F32) for i in range(2)]
                ABrb = [sb(st2, f"ssABb{i}", [128, 32, 240], BF16) for i in range(2)]
                CTb = [sb(st2, f"ssCTb{i}", [128, 32, 16], BF16) for i in range(2)]
                Cn = sb(st2, "ssCn", [128, 2, 64], F32)
                CT = [sb(st2, f"ssCT{i}", [128, 32, 16], F32) for i in range(3)]
                dgi = sb(st2, "ssdgi", [64, 8, 16], F32)
                Drep = sb(st2, "ssDrep", [128, 64], F32)
                r_p = S.res(); s_p = S.dsem(); r_cn = S.res(); s_cn = S.dsem()
                D_ = lambda fn, rd=(), wr=(): S.op("dve", fn, reads=[r_p] + list(rd), writes=[r_p] + list(wr))
                A_ = lambda fn, rd=(), wr=(): S.op("act", fn, reads=[r_p] + list(rd), writes=[r_p] + list(wr))
                TT = lambda o, a, b_, op: D_(lambda e: e.tensor_tensor(out=o, in0=a, in1=b_, op=op))
                for k2, src in enumerate([a_re, a_im]):
                    for gh in range(2):
                        S.dma("sp", An[:, k2, gh * 64:(gh + 1) * 64], src.ap()[l][gh * 32:(gh + 1) * 32, :], s_p, writes=[r_p])
                for gh in range(2):
                    S.dma("sp", Ldt[gh * 64:(gh + 1) * 64, :], dap(log_dt, l * 64 + gh * 32, [[0, 64], [1, 32]]), s_p, writes=[r_p])
                    S.dma("sp", Bre[gh * 64:(gh + 1) * 64, :, :], b_re.ap()[l][gh * 32:(gh + 1) * 32].rearrange("g p i -> p g i"), s_p, writes=[r_p])
                    S.dma("sp", Bim[gh * 64:(gh + 1) * 64, :, :], b_im.ap()[l][gh * 32:(gh + 1) * 32].rearrange("g p i -> p g i"), s_p, writes=[r_p])
                S.dma("sp", dgi[:], dap(ssm_d, l * 1024, [[16, 64], [0, 8], [1, 16]]), s_p, writes=[r_p])
                D_(lambda e: e.memset(hpi[:], math.pi / 2))
                for k2, dst in enumerate([Are, Aim]):
                    S.op("pe", lambda e: e.transpose(out=PS[k2][:, 0:32], in_=An[:, k2, :], identity=ident[0:32, 0:32]),
                         reads=[r_p], writes=[PSR[k2]])
                    D_(lambda e: e.tensor_copy(out=dst[:], in_=PS[k2][:, 0:32]), rd=[PSR[k2]])
                A_(lambda e: e.activation(out=Ldt[:], in_=Ldt[:], func=AF.Exp))
                TT(dre[:], Ldt[:], Are[:], ALU.mult)
                TT(dim[:], Ldt[:], Aim[:], ALU.mult)
                A_(lambda e: e.activation(out=mag[:], in_=dre[:], func=AF.Exp))
                A_(lambda e: e.activation(out=cs[:, 1, :], in_=dim[:], func=AF.Sin, scale=1.0 / 16))
                A_(lambda e: e.activation(out=cs[:, 0, :], in_=dim[:], func=AF.Sin, bias=hpi[:], scale=-1.0 / 16))
                for _ in range(4):
                    TT(t1[:], cs[:, 0, :], cs[:, 0, :], ALU.mult)
                    TT(t2[:], cs[:, 1, :], cs[:, 1, :], ALU.mult)
                    D_(lambda e: e.scalar_tensor_tensor(out=cs[:, 1, :], in0=cs[:, 0, :], scalar=2.0, in1=cs[:, 1, :],
                                                        op0=ALU.mult, op1=ALU.mult))
                    TT(cs[:, 0, :], t1[:], t2[:], ALU.subtract)
                D_(lambda e: e.memset(PW[:, 0, 0, :], 1.0))
                D_(lambda e: e.memset(PW[:, 0, 1, :], 0.0))
                TT(PW[:, 1, 0, :], mag[:], cs[:, 0, :], ALU.mult)
                TT(PW[:, 1, 1, :], mag[:], cs[:, 1, :], ALU.mult)
                ar, ai = PW[:, 1, 0, :], PW[:, 1, 1, :]

                def cmul(o_re, o_im, x_re, x_im, y_re, y_im):
                    TT(t1[:], x_re, y_re, ALU.mult); TT(t2[:], x_im, y_im, ALU.mult)
                    TT(o_re, t1[:], t2[:], ALU.subtract)
                    TT(t1[:], x_re, y_im, ALU.mult); TT(t2[:], x_im, y_re, ALU.mult)
                    TT(o_im, t1[:], t2[:], ALU.add)
                for k in range(2, 9):
                    cmul(PW[:, k, 0, :], PW[:, k, 1, :], PW[:, k - 1, 0, :], PW[:, k - 1, 1, :], ar, ai)
                TT(t1[:], Are[:], Are[:], ALU.mult); TT(t2[:], Aim[:], Aim[:], ALU.mult)
                TT(mag[:], t1[:], t2[:], ALU.add)
                D_(lambda e: e.reciprocal(out=mag[:], in_=mag[:]))
                D_(lambda e: e.tensor_scalar(out=dre[:], in0=ar, scalar1=-1.0, scalar2=None, op0=ALU.add))
                TT(t1[:], dre[:], Are[:], ALU.mult); TT(t2[:], ai, Aim[:], ALU.mult)
                TT(t1[:], t1[:], t2[:], ALU.add); TT(FF[:, 0, :], t1[:], mag[:], ALU.mult)
                TT(t1[:], ai, Are[:], ALU.mult); TT(t2[:], dre[:], Aim[:], ALU.mult)
                TT(t1[:], t1[:], t2[:], ALU.subtract); TT(FF[:, 1, :], t1[:], mag[:], ALU.mult)
                for k in range(8):
                    cmul(FP[:, k, 0, :], FP[:, k, 1, :], PW[:, k, 0, :], PW[:, k, 1, :], FF[:, 0, :], FF[:, 1, :])
                D_(lambda e: e.memset(ABr[0][:], 0.0)); D_(lambda e: e.memset(ABr[1][:], 0.0))
                for tau in range(8):
                    bk = 7 - tau
                    fr_ = bc(FP, tau * 64, 512, 32, 16); fi_ = bc(FP, tau * 64 + 32, 512, 32, 16)
                    osl = slice(bk * 16, (bk + 1) * 16)
                    TT(T1[:], Bre[:], fr_, ALU.mult); TT(T2[:], Bim[:], fi_, ALU.mult)
                    TT(ABr[0][:, :, osl], T1[:], T2[:], ALU.subtract)
                    TT(T1[:], Bim[:], fr_, ALU.mult); TT(T2[:], Bre[:], fi_, ALU.mult)
                    TT(ABr[1][:, :, osl], T1[:], T2[:], ALU.add)
                for k2, src in enumerate([c_re, c_im]):
                    for blk in range(4):
                        for gh in range(2):
                            g0 = gh * 32 + blk * 8
                            S.dma("sp", Cn[:, gh, :], src.ap()[l][g0:g0 + 8].rearrange("g o p -> (g o) p"), s_cn, writes=[r_cn])
                        pb = 2 + (k2 * 4 + blk) % 2
                        S.op("pe", lambda e: e.transpose(out=PS[pb][:, 0:128], in_=Cn[:].rearrange("p a b -> p (a b)"), identity=ident[:]),
                             reads=[r_cn], writes=[PSR[pb]])
                        D_(lambda e: e.tensor_copy(out=CT[k2][:, blk * 8:(blk + 1) * 8, :].rearrange("p a b -> p (a b)"), in_=PS[pb][:, 0:128]),
                           rd=[PSR[pb]])
                D_(lambda e: e.tensor_scalar(out=CT[2][:], in0=CT[1][:], scalar1=-1.0, scalar2=None, op0=ALU.mult))
                D_(lambda e: e.tensor_copy(out=ABrb[0][:], in_=ABr[0][:]))
                A_(lambda e: e.activation(out=ABrb[1][:], in_=ABr[1][:], func=AF.Copy))
                D_(lambda e: e.tensor_copy(out=CTb[0][:], in_=CT[0][:]))
                D_(lambda e: e.tensor_copy(out=CTb[1][:], in_=CT[2][:]))
                S.op("pe", lambda e: e.matmul(PS[4][:, 0:64], dgi[:].rearrange("p a b -> p (a b)"), ident[0:64, 0:64], start=True, stop=True),
                     reads=[r_p], writes=[PSR[4]])
                D_(lambda e: e.tensor_copy(out=Drep[:], in_=PS[4][:, 0:64]), rd=[PSR[4]])
                for g4 in range(16):
                    pb = 5 + g4 % 2
                    for q in range(4):
                        g = g4 * 4 + q
                        gh, gl = g // 32, g % 32
                        ps_ = slice(gh * 64, (gh + 1) * 64)
                        for t in range(8):
                            wsl = slice((7 - t) * 16, (7 - t) * 16 + 128)
                            osl = slice(q * 128 + t * 16, q * 128 + (t + 1) * 16)
                            S.op("pe", lambda e: e.matmul(PS[pb][:, osl], ABrb[0][ps_, gl, wsl], CTb[0][ps_, gl, :], start=True, stop=False),
                                 reads=[r_p], writes=[PSR[pb]], sig=False)
                            S.op("pe", lambda e: e.matmul(PS[pb][:, osl], ABrb[1][ps_, gl, wsl], CTb[1][ps_, gl, :], start=False, stop=True),
                                 reads=[r_p], writes=[PSR[pb]], sig=(t == 7 and q == 3))
                    for q in range(4):
                        g = g4 * 4 + q
                        S.op("dve", lambda e: e.scalar_tensor_tensor(out=W1b[:, g, :], in0=ident[:], scalar=Drep[:, g:g + 1],
                                                                     in1=PS[pb][:, q * 128:(q + 1) * 128], op0=ALU.mult, op1=ALU.add),
                             reads=[r_p, PSR[pb]], writes=[r_W])
                for k2, dst in enumerate([W2re, W2im]):
                    for g8 in range(8):
                        pb = (k2 * 8 + g8) % 2
                        for q in range(8):
                            g = g8 * 8 + q
                            gh, gl = g // 32, g % 32
                            ps_ = slice(gh * 64, (gh + 1) * 64)
                            S.op("pe", lambda e: e.transpose(out=PS[pb][:, q * 64:(q + 1) * 64], in_=ABr[k2][ps_, gl, 0:128],
                                                             identity=ident[ps_, ps_]), reads=[r_p], writes=[PSR[pb]], sig=(q == 7))
                        S.op("act", lambda e: e.activation(out=dst[:, g8 * 8:(g8 + 1) * 8, :].rearrange("p a b -> p (a b)"), in_=PS[pb][:],
                                                           func=AF.Copy), reads=[PSR[pb]], writes=[r_W])
                for t in range(8):
                    k = t + 1
                    pr_ = bc(PW, k * 64, 576, 32, 16); pi_ = bc(PW, k * 64 + 32, 576, 32, 16)
                    osl = slice(t * 16, (t + 1) * 16)
                    TT(T1[:], CT[0][:], pr_, ALU.mult); TT(T2[:], CT[1][:], pi_, ALU.mult)
                    D_(lambda e: e.tensor_tensor(out=W4re[:, :, osl], in0=T1[:], in1=T2[:], op=ALU.subtract), wr=[r_W])
                    TT(T1[:], CT[2][:], pr_, ALU.mult); TT(T2[:], CT[0][:], pi_, ALU.mult)
                    D_(lambda e: e.tensor_tensor(out=W4im[:, :, osl], in0=T1[:], in1=T2[:], op=ALU.subtract), wr=[r_W])
                for r2 in range(2):
                    D_(lambda e: e.tensor_copy(out=AR2[:, r2, :], in_=PW[:, 8, 0, :]), wr=[r_W])
                D_(lambda e: e.tensor_copy(out=NAI[:, 1, :], in_=PW[:, 8, 1, :]), wr=[r_W])
                D_(lambda e: e.tensor_scalar(out=NAI[:, 0, :], in0=PW[:, 8, 1, :], scalar1=-1.0, scalar2=None, op0=ALU.mult), wr=[r_W])
                S.barrier()
            SH = sb(st, "ssSH", [128, 2, 32, 2, 257], BF16)
            Hst = [sb(st, f"ssH{b}", [128, 2, 32, 2], F32) for b in range(2)]
            Pt = sb(st, "ssP", [128, 2, 32, 2], F32)
            Qt = sb(st, "ssQ", [128, 2, 32, 2], F32)
            r_P = S.res(); r_Q = S.res(); r_SHs = S.res()
            with ExitStack() as st3:
                ub = [sb(st3, f"ssub{i}", [128, 512], BF16) for i in range(4)]
                r_ub = [S.res() for _ in range(4)]; s_ub = [S.dsem() for _ in range(4)]
                for b in range(2):
                    S.op("dve", lambda e: e.memset(SH[:, :, :, b, 0:1], 0.0), writes=[r_SH[b]])
                    S.op("dve", lambda e: e.memset(Hst[b][:], 0.0), writes=[r_H[b]])
                uc = 0
                for gl in range(32):
                    a = 2 * (gl % 2)
                    for gh in range(2):
                        g = gh * 32 + gl
                        ui = uc % 4; uc += 1
                        S.dma("sp", ub[ui][:], U2.ap()[g].rearrange("s i n -> (s i) n"), s_ub[ui], writes=[r_ub[ui]])
                        ps_ = slice(gh * 64, (gh + 1) * 64)
                        S.op("pe", lambda e: e.matmul(PS[a][ps_, :], W2re[:, g, :], ub[ui][:], start=True, stop=True),
                             reads=[r_W, r_ub[ui]], writes=[PSR[a]])
                        S.op("pe", lambda e: e.matmul(PS[a + 1][ps_, :], W2im[:, g, :], ub[ui][:], start=True, stop=True),
                             reads=[r_W, r_ub[ui]], writes=[PSR[a + 1]])
                    S.op("act", lambda e: e.activation(out=SH[:, 0, gl, :, 1:257], in_=PS[a][:].rearrange("p (b c) -> p b c", b=2), func=AF.Copy),
                         reads=[PSR[a]], writes=r_SH)
                    S.op("dve", lambda e: e.tensor_copy(out=SH[:, 1, gl, :, 1:257], in_=PS[a + 1][:].rearrange("p (b c) -> p b c", b=2)),
                         reads=[PSR[a + 1]], writes=r_SH)
                S.barrier()
            state = {"c": 0}

            ARb = bass.AP(AR2.tensor if hasattr(AR2, "tensor") else AR2, 0, [[64, 128], [1, 64], [0, 2]])
            NAb = [bass.AP(NAI.tensor if hasattr(NAI, "tensor") else NAI, r2 * 32, [[64, 128], [1, 32], [0, 2]]) for r2 in range(2)]

            def tick():
                c = state["c"]
                if c >= 256:
                    return
                state["c"] = c + 1
                Hc, Hn = Hst[c % 2], Hst[(c + 1) % 2]
                rHc, rHn = r_H[c % 2], r_H[(c + 1) % 2]
                S.op("dve", lambda e: e.tensor_tensor(out=Pt[:].rearrange("p r g b -> p (r g) b"), in0=ARb,
                                                      in1=Hc[:].rearrange("p r g b -> p (r g) b"), op=ALU.mult),
                     reads=[rHc, r_W], writes=[r_P])
                S.op("pool", lambda e: e.tensor_tensor(out=Qt[:, 0, :, :], in0=NAb[0], in1=Hc[:, 1, :, :], op=ALU.mult),
                     reads=[rHc, r_W], writes=[r_Q])
                S.op("pool", lambda e: e.tensor_tensor(out=Qt[:, 1, :, :], in0=NAb[1], in1=Hc[:, 0, :, :], op=ALU.mult),
                     reads=[rHc, r_W], writes=[r_Q])
                S.op("dve", lambda e: e.tensor_tensor(out=Pt[:], in0=Pt[:], in1=Qt[:], op=ALU.add), reads=[r_P, r_Q], writes=[r_P])
                S.op("dve", lambda e: e.tensor_tensor(out=Hn[:], in0=Pt[:], in1=SH[:, :, :, :, c + 1], op=ALU.add),
                     reads=[r_P] + r_SH, writes=[rHn])
                S.op("pool", lambda e: e.tensor_copy(out=SH[:, :, :, :, c + 1], in_=Hn[:]), reads=[rHn], writes=r_SH)

            attn_fn(l, tick)
            while state["c"] < 256:
                tick()
            with ExitStack() as st4:
                ub = [sb(st4, f"ssub{i}", [128, 512], BF16) for i in range(4)]
                r_ub = [S.res() for _ in range(4)]; s_ub = [S.dsem() for _ in range(4)]
                yo = [sb(st4, f"ssyo{i}", [128, 512], BF16) for i in range(2)]
                r_yo = [S.res() for _ in range(2)]; s_yo = [S.dsem() for _ in range(2)]
                yc = 0; uc = 0
                for g in range(64):
                    gh, gl = g // 32, g % 32
                    ps_ = slice(gh * 64, (gh + 1) * 64)
                    ui = uc % 4; uc += 1
                    S.dma("sp", ub[ui][:], U2.ap()[g].rearrange("s i n -> (s i) n"), s_ub[ui], writes=[r_ub[ui]])
                    pb = 4 + g % 4
                    S.op("pe", lambda e: e.matmul(PS[pb][:], W1b[:, g, :], ub[ui][:], start=True, stop=False),
                         reads=[r_W, r_ub[ui]], writes=[PSR[pb]], sig=False)
                    S.op("pe", lambda e: e.matmul(PS[pb][:], W4re[ps_, gl, :], SH[ps_, 0, gl, :, 0:256], start=False, stop=False),
                         reads=[r_W] + r_SH, writes=[PSR[pb]], sig=False)
                    S.op("pe", lambda e: e.matmul(PS[pb][:], W4im[ps_, gl, :], SH[ps_, 1, gl, :, 0:256], start=False, stop=True),
                         reads=[r_W] + r_SH, writes=[PSR[pb]])
                    yi = yc % 2; yc += 1
                    if g % 2 == 0:
                        S.op("act", lambda e: e.activation(out=yo[yi][:], in_=PS[pb][:], func=AF.Copy), reads=[PSR[pb]], writes=[r_yo[yi]])
                    else:
                        S.op("dve", lambda e: e.tensor_copy(out=yo[yi][:], in_=PS[pb][:]), reads=[PSR[pb]], writes=[r_yo[yi]])
                    S.dma("act", Y2.ap()[g].rearrange("t o n -> (t o) n"), yo[yi][:], s_yo[yi], reads=[r_yo[yi]])
                S.barrier()

    class WStream:
        def __init__(self, st, tag, nkmax, items, nbf=2, dist=1):
            self.items = items
            self.nbf = nbf
            self.dist = dist
            self.wst = [sb(st, f"{tag}ws{i}", [128, nkmax, 128], F32) for i in range(2)]
            self.wbf = [sb(st, f"{tag}wb{i}", [128, nkmax, 128], BF16) for i in range(nbf)]
            self.r_ws = [S.res() for _ in range(2)]; self.r_wb = [S.res() for _ in range(nbf)]
            self.s_ws = [S.dsem() for i in range(2)]
            self.issued = 0
            self.idx = 0

        def _issue(self, k):
            view, nk = self.items[k]
            i = k % 2
            j = k % self.nbf
            S.dma("sp", self.wst[i][:, 0:nk, :], view, self.s_ws[i], writes=[self.r_ws[i]])
            S.op("pool", lambda e: e.tensor_copy(out=self.wbf[j][:, 0:nk, :], in_=self.wst[i][:, 0:nk, :]),
                 reads=[self.r_ws[i]], writes=[self.r_wb[j]])

        def next(self):
            while self.issued < min(len(self.items), self.idx + self.dist + 1):
                self._issue(self.issued)
                self.issued += 1
            j = self.idx % self.nbf
            self.idx += 1
            return self.wbf[j], self.r_wb[j]

    def mm_acc(pb, wb, r_wb, X, r_X, nk, tsl, first=True, last=True):
        for c in range(nk):
            S.op("pe", lambda e: e.matmul(PS[pb][:], wb[:, c, :], X[:, c, tsl], start=(first and c == 0), stop=(last and c == nk - 1)),
                 reads=[r_wb, r_X], writes=[PSR[pb]], sig=(c == nk - 1))

    def phase_mix(l):
        with ExitStack() as st:
            zT = sb(st, "mxz", [128, 8, NT], BF16)
            r_z = S.res()
            glb = sb(st, "mxgb", [128, 8], F32); r_gb = S.res(); s_gb = S.dsem("mxsgb")
            S.dma("sp", glb[:], glu_b.ap()[l].rearrange("(c p) -> p c", p=128), s_gb, writes=[r_gb], allow_slow_non_contiguous=True)
            with ExitStack() as st2:
                yT = sb(st2, "mxy", [128, 8, NT], BF16); r_y = S.res()
                yl = [sb(st2, f"mxyl{i}", [128, 8, 512], BF16) for i in range(2)]
                r_yl = [S.res() for _ in range(2)]; s_yl = [S.dsem(f"mxsyl{i}") for i in range(2)]
                sgt = [sb(st2, f"mxsg{i}", [128, 512], BF16) for i in range(2)]; r_sg = [S.res() for _ in range(2)]
                wgv = glu_w.ap()[l].rearrange("(c p) n -> p c n", p=128)
                ws = WStream(st2, "mxa", 8, [(wgv[:, :, m * 128:(m + 1) * 128], 8) for m in range(8)])
                for mu in range(8):
                    i = mu % 2
                    for gl in range(8):
                        S.dma("sp", yl[i][gl * 16:(gl + 1) * 16, :, :], Y2.ap()[mu * 8 + gl].rearrange("t o n -> o t n"),
                              s_yl[i], writes=[r_yl[i]])
                    S.op("act", lambda e: e.activation(out=yT[:, mu, :].rearrange("p (n t) -> p t n", t=8), in_=yl[i][:],
                                                       func=AF.Gelu_apprx_tanh), reads=[r_yl[i]], writes=[r_y])
                cnt = 0
                for m in range(8):
                    wb, r_wb = ws.next()
                    for j in range(8):
                        pb = cnt % 4; k = cnt % 2; cnt += 1
                        tsl = slice(j * 512, (j + 1) * 512)
                        mm_acc(pb, wb, r_wb, yT, r_y, 8, tsl)
                        S.op("act", lambda e: e.activation(out=sgt[k][:], in_=PS[pb][:], func=AF.Sigmoid, bias=glb[:, m:m + 1], scale=1.0),
                             reads=[PSR[pb], r_gb], writes=[r_sg[k]])
                        S.op("dve", lambda e: e.tensor_tensor(out=zT[:, m, tsl], in0=yT[:, m, tsl], in1=sgt[k][:], op=ALU.mult),
                             reads=[r_sg[k], r_y], writes=[r_z])
                S.barrier()
            with ExitStack() as st2:
                oT = sb(st2, "mxo", [128, 8, NT], BF16); r_o = S.res(); s_o = S.dsem("mxso")
                tmp = sb(st2, "mxt", [128, NT], F32); r_t = S.res()
                gt = [sb(st2, f"mxg{i}", [128, 512], BF16) for i in range(4)]; r_g = [S.res() for _ in range(4)]
                s_g = [S.dsem(f"mxsg{i}") for i in range(4)]
                mo = [sb(st2, f"mxmo{i}", [128, 512], BF16) for i in range(2)]; r_mo = [S.res() for _ in range(2)]
                s_mo = [S.dsem(f"mxsmo{i}") for i in range(2)]
                t2 = [sb(st2, f"mxt2{i}", [128, 512], F32) for i in range(2)]; r_t2 = [S.res() for _ in range(2)]
                wav = w_ba.ap()[l].rearrange("(c p) n -> p c n", p=128)
                wsv = w_bs.ap()[l].rearrange("(c p) n -> p c n", p=128)
                its = []
                for m in range(16):
                    its.append((wav[:, :, m * 128:(m + 1) * 128], 8))
                    its.append((wsv[:, :, m * 128:(m + 1) * 128], 8))
                ws = WStream(st2, "mxb", 8, its)
                OTv = OT.ap().rearrange("c p n -> p c n")
                for j in range(8):
                    S.dma("sp", oT[:, :, j * 512:(j + 1) * 512], OTv[:, :, j * 512:(j + 1) * 512], s_o, writes=[r_o])
                cnt = 0; gc = 0
                for m in range(16):
                    wb, r_wb = ws.next()
                    for j in range(8):
                        pb = cnt % 4; cnt += 1
                        tsl = slice(j * 512, (j + 1) * 512)
                        gi = gc % 4; gc += 1
                        S.dma("sp", gt[gi][:], GA.ap()[m][:, tsl], s_g[gi], writes=[r_g[gi]])
                        mm_acc(pb, wb, r_wb, oT, r_o, 8, tsl)
                        S.op("dve", lambda e: e.tensor_tensor(out=tmp[:, tsl], in0=PS[pb][:], in1=gt[gi][:], op=ALU.mult),
                             reads=[PSR[pb], r_g[gi]], writes=[r_t])
                    wb, r_wb = ws.next()
                    for j in range(8):
                        pb = cnt % 4; k = cnt % 2; cnt += 1
                        tsl = slice(j * 512, (j + 1) * 512)
                        gi = gc % 4; gc += 1
                        S.dma("sp", gt[gi][:], GS.ap()[m][:, tsl], s_g[gi], writes=[r_g[gi]])
                        mm_acc(pb, wb, r_wb, zT, r_z, 8, tsl)
                        S.op("dve", lambda e: e.tensor_tensor(out=t2[k][:], in0=PS[pb][:], in1=gt[gi][:], op=ALU.mult),
                             reads=[PSR[pb], r_g[gi]], writes=[r_t2[k]])
                        S.op("pool", lambda e: e.tensor_tensor(out=mo[k][:], in0=t2[k][:], in1=tmp[:, tsl], op=ALU.add),
                             reads=[r_t2[k], r_t], writes=[r_mo[k]])
                        S.dma("act", H.ap()[m][:, tsl], mo[k][:], s_mo[k], reads=[r_mo[k]])
                S.barrier()
        with ExitStack() as st:
            Ms = sb(st, "mxM", [128, 16, NT], BF16); r_M = S.res(); s_M = S.dsem("mxsM")
            xt = [sb(st, f"mxx{i}", [128, 512], F32) for i in range(4)]; r_x = [S.res() for _ in range(4)]
            s_x = [S.dsem(f"mxsx{i}") for i in range(4)]
            wov = w_out.ap()[l].rearrange("(c p) n -> p c n", p=128)
            ws = WStream(st, "mxc", 16, [(wov[:, :, m * 128:(m + 1) * 128], 16) for m in range(16)])
            Hv = H.ap().rearrange("c p n -> p c n")
            for j in range(8):
                S.dma("sp", Ms[:, :, j * 512:(j + 1) * 512], Hv[:, :, j * 512:(j + 1) * 512], s_M, writes=[r_M])
            cnt = 0
            for m in range(16):
                wb, r_wb = ws.next()
                for j in range(8):
                    pb = cnt % 4; xi = cnt % 4; cnt += 1
                    tsl = slice(j * 512, (j + 1) * 512)
                    S.dma("sp", xt[xi][:], XT.ap()[m][:, tsl], s_x[xi], writes=[r_x[xi]])
                    mm_acc(pb, wb, r_wb, Ms, r_M, 16, tsl)
                    S.op("dve", lambda e: e.tensor_tensor(out=xt[xi][:], in0=PS[pb][:], in1=xt[xi][:], op=ALU.add),
                         reads=[PSR[pb], r_x[xi]], writes=[r_x[xi]])
                    S.dma("act", XT.ap()[m][:, tsl], xt[xi][:], s_x[xi], reads=[r_x[xi]])
            S.barrier()

    def phase_ffn_up(l):
        with ExitStack() as st:
            Hs = sb(st, "fuH", [128, 16, NT], BF16); r_H = S.res()
            phase_norm(norm2_g, l, f"n2{l}", Hs, r_H)
            cw = sb(st, "fucw", [128, 3, 86], F32); cb = sb(st, "fucb", [128, 86], F32); r_cw = S.res(); s_cw = S.dsem("fuscw")
            up = [[sb(st, f"fuu{a}{i}", [128, 514], F32) for i in range(2)] for a in range(2)]
            r_up = [[S.res() for i in range(2)] for a in range(2)]
            cc = [[sb(st, f"fuc{a}{i}", [128, 512], F32) for i in range(2)] for a in range(2)]
            r_cc = [[S.res() for i in range(2)] for a in range(2)]
            ga = [sb(st, f"fug{i}", [128, 512], F32) for i in range(2)]; r_ga = [S.res() for _ in range(2)]
            ao = [sb(st, f"fuo{i}", [128, 512], BF16) for i in range(2)]; r_ao = [S.res() for _ in range(2)]
            s_ao = [S.dsem(f"fusao{i}") for i in range(2)]
            wuv = w_up.ap()[l].rearrange("(c p) n -> p c n", p=128)
            its = []
            for m in range(NFT):
                its.append((wuv[:, :, m * 128:(m + 1) * 128], 16))
                its.append((wuv[:, :, (NFT + m) * 128:(NFT + m + 1) * 128], 16))
            ws = WStream(st, "fu", 16, its, nbf=4, dist=2)
            for k3 in range(3):
                S.dma("sp", cw[:, k3, :], conv_w.ap()[l][k3].rearrange("(m p) -> p m", p=128), s_cw, writes=[r_cw],
                      allow_slow_non_contiguous=True)
            S.dma("sp", cb[:], conv_b.ap()[l].rearrange("(m p) -> p m", p=128), s_cw, writes=[r_cw], allow_slow_non_contiguous=True)
            cnt = 0
            for m in range(NFT):
                wba, r_wba = ws.next()
                wbv, r_wbv = ws.next()
                for j in range(8):
                    k = cnt % 2; k4 = cnt % 4; cnt += 1
                    tsl = slice(j * 512, (j + 1) * 512)
                    pbs = (2 * k4, 2 * k4 + 1)
                    mm_acc(pbs[0], wba, r_wba, Hs, r_H, 16, tsl)
                    mm_acc(pbs[1], wbv, r_wbv, Hs, r_H, 16, tsl)
                    for a in range(2):
                        fm = m + a * NFT
                        pb = pbs[a]
                        u_, ru_ = up[a][k], r_up[a][k]
                        if j % 4 == 0:
                            S.op("pool", lambda e: e.memset(u_[:, 0:2], 0.0), writes=[ru_])
                        else:
                            S.op("pool", lambda e: e.tensor_copy(out=u_[:, 0:2], in_=up[a][1 - k][:, 512:514]),
                                 reads=[r_up[a][1 - k]], writes=[ru_])
                        S.op("act", lambda e: e.activation(out=u_[:, 2:514], in_=PS[pb][:], func=AF.Copy), reads=[PSR[pb]], writes=[ru_])
                        S.op("act", lambda e: e.activation(out=cc[a][k][:], in_=PS[pb][:], func=AF.Identity, bias=cb[:, fm:fm + 1],
                                                           scale=cw[:, 2, fm:fm + 1]), reads=[PSR[pb], r_cw], writes=[r_cc[a][k]])
                        S.op("dve", lambda e: e.scalar_tensor_tensor(out=cc[a][k][:], in0=u_[:, 1:513], scalar=cw[:, 1, fm:fm + 1],
                                                                     in1=cc[a][k][:], op0=ALU.mult, op1=ALU.add),
                             reads=[ru_, r_cw, r_cc[a][k]], writes=[r_cc[a][k]])
                        S.op("dve", lambda e: e.scalar_tensor_tensor(out=cc[a][k][:], in0=u_[:, 0:512], scalar=cw[:, 0, fm:fm + 1],
                                                                     in1=cc[a][k][:], op0=ALU.mult, op1=ALU.add),
                             reads=[ru_, r_cw, r_cc[a][k]], writes=[r_cc[a][k]])
                    S.op("act", lambda e: e.activation(out=ga[k][:], in_=cc[0][k][:], func=AF.Gelu_apprx_tanh),
                         reads=[r_cc[0][k]], writes=[r_ga[k]])
                    S.op("dve", lambda e: e.tensor_tensor(out=ao[k][:], in0=ga[k][:], in1=cc[1][k][:], op=ALU.mult),
                         reads=[r_ga[k], r_cc[1][k]], writes=[r_ao[k]])
                    S.dma("act", AT.ap()[m][:, tsl], ao[k][:], s_ao[k], reads=[r_ao[k]])
            S.barrier()

    def phase_ffn_down(l):
        for (k0, k1) in KGROUPS:
            nk = k1 - k0
            with ExitStack() as st:
                As = sb(st, "fdA", [128, 15, NT], BF16); r_A = S.res(); s_A = S.dsem(f"fdsA{k0}")
                xt = [sb(st, f"fdx{i}", [128, 512], F32) for i in range(4)]; r_x = [S.res() for _ in range(4)]
                s_x = [S.dsem(f"fdsx{k0}_{i}") for i in range(4)]
                wdv = w_down.ap()[l].rearrange("(c p) n -> p c n", p=128)
                ws = WStream(st, f"fd{k0}", 15, [(wdv[:, k0:k1, m * 128:(m + 1) * 128], nk) for m in range(16)])
                ATv = AT.ap().rearrange("c p n -> p c n")
                for j in range(8):
                    S.dma("sp", As[:, 0:nk, j * 512:(j + 1) * 512], ATv[:, k0:k1, j * 512:(j + 1) * 512], s_A, writes=[r_A])
                cnt = 0
                for m in range(16):
                    wb, r_wb = ws.next()
                    for j in range(8):
                        pb = cnt % 4; xi = cnt % 4; cnt += 1
                        tsl = slice(j * 512, (j + 1) * 512)
                        S.dma("sp", xt[xi][:], XT.ap()[m][:, tsl], s_x[xi], writes=[r_x[xi]])
                        mm_acc(pb, wb, r_wb, As, r_A, nk, tsl)
                        S.op("dve", lambda e: e.tensor_tensor(out=xt[xi][:], in0=PS[pb][:], in1=xt[xi][:], op=ALU.add),
                             reads=[PSR[pb], r_x[xi]], writes=[r_x[xi]])
                        S.dma("act", XT.ap()[m][:, tsl], xt[xi][:], s_x[xi], reads=[r_x[xi]])
                S.barrier()

    def phase_transpose_out():
        with ExitStack() as st:
            xs = [sb(st, f"pox{i}", [128, 16, 512], F32) for i in range(2)]
            yo = [sb(st, f"poy{i}", [128, 4, D], F32) for i in range(2)]
            r_x = [S.res() for _ in range(2)]; r_y = [S.res() for _ in range(2)]
            s_x = [S.dsem(f"posx{i}") for i in range(2)]; s_y = [S.dsem(f"posy{i}") for i in range(2)]
            XTv = XT.ap().rearrange("c p n -> p c n")
            yv = y_out.ap().rearrange("(n s p) d -> n p s d", s=4, p=128)
            cnt = 0
            for j in range(8):
                b = j % 2
                S.dma("sp", xs[b][:], XTv[:, :, j * 512:(j + 1) * 512], s_x[b], writes=[r_x[b]])
                for s4 in range(4):
                    for c4 in range(4):
                        pb = cnt % 8; cnt += 1
                        for cc_ in range(4):
                            c = c4 * 4 + cc_
                            S.op("pe", lambda e: e.transpose(out=PS[pb][:, cc_ * 128:(cc_ + 1) * 128],
                                                             in_=xs[b][:, c, s4 * 128:(s4 + 1) * 128], identity=ident[:]),
                                 reads=[r_x[b]], writes=[PSR[pb]], sig=(cc_ == 3))
                        if cnt % 2 == 0:
                            S.op("act", lambda e: e.activation(out=yo[b][:, s4, c4 * 512:(c4 + 1) * 512], in_=PS[pb][:], func=AF.Copy),
                                 reads=[PSR[pb]], writes=[r_y[b]])
                        else:
                            S.op("dve", lambda e: e.tensor_copy(out=yo[b][:, s4, c4 * 512:(c4 + 1) * 512], in_=PS[pb][:]),
                                 reads=[PSR[pb]], writes=[r_y[b]])
                S.dma("act", yv[j], yo[b][:], s_y[b], reads=[r_y[b]])
            S.barrier()

    phase_transpose_in()
    for l in range(nlayers):
        S.rotate()
        if "ip" in PH: phase_inproj(l)
        if "ss" in PH: phase_ssm(l, phase_attn)
        if "mx" in PH: phase_mix(l)
        if "fu" in PH: phase_ffn_up(l)
        if "fd" in PH: phase_ffn_down(l)
    phase_transpose_out()
    stack.close()
    return nc


def consts():
    ident = np.eye(128, dtype=np.float32)
    p = np.arange(128)[:, None].astype(np.float64)
    f = np.arange(512)[None, :].astype(np.float64)
    al = np.zeros((128, 5, 512), np.float32)
    al[:, 0, :] = (f - p)
    for m in range(4):
        d = f - p - 128.0 * m
        vis = (p + 128 * m) < (np.floor(f / 64) + 1) * 64
        al[:, 1 + m, :] = np.where(vis, np.abs(d), 1.0e9)
    blk = np.zeros((128, 128), np.float32)
    blk[:64, :64] = 1.0
    blk[64:, 64:] = 1.0
    pf = np.zeros((128, 513), np.float32)
    pf[:, 0] = np.arange(128)
    pf[:, 1:] = np.arange(512)[None, :]
    import ml_dtypes
    t = np.arange(SEQ) % 512
    aug = np.zeros((8, 2, SEQ), np.float32)
    for h in range(8):
        sl = 2.0 ** -(h + 1)
        aug[h, 0] = -8.0 * sl * 16.0 * (t // 16)
        aug[h, 1] = -8.0 * sl * (t % 16)
    return {"c_ident": ident, "c_alibi": al, "c_blk": blk, "c_pf": pf, "c_aug": aug.astype(ml_dtypes.bfloat16)}


PARAM_NAMES = ["norm1_g", "w_in", "q_norm_g", "k_norm_g", "lambda_q1", "lambda_k1", "lambda_q2", "lambda_k2",
               "subln_g", "ssm_a_re", "ssm_a_im", "ssm_log_dt", "ssm_b_re", "ssm_b_im", "ssm_c_re", "ssm_c_im",
               "ssm_d", "ssm_glu_w", "ssm_glu_b", "w_branch_attn", "w_branch_ssm", "w_out", "norm2_g",
               "ffn_w_up", "ffn_conv_w", "ffn_conv_b", "ffn_w_down"]


def make_in_maps(inputs):
    x = np.ascontiguousarray(np.asarray(inputs["x"], dtype=np.float32))
    shared = {k: np.ascontiguousarray(np.asarray(inputs[k], dtype=np.float32)) for k in PARAM_NAMES}
    shared.update(consts())
    maps = []
    for c in range(NCORES):
        m = dict(shared)
        m["x"] = x[2 * c:2 * c + 2].reshape(NT, D)
        maps.append(m)
    return maps


def kernel(**inputs):
    nc = build_program()
    res = run_bass_kernel_spmd(nc, make_in_maps(inputs), core_ids=list(range(NCORES)))
    out = np.stack([r["y"].reshape(2, SEQ, D) for r in res.results], axis=0).reshape(16, SEQ, D)
    return out.astype(np.float32)
```

```python
import math
from contextlib import ExitStack
import numpy as np
import concourse.bass as bass
import concourse.mybir as mybir
from concourse.bass_utils import run_bass_kernel_spmd

F32 = mybir.dt.float32
BF16 = mybir.dt.bfloat16
AF = mybir.ActivationFunctionType
ALU = mybir.AluOpType
AX = mybir.AxisListType

NCORES = 8
D = 2048
NT = 4096
SEQ = 2048
DEPTH = 4
DFF = 5504
EPS = 1e-6
NFT = DFF // 128
KGROUPS = [(0, 15), (15, 29), (29, 43)]


class Res:
    __slots__ = ("name", "w", "rs")

    def __init__(self, name):
        self.name = name
        self.w = None
        self.rs = []


class Sched:
    def __init__(self, nc, stack):
        self.nc = nc
        self.stack = stack
        self.eng = {"pe": nc.tensor, "act": nc.scalar, "dve": nc.vector, "pool": nc.gpsimd, "sp": nc.sync}
        self.sem = {}
        self.cnt = {}
        self.waited = {e: {} for e in self.eng}
        self.pending = {e: [] for e in self.eng}
        self.dma_sems = []
        self.free_dsems = []
        self.ndsem = 0
        self.cur = {}
        self.gen = 0
        self.rotate()
        self.nres = 0

    def rotate(self):
        self.gen += 1
        for e in ("pe", "act", "dve", "pool"):
            n = f"c_{e}_{self.gen}"
            self._newsem(n)
            self.cur[e] = n

    def _newsem(self, name):
        self.sem[name] = self.stack.enter_context(self.nc.semaphore(name))
        self.cnt[name] = 0

    def res(self, name=None):
        self.nres += 1
        return Res(name or f"r{self.nres}")

    def dsem(self, name=None):
        if self.free_dsems:
            n = self.free_dsems.pop()
        else:
            self.ndsem += 1
            n = f"d{self.ndsem}"
            self._newsem(n)
        self.dma_sems.append(n)
        return n

    def _wait(self, e, ev):
        if ev is None:
            return
        s, v = ev
        if e == "pe" and s.startswith("c_pe"):
            return
        if self.waited[e].get(s, 0) >= v:
            return
        self.eng[e].wait_ge(self.sem[s], v)
        self.waited[e][s] = v

    def _deps(self, e, reads, writes):
        for r in reads:
            self._wait(e, r.w)
        for w in writes:
            self._wait(e, w.w)
            for ev in w.rs:
                self._wait(e, ev)

    def op(self, e, fn, reads=(), writes=(), sig=True):
        self._deps(e, reads, writes)
        ins = fn(self.eng[e])
        if not sig:
            self.pending[e].append((tuple(reads), tuple(writes)))
            return None
        s = self.cur[e]
        self.cnt[s] += 1
        ins.then_inc(self.sem[s], 1)
        ev = (s, self.cnt[s])
        for (prs, pws) in self.pending[e]:
            for r in prs:
                r.rs.append(ev)
            for w in pws:
                w.w = ev
                w.rs = []
        self.pending[e] = []
        for r in reads:
            r.rs.append(ev)
        for w in writes:
            w.w = ev
            w.rs = []
        return ev

    def dma(self, q, out, in_, sem, reads=(), writes=(), **kw):
        self._deps(q, reads, writes)
        ins = self.eng[q].dma_start(out=out, in_=in_, **kw)
        self.cnt[sem] += 16
        ins.then_inc(self.sem[sem], 16)
        ev = (sem, self.cnt[sem])
        for r in reads:
            r.rs.append(ev)
        for w in writes:
            w.w = ev
            w.rs = []
        return ev

    def barrier(self):
        for s in self.dma_sems:
            if self.cnt[s] > self.waited["sp"].get(s, 0):
                self.eng["sp"].wait_ge(self.sem[s], self.cnt[s])
        self.nc.all_engine_barrier()
        for e in self.eng:
            for s in self.cnt:
                self.waited[e][s] = self.cnt[s]
        self.free_dsems.extend(self.dma_sems)
        self.dma_sems = []


def dap(t, off, pat):
    return bass.AP(t, off, [list(p) for p in pat])


def build_program(nlayers=DEPTH, dbg=None, PH=("n1", "ip", "at", "ss", "mx", "n2", "fu", "fd")):
    nc = bass.Bass("TRN2", target_bir_lowering=False)
    stack = ExitStack()
    S = Sched(nc, stack)
    L = DEPTH

    def din(name, shape, dt=F32):
        return nc.dram_tensor(name, list(shape), dt, kind="ExternalInput")

    def dscr(name, shape, dt, out=False):
        return nc.dram_tensor(name, list(shape), dt, kind=("ExternalOutput" if out else "Internal"))

    x_in = din("x", [NT, D])
    y_out = nc.dram_tensor("y", [NT, D], F32, kind="ExternalOutput")
    norm1_g = din("norm1_g", [L, D]); norm2_g = din("norm2_g", [L, D])
    w_in = din("w_in", [L, D, 8192])
    q_norm_g = din("q_norm_g", [L, 64]); k_norm_g = din("k_norm_g", [L, 64])
    lam_q1 = din("lambda_q1", [L, 64]); lam_k1 = din("lambda_k1", [L, 64])
    lam_q2 = din("lambda_q2", [L, 64]); lam_k2 = din("lambda_k2", [L, 64])
    subln_g = din("subln_g", [L, 128])
    a_re = din("ssm_a_re", [L, 64, 64]); a_im = din("ssm_a_im", [L, 64, 64])
    log_dt = din("ssm_log_dt", [L, 64])
    b_re = din("ssm_b_re", [L, 64, 64, 16]); b_im = din("ssm_b_im", [L, 64, 64, 16])
    c_re = din("ssm_c_re", [L, 64, 16, 64]); c_im = din("ssm_c_im", [L, 64, 16, 64])
    ssm_d = din("ssm_d", [L, 1024])
    glu_w = din("ssm_glu_w", [L, 1024, 1024]); glu_b = din("ssm_glu_b", [L, 1024])
    w_ba = din("w_branch_attn", [L, 1024, D]); w_bs = din("w_branch_ssm", [L, 1024, D])
    w_out = din("w_out", [L, D, D])
    w_up = din("ffn_w_up", [L, D, 2 * DFF]); conv_w = din("ffn_conv_w", [L, 3, 2 * DFF])
    conv_b = din("ffn_conv_b", [L, 2 * DFF]); w_down = din("ffn_w_down", [L, DFF, D])
    c_ident = din("c_ident", [128, 128])
    c_alibi = din("c_alibi", [128, 5, 512])
    c_blk = din("c_blk", [128, 128])
    c_pf = din("c_pf", [128, 513])
    c_aug = din("c_aug", [8, 2, SEQ], BF16)

    isdbg = lambda n: dbg is not None and n in dbg
    XT = dscr("XT", [16, 128, NT], F32, isdbg("XT"))
    H = dscr("H", [16, 128, NT], BF16, isdbg("H"))
    QT = dscr("QT", [8, 128, NT], BF16, isdbg("QT"))
    KT = dscr("KT", [8, 128, NT], BF16, isdbg("KT"))
    V = dscr("V", [NT, 1024], BF16, isdbg("V"))
    U2 = dscr("U2", [64, 8, 16, 512], BF16, isdbg("U2"))
    GA = dscr("GA", [16, 128, NT], BF16, isdbg("GA"))
    GS = dscr("GS", [16, 128, NT], BF16, isdbg("GS"))
    OT = dscr("OT", [8, 128, NT], BF16, isdbg("OT"))
    Y2 = dscr("Y2", [64, 8, 16, 512], BF16, isdbg("Y2"))
    AT = dscr("AT", [NFT, 128, NT], BF16, isdbg("AT"))

    PS = [nc.alloc_psum_tensor(f"ps{i}", [128, 512], F32) for i in range(8)]
    PSR = [S.res(f"ps{i}") for i in range(8)]

    _uid = [0]

    def sb(st, name, shape, dt):
        _uid[0] += 1
        return st.enter_context(nc.sbuf_tensor(f"{name}_{_uid[0]}", list(shape), dt))

    ident = sb(stack, "ident", [128, 128], F32)
    identb = sb(stack, "identb", [128, 128], BF16)
    onesb = sb(stack, "onesb", [128, 128], BF16)
    blkb = sb(stack, "blkb", [128, 128], BF16)
    blkf = sb(stack, "blkf", [128, 128], F32)
    epsc = sb(stack, "epsc", [128, 1], F32)
    r_const = S.res("const")
    s_const = S.dsem("d_const")
    S.dma("sp", ident[:], c_ident.ap(), s_const, writes=[r_const])
    S.dma("sp", blkf[:], c_blk.ap(), s_const, writes=[r_const])
    S.op("dve", lambda e: e.tensor_copy(out=identb[:], in_=ident[:]), reads=[r_const], writes=[r_const])
    S.op("dve", lambda e: e.tensor_copy(out=blkb[:], in_=blkf[:]), reads=[r_const], writes=[r_const])
    S.op("dve", lambda e: e.memset(onesb[:], 1.0), writes=[r_const])
    S.op("dve", lambda e: e.memset(epsc[:], EPS), writes=[r_const])
    S.barrier()

    def phase_transpose_in():
        with ExitStack() as st:
            xin = [sb(st, f"p0x{i}", [128, 4, D], F32) for i in range(2)]
            xo = [sb(st, f"p0o{i}", [128, 16, 512], F32) for i in range(2)]
            r_in = [S.res() for _ in range(2)]; r_o = [S.res() for _ in range(2)]
            s_in = [S.dsem(f"p0si{i}") for i in range(2)]; s_o = [S.dsem(f"p0so{i}") for i in range(2)]
            xv = x_in.ap().rearrange("(n s p) d -> n p s d", s=4, p=128)
            XTv = XT.ap().rearrange("c p n -> p c n")
            for j in range(8):
                b = j % 2
                S.dma("sp", xin[b][:], xv[j], s_in[b], writes=[r_in[b]])
                for c in range(16):
                    pb = c % 8
                    for s4 in range(4):
                        last = s4 == 3
                        S.op("pe", lambda e, s4=s4, c=c, pb=pb: e.transpose(
                            out=PS[pb][:, s4 * 128:(s4 + 1) * 128], in_=xin[b][:, s4, c * 128:(c + 1) * 128],
                            identity=ident[:]), reads=[r_in[b]], writes=[PSR[pb]], sig=last)
                    ee = "act" if c % 2 == 0 else "dve"
                    if ee == "act":
                        S.op("act", lambda e, c=c, pb=pb: e.activation(out=xo[b][:, c, :], in_=PS[pb][:], func=AF.Copy),
                             reads=[PSR[pb]], writes=[r_o[b]])
                    else:
                        S.op("dve", lambda e, c=c, pb=pb: e.tensor_copy(out=xo[b][:, c, :], in_=PS[pb][:]),
                             reads=[PSR[pb]], writes=[r_o[b]])
                S.dma("act", XTv[:, :, j * 512:(j + 1) * 512], xo[b][:], s_o[b], reads=[r_o[b]])
            S.barrier()

    def phase_norm(gain_dram, l, tag, Hs, r_H):
        with ExitStack() as st:
            g = sb(st, tag + "g", [128, 16], F32)
            xs = [sb(st, f"{tag}x{i}", [128, 16, 256], F32) for i in range(2)]
            sq = sb(st, tag + "sq", [128, 16, 256], BF16)
            rs = [sb(st, f"{tag}rs{i}", [128, 256], F32) for i in range(2)]
            r_g = S.res(); r_x = [S.res() for _ in range(2)]; r_sq = S.res(); r_rs = [S.res() for _ in range(2)]
            s_g = S.dsem(); s_x = [S.dsem() for i in range(2)]
            S.dma("sp", g[:], gain_dram.ap()[l].rearrange("(c p) -> p c", p=128), s_g, writes=[r_g],
                  allow_slow_non_contiguous=True)
            XTv = XT.ap().rearrange("c p n -> p c n")
            for j in range(16):
                b = j % 2
                tsl = slice(j * 256, (j + 1) * 256)
                S.dma("sp", xs[b][:], XTv[:, :, tsl], s_x[b], writes=[r_x[b]])
                S.op("act", lambda e: e.activation(out=sq[:], in_=xs[b][:], func=AF.Square), reads=[r_x[b]], writes=[r_sq])
                pb = j % 8
                for c in range(16):
                    S.op("pe", lambda e, c=c: e.matmul(PS[pb][:, 0:256], onesb[:], sq[:, c, :], start=(c == 0), stop=(c == 15)),
                         reads=[r_sq], writes=[PSR[pb]], sig=(c == 15))
                S.op("act", lambda e: e.activation(out=rs[b][:], in_=PS[pb][:, 0:256], func=AF.Ln, bias=epsc[:], scale=1.0 / D),
                     reads=[PSR[pb]], writes=[r_rs[b]])
                S.op("act", lambda e: e.activation(out=rs[b][:], in_=rs[b][:], func=AF.Exp, scale=-0.5), reads=[r_rs[b]], writes=[r_rs[b]])
                for c in range(16):
                    S.op("dve", lambda e, c=c: e.scalar_tensor_tensor(out=Hs[:, c, tsl], in0=xs[b][:, c, :], scalar=g[:, c:c + 1],
                                                                 in1=rs[b][:], op0=ALU.mult, op1=ALU.mult),
                         reads=[r_x[b], r_rs[b], r_g], writes=[r_H])
            S.barrier()

    def phase_inproj(l):
        with ExitStack() as st:
            Hs = sb(st, "p2H", [128, 16, NT], BF16)
            r_H = S.res()
            phase_norm(norm1_g, l, f"n1{l}", Hs, r_H)
            wv = w_in.ap()[l].rearrange("(c p) n -> p c n", p=128)
            ws = WStream(st, "p2", 16, [(wv[:, :, m * 128:(m + 1) * 128], 16) for m in range(64)])
            qg = sb(st, "p2qg", [128, 2], F32)
            sqb = [sb(st, f"p2sq{i}", [128, 512], BF16) for i in range(2)]
            rsb = [sb(st, f"p2rs{i}", [128, 512], F32) for i in range(2)]
            NO = 4
            ob = [sb(st, f"p2o{i}", [128, 512], BF16) for i in range(NO)]
            vb = [sb(st, f"p2v{i}", [128, 512], BF16) for i in range(2)]
            r_qg = S.res(); r_sq = [S.res() for _ in range(2)]; r_rs = [S.res() for _ in range(2)]
            r_o = [S.res() for _ in range(NO)]; r_v = [S.res() for _ in range(2)]
            s_H = S.dsem("p2sH")
            s_qg = S.dsem("p2sqg"); s_o = [S.dsem(f"p2so{i}") for i in range(NO)]
            for hh in range(2):
                S.dma("sp", qg[hh * 64:(hh + 1) * 64, 0:1], dap(q_norm_g, l * 64, [[1, 64], [1, 1]]), s_qg, writes=[r_qg])
                S.dma("sp", qg[hh * 64:(hh + 1) * 64, 1:2], dap(k_norm_g, l * 64, [[1, 64], [1, 1]]), s_qg, writes=[r_qg])
            Vv = V.ap().rearrange("(n s p) f -> n p s f", s=4, p=128)
            U2v = U2.ap()
            cnt = 0
            oc = 0
            for m in range(64):
                wbm, r_wbm = ws.next()
                for j in range(8):
                    pb = cnt % 6
                    cnt += 1
                    for c in range(16):
                        S.op("pe", lambda e, c=c: e.matmul(PS[pb][:], wbm[:, c, :], Hs[:, c, j * 512:(j + 1) * 512],
                                                           start=(c == 0), stop=(c == 15)),
                             reads=[r_wbm, r_H], writes=[PSR[pb]], sig=(c == 15))
                    oi = oc % NO
                    oc += 1
                    tsl = slice(j * 512, (j + 1) * 512)
                    if m < 16:
                        kq = 0 if m < 8 else 1
                        hh = m % 8
                        qi = cnt % 2
                        pb2 = 6 + (cnt % 2)
                        S.op("act", lambda e: e.activation(out=sqb[qi][:], in_=PS[pb][:], func=AF.Square),
                             reads=[PSR[pb]], writes=[r_sq[qi]])
                        S.op("pe", lambda e: e.matmul(PS[pb2][:], blkb[:], sqb[qi][:], start=True, stop=True),
                             reads=[r_sq[qi]], writes=[PSR[pb2]])
                        S.op("act", lambda e: e.activation(out=rsb[qi][:], in_=PS[pb2][:], func=AF.Ln, bias=epsc[:], scale=1.0 / 64),
                             reads=[PSR[pb2]], writes=[r_rs[qi]])
                        S.op("act", lambda e: e.activation(out=rsb[qi][:], in_=rsb[qi][:], func=AF.Exp, scale=-0.5),
                             reads=[r_rs[qi]], writes=[r_rs[qi]])
                        S.op("dve", lambda e: e.scalar_tensor_tensor(out=ob[oi][:], in0=PS[pb][:], scalar=qg[:, kq:kq + 1],
                                                                     in1=rsb[qi][:], op0=ALU.mult, op1=ALU.mult),
                             reads=[PSR[pb], r_rs[qi], r_qg], writes=[r_o[oi]])
                        dst = (QT if kq == 0 else KT).ap()[hh][:, tsl]
                        S.dma("act", dst, ob[oi][:], s_o[oi], reads=[r_o[oi]])
                    elif m < 24:
                        hh = m - 16
                        vi = cnt % 2
                        pb2 = 6 + (cnt % 2)
                        S.op("act", lambda e: e.activation(out=vb[vi][:], in_=PS[pb][:], func=AF.Copy),
                             reads=[PSR[pb]], writes=[r_v[vi]])
                        pv = PS[pb2].bitcast(BF16)
                        for s4 in range(4):
                            S.op("pe", lambda e, s4=s4: e.transpose(out=pv[:, s4 * 128:(s4 + 1) * 128],
                                                                    in_=vb[vi][:, s4 * 128:(s4 + 1) * 128], identity=identb[:]),
                                 reads=[r_v[vi]], writes=[PSR[pb2]], sig=(s4 == 3))
                        S.op("dve", lambda e: e.tensor_copy(out=ob[oi][:], in_=pv[:, 0:512]), reads=[PSR[pb2]], writes=[r_o[oi]])
                        S.dma("act", Vv[j][:, :, hh * 128:(hh + 1) * 128], ob[oi][:].rearrange("p (s f) -> p s f", s=4),
                              s_o[oi], reads=[r_o[oi]])
                    elif m < 32:
                        mu = m - 24
                        S.op("act", lambda e: e.activation(out=ob[oi][:].rearrange("p (s c) -> p c s", s=8),
                                                           in_=PS[pb][:].rearrange("p (c s) -> p c s", s=8), func=AF.Copy),
                             reads=[PSR[pb]], writes=[r_o[oi]])
                        for gl in range(8):
                            gidx = mu * 8 + gl
                            S.dma("act", U2v[gidx].rearrange("s i n -> i s n")[:, :, j * 64:(j + 1) * 64],
                                  ob[oi][gl * 16:(gl + 1) * 16, :].rearrange("p (s c) -> p s c", s=8),
                                  s_o[oi], reads=[r_o[oi]])
                    else:
                        mg = m - 32
                        S.op("act", lambda e: e.activation(out=ob[oi][:], in_=PS[pb][:], func=AF.Sigmoid),
                             reads=[PSR[pb]], writes=[r_o[oi]])
                        dst = (GA if mg < 16 else GS).ap()[mg % 16][:, tsl]
                        S.dma("act", dst, ob[oi][:], s_o[oi], reads=[r_o[oi]])
            S.barrier()


    def phase_attn(l, tick=None):
        lam_init = 0.8 - 0.6 * math.exp(-0.3 * l)
        with ExitStack() as st:
            alib = sb(st, "atal", [128, 5, 512], F32)
            lamt = sb(st, "atlam", [128, 4, 64], F32)
            ltmp = sb(st, "atlt", [128, 64], F32)
            lsc = sb(st, "atls", [128, 8], F32)
            sg = sb(st, "atsg", [128, 1], F32)
            btab = sb(st, "atbt", [128, 8, 13], F32)
            btab2 = sb(st, "atbt2", [128, 8, 13], F32)
            pcol = sb(st, "atpc", [128, 8], F32)
            cpf = sb(st, "atcpf", [128, 513], F32)
            qT = [sb(st, f"atq{i}", [128, 2, SEQ], BF16) for i in range(2)]
            kT = [sb(st, f"atk{i}", [128, 2, SEQ], BF16) for i in range(2)]
            vt = [sb(st, f"atv{i}", [128, 16, 128], BF16) for i in range(2)]
            NB = 6
            sbias = [sb(st, f"atsb{i}", [128, 512], F32) for i in range(NB)]
            Eb = [sb(st, f"atE{i}", [128, 512], BF16) for i in range(NB)]
            fr2 = sb(st, "atfr2", [128, 512], F32); r_fr2 = S.res()
            o1s = sb(st, "ato1s", [128, 512], F32); r_o1s = S.res()
            o2s = sb(st, "ato2s", [128, 512], F32); r_o2s = S.res()
            fr = sb(st, "atfr", [128, 512], F32)
            ft1 = sb(st, "atft1", [128, 512], F32)
            ft2 = sb(st, "atft2", [128, 512], F32)
            fo = sb(st, "atfo", [128, 512], F32)
            fsq = sb(st, "atfsq", [128, 512], BF16)
            frs = sb(st, "atfrs", [128, 512], F32)
            fon = [sb(st, f"atfon{i}", [128, 512], BF16) for i in range(2)]
            r_c = S.res(); r_q = [S.res() for _ in range(2)]
            r_sb = [S.res() for _ in range(NB)]; r_E = [S.res() for _ in range(NB)]
            r_fr = S.res(); r_ft1 = S.res(); r_ft2 = S.res(); r_fo = S.res(); r_fsq = S.res(); r_frs = S.res()
            r_fon = [S.res() for _ in range(2)]
            s_c = S.dsem("atsc"); s_q = [S.dsem(f"atsq{i}") for i in range(2)]
            s_on = [S.dsem(f"atson{i}") for i in range(2)]
            S.dma("sp", alib[:], c_alibi.ap(), s_c, writes=[r_c])
            for idx, t in enumerate([lam_q1, lam_k1, lam_q2, lam_k2]):
                S.dma("sp", lamt[:, idx, :], dap(t, l * 64, [[0, 128], [1, 64]]), s_c, writes=[r_c])
            S.dma("sp", sg[:], dap(subln_g, l * 128, [[1, 128], [1, 1]]), s_c, writes=[r_c])
            for k2 in range(2):
                S.op("dve", lambda e: e.tensor_tensor(out=ltmp[:], in0=lamt[:, 2 * k2, :], in1=lamt[:, 2 * k2 + 1, :], op=ALU.mult),
                     reads=[r_c], writes=[r_c])
                S.op("dve", lambda e: e.tensor_reduce(out=lsc[:, k2:k2 + 1], in_=ltmp[:], axis=AX.X, op=ALU.add),
                     reads=[r_c], writes=[r_c])
            S.op("act", lambda e: e.activation(out=lsc[:, 2:4], in_=lsc[:, 0:2], func=AF.Exp), reads=[r_c], writes=[r_c])
            S.op("dve", lambda e: e.tensor_tensor(out=lsc[:, 4:5], in0=lsc[:, 3:4], in1=lsc[:, 2:3], op=ALU.subtract),
                 reads=[r_c], writes=[r_c])
            S.op("dve", lambda e: e.tensor_scalar(out=lsc[:, 5:6], in0=lsc[:, 4:5], scalar1=-lam_init, scalar2=None, op0=ALU.add),
                 reads=[r_c], writes=[r_c])
            S.op("dve", lambda e: e.tensor_scalar(out=sg[:], in0=sg[:], scalar1=(1.0 - lam_init), scalar2=None, op0=ALU.mult),
                 reads=[r_c], writes=[r_c])
            for hh in range(8):
                for rel in range(13):
                    S.op("pool", lambda e: e.memset(btab[:, hh, rel:rel + 1], -(2.0 ** -(hh + 1)) * 128.0 * rel), writes=[r_c])
            S.dma("sp", cpf[:], c_pf.ap(), s_c, writes=[r_c])
            for hh in range(8):
                sl_ = 2.0 ** -(hh + 1)
                S.op("dve", lambda e: e.tensor_scalar(out=pcol[:, hh:hh + 1], in0=cpf[:, 0:1], scalar1=sl_, scalar2=None, op0=ALU.mult),
                     reads=[r_c], writes=[r_c])
                S.op("dve", lambda e: e.tensor_scalar(out=btab2[:, hh, :], in0=btab[:, hh, :], scalar1=pcol[:, hh:hh + 1], scalar2=None,
                                                      op0=ALU.add), reads=[r_c], writes=[r_c])
            nlam = lsc[:, 5:6]
            for qi in range(2):
                S.op("dve", lambda e: e.memset(kT[qi][64:66, :, :], 1.0), writes=[r_q[qi]])
            it = 0
            bh = 0
            pending = [None]
            for b in range(2):
                for hh in range(8):
                    qi = bh % 2
                    bh += 1
                    tok = slice(b * SEQ, (b + 1) * SEQ)
                    for c2 in range(2):
                        S.dma("sp", qT[qi][0:64, c2, :], QT.ap()[hh][c2 * 64:(c2 + 1) * 64, tok], s_q[qi], writes=[r_q[qi]])
                        S.dma("sp", kT[qi][0:64, c2, :], KT.ap()[hh][c2 * 64:(c2 + 1) * 64, tok], s_q[qi], writes=[r_q[qi]])
                        S.dma("sp", qT[qi][64:66, c2, :], c_aug.ap()[hh], s_q[qi], writes=[r_q[qi]])
                    S.dma("sp", vt[qi][:], V.ap()[tok, hh * 128:(hh + 1) * 128].rearrange("(t p) f -> p t f", p=128),
                          s_q[qi], writes=[r_q[qi]])
                    slope = 2.0 ** -(hh + 1)
                    for j in range(4):
                        nk = 4 * (j + 1)
                        slots = {}

                        def keep(i):
                            rel = 4 * j - i
                            return rel <= 0 or slope * (128 * rel - 127) <= 40.0
                        tiles = [i for i in range(nk) if keep(i)]
                        nkk = len(tiles)

                        def emit_S(n):
                            nonlocal it
                            i = tiles[n]
                            a = 2 * (n % 2)
                            ksl = slice(i * 128, (i + 1) * 128)
                            rel = 4 * j - i
                            c0 = 0 if rel >= 1 else -rel * 128
                            pidx = 0 if rel >= 1 else 1 - rel
                            cs_ = slice(c0, 512)
                            qs_ = slice(j * 512 + c0, (j + 1) * 512)
                            kk = 66 if rel >= 1 else 64
                            xs2 = []
                            for c2 in range(2):
                                S.op("pe", lambda e: e.matmul(PS[a + c2][:, cs_], kT[qi][0:kk, c2, ksl], qT[qi][0:kk, c2, qs_], start=True, stop=True),
                                     reads=[r_q[qi]], writes=[PSR[a + c2]])
                            for c2 in range(2):
                                x = it % NB
                                it += 1
                                xs2.append(x)
                                if rel >= 1:
                                    S.op("act", lambda e: e.activation(out=Eb[x][:], in_=PS[a + c2][:], func=AF.Exp,
                                                                       bias=btab2[:, hh, rel:rel + 1], scale=0.125),
                                         reads=[PSR[a + c2], r_c], writes=[r_E[x]])
                                else:
                                    S.op("dve", lambda e: e.scalar_tensor_tensor(out=sbias[x][:, cs_], in0=alib[:, pidx, cs_], scalar=-8.0 * slope,
                                                                                 in1=PS[a + c2][:, cs_], op0=ALU.mult, op1=ALU.add),
                                         reads=[r_c, PSR[a + c2]], writes=[r_sb[x]])
                                    S.op("act", lambda e: e.activation(out=Eb[x][:, cs_], in_=sbias[x][:, cs_], func=AF.Exp, scale=0.125),
                                         reads=[r_sb[x]], writes=[r_E[x]])
                            if tick is not None:
                                tick()
                            slots[n] = (xs2, cs_)

                        def emit_OZ(n):
                            i = tiles[n]
                            (x1, x2), cs_ = slots[n]
                            st_, sp_ = (n == 0), (n == nkk - 1)
                            S.op("pe", lambda e: e.matmul(PS[4][:, cs_], vt[qi][:, i, :], Eb[x1][:, cs_], start=st_, stop=sp_),
                                 reads=[r_q[qi], r_E[x1]], writes=[PSR[4]])
                            S.op("pe", lambda e: e.matmul(PS[5][:, cs_], onesb[:], Eb[x1][:, cs_], start=st_, stop=sp_),
                                 reads=[r_E[x1]], writes=[PSR[5]])
                            S.op("pe", lambda e: e.matmul(PS[6][:, cs_], vt[qi][:, i, :], Eb[x2][:, cs_], start=st_, stop=sp_),
                                 reads=[r_q[qi], r_E[x2]], writes=[PSR[6]])
                            S.op("pe", lambda e: e.matmul(PS[7][:, cs_], onesb[:], Eb[x2][:, cs_], start=st_, stop=sp_),
                                 reads=[r_E[x2]], writes=[PSR[7]])

                        for n in range(nkk + 2):
                            if n < nkk:
                                emit_S(n)
                            if n == 1 and pending[0] is not None:
                                pending[0]()
                                pending[0] = None
                            if n >= 2:
                                emit_OZ(n - 2)

                        S.op("act", lambda e: e.activation(out=fr[:], in_=PS[5][:], func=AF.Ln), reads=[PSR[5]], writes=[r_fr])
                        S.op("act", lambda e: e.activation(out=fr2[:], in_=PS[7][:], func=AF.Ln), reads=[PSR[7]], writes=[r_fr2])
                        S.op("dve", lambda e: e.tensor_copy(out=o1s[:], in_=PS[4][:]), reads=[PSR[4]], writes=[r_o1s])
                        S.op("dve", lambda e: e.tensor_copy(out=o2s[:], in_=PS[6][:]), reads=[PSR[6]], writes=[r_o2s])

                        def finalize(b=b, hh=hh, j=j, oi=(bh * 4 + j) % 2):
                            S.op("act", lambda e: e.activation(out=fr[:], in_=fr[:], func=AF.Exp, scale=-1.0), reads=[r_fr], writes=[r_fr])
                            S.op("pool", lambda e: e.tensor_tensor(out=ft1[:], in0=o1s[:], in1=fr[:], op=ALU.mult),
                                 reads=[r_o1s, r_fr], writes=[r_ft1])
                            S.op("act", lambda e: e.activation(out=fr2[:], in_=fr2[:], func=AF.Exp, scale=-1.0), reads=[r_fr2], writes=[r_fr2])
                            S.op("dve", lambda e: e.tensor_tensor(out=ft2[:], in0=o2s[:], in1=fr2[:], op=ALU.mult),
                                 reads=[r_o2s, r_fr2], writes=[r_ft2])
                            S.op("dve", lambda e: e.scalar_tensor_tensor(out=fo[:], in0=ft2[:], scalar=nlam, in1=ft1[:],
                                                                         op0=ALU.mult, op1=ALU.add),
                                 reads=[r_ft1, r_ft2, r_c], writes=[r_fo])
                            S.op("pool", lambda e: e.tensor_tensor(out=fsq[:], in0=fo[:], in1=fo[:], op=ALU.mult), reads=[r_fo], writes=[r_fsq])
                            S.op("pe", lambda e: e.matmul(PS[5][:], onesb[:], fsq[:], start=True, stop=True), reads=[r_fsq], writes=[PSR[5]])
                            S.op("act", lambda e: e.activation(out=frs[:], in_=PS[5][:], func=AF.Ln, bias=epsc[:], scale=1.0 / 128),
                                 reads=[PSR[5]], writes=[r_frs])
                            S.op("act", lambda e: e.activation(out=frs[:], in_=frs[:], func=AF.Exp, scale=-0.5), reads=[r_frs], writes=[r_frs])
                            S.op("dve", lambda e: e.scalar_tensor_tensor(out=fon[oi][:], in0=fo[:], scalar=sg[:, 0:1], in1=frs[:],
                                                                         op0=ALU.mult, op1=ALU.mult),
                                 reads=[r_fo, r_frs, r_c], writes=[r_fon[oi]])
                            S.dma("act", OT.ap()[hh][:, b * SEQ + j * 512:b * SEQ + (j + 1) * 512], fon[oi][:], s_on[oi], reads=[r_fon[oi]])
                        pending[0] = finalize
            pending[0]()
            S.barrier()

    def phase_ssm(l, attn_fn):
        def bc(t, off, rowlen, n1, n2):
            return bass.AP(t, off, [[rowlen, 128], [1, n1], [0, n2]])
        with ExitStack() as st:
            W1b = sb(st, "ssW1", [128, 64, 128], BF16)
            W4re = sb(st, "ssW4r", [128, 32, 128], BF16); W4im = sb(st, "ssW4i", [128, 32, 128], BF16)
            AR2 = sb(st, "ssAR2", [128, 2, 32], F32); NAI = sb(st, "ssNAI", [128, 2, 32], F32)
            W2re = sb(st, "ssW2r", [128, 64, 64], BF16); W2im = sb(st, "ssW2i", [128, 64, 64], BF16)
            r_W = S.res()
            r_SH = [S.res() for _ in range(2)]; r_H = [S.res() for _ in range(2)]
            with ExitStack() as st2:
                An = sb(st2, "ssAn", [32, 2, 128], F32)
                Are = sb(st2, "ssAre", [128, 32], F32); Aim = sb(st2, "ssAim", [128, 32], F32)
                Ldt = sb(st2, "ssLdt", [128, 32], F32)
                dre = sb(st2, "ssdre", [128, 32], F32); dim = sb(st2, "ssdim", [128, 32], F32)
                mag = sb(st2, "ssmag", [128, 32], F32)
                cs = sb(st2, "sscs", [128, 2, 32], F32)
                t1 = sb(st2, "sst1", [128, 32], F32); t2 = sb(st2, "sst2", [128, 32], F32)
                PW = sb(st2, "ssPW", [128, 9, 2, 32], F32)
                FF = sb(st2, "ssFF", [128, 2, 32], F32)
                FP = sb(st2, "ssFP", [128, 8, 2, 32], F32)
                hpi = sb(st2, "sshpi", [128, 1], F32)
                Bre = sb(st2, "ssBre", [128, 32, 16], F32); Bim = sb(st2, "ssBim", [128, 32, 16], F32)
                T1 = sb(st2, "ssT1", [128, 32, 16], F32); T2 = sb(st2, "ssT2", [128, 32, 16], F32)
                ABr = [sb(st2, f"ssAB{i}", [128, 32, 240], F32) for i in range(2)]
                ABrb = [sb(st2, f"ssABb{i}", [128, 32, 240], BF16) for i in range(2)]
                CTb = [sb(st2, f"ssCTb{i}", [128, 32, 16], BF16) for i in range(2)]
                Cn = sb(st2, "ssCn", [128, 2, 64], F32)
                CT = [sb(st2, f"ssCT{i}", [128, 32, 16], F32) for i in range(3)]
                dgi = sb(st2, "ssdgi", [64, 8, 16], F32)
                Drep = sb(st2, "ssDrep", [128, 64], F32)
                r_p = S.res(); s_p = S.dsem(); r_cn = S.res(); s_cn = S.dsem()
                D_ = lambda fn, rd=(), wr=(): S.op("dve", fn, reads=[r_p] + list(rd), writes=[r_p] + list(wr))
                A_ = lambda fn, rd=(), wr=(): S.op("act", fn, reads=[r_p] + list(rd), writes=[r_p] + list(wr))
                TT = lambda o, a, b_, op: D_(lambda e: e.tensor_tensor(out=o, in0=a, in1=b_, op=op))
                for k2, src in enumerate([a_re, a_im]):
                    for gh in range(2):
                        S.dma("sp", An[:, k2, gh * 64:(gh + 1) * 64], src.ap()[l][gh * 32:(gh + 1) * 32, :], s_p, writes=[r_p])
                for gh in range(2):
                    S.dma("sp", Ldt[gh * 64:(gh + 1) * 64, :], dap(log_dt, l * 64 + gh * 32, [[0, 64], [1, 32]]), s_p, writes=[r_p])
                    S.dma("sp", Bre[gh * 64:(gh + 1) * 64, :, :], b_re.ap()[l][gh * 32:(gh + 1) * 32].rearrange("g p i -> p g i"), s_p, writes=[r_p])
                    S.dma("sp", Bim[gh * 64:(gh + 1) * 64, :, :], b_im.ap()[l][gh * 32:(gh + 1) * 32].rearrange("g p i -> p g i"), s_p, writes=[r_p])
                S.dma("sp", dgi[:], dap(ssm_d, l * 1024, [[16, 64], [0, 8], [1, 16]]), s_p, writes=[r_p])
                D_(lambda e: e.memset(hpi[:], math.pi / 2))
                for k2, dst in enumerate([Are, Aim]):
                    S.op("pe", lambda e: e.transpose(out=PS[k2][:, 0:32], in_=An[:, k2, :], identity=ident[0:32, 0:32]),
                         reads=[r_p], writes=[PSR[k2]])
                    D_(lambda e: e.tensor_copy(out=dst[:], in_=PS[k2][:, 0:32]), rd=[PSR[k2]])
                A_(lambda e: e.activation(out=Ldt[:], in_=Ldt[:], func=AF.Exp))
                TT(dre[:], Ldt[:], Are[:], ALU.mult)
                TT(dim[:], Ldt[:], Aim[:], ALU.mult)
                A_(lambda e: e.activation(out=mag[:], in_=dre[:], func=AF.Exp))
                A_(lambda e: e.activation(out=cs[:, 1, :], in_=dim[:], func=AF.Sin, scale=1.0 / 16))
                A_(lambda e: e.activation(out=cs[:, 0, :], in_=dim[:], func=AF.Sin, bias=hpi[:], scale=-1.0 / 16))
                for _ in range(4):
                    TT(t1[:], cs[:, 0, :], cs[:, 0, :], ALU.mult)
                    TT(t2[:], cs[:, 1, :], cs[:, 1, :], ALU.mult)
                    D_(lambda e: e.scalar_tensor_tensor(out=cs[:, 1, :], in0=cs[:, 0, :], scalar=2.0, in1=cs[:, 1, :],
                                                        op0=ALU.mult, op1=ALU.mult))
                    TT(cs[:, 0, :], t1[:], t2[:], ALU.subtract)
                D_(lambda e: e.memset(PW[:, 0, 0, :], 1.0))
                D_(lambda e: e.memset(PW[:, 0, 1, :], 0.0))
                TT(PW[:, 1, 0, :], mag[:], cs[:, 0, :], ALU.mult)
                TT(PW[:, 1, 1, :], mag[:], cs[:, 1, :], ALU.mult)
                ar, ai = PW[:, 1, 0, :], PW[:, 1, 1, :]

                def cmul(o_re, o_im, x_re, x_im, y_re, y_im):
                    TT(t1[:], x_re, y_re, ALU.mult); TT(t2[:], x_im, y_im, ALU.mult)
                    TT(o_re, t1[:], t2[:], ALU.subtract)
                    TT(t1[:], x_re, y_im, ALU.mult); TT(t2[:], x_im, y_re, ALU.mult)
                    TT(o_im, t1[:], t2[:], ALU.add)
                for k in range(2, 9):
                    cmul(PW[:, k, 0, :], PW[:, k, 1, :], PW[:, k - 1, 0, :], PW[:, k - 1, 1, :], ar, ai)
                TT(t1[:], Are[:], Are[:], ALU.mult); TT(t2[:], Aim[:], Aim[:], ALU.mult)
                TT(mag[:], t1[:], t2[:], ALU.add)
                D_(lambda e: e.reciprocal(out=mag[:], in_=mag[:]))
                D_(lambda e: e.tensor_scalar(out=dre[:], in0=ar, scalar1=-1.0, scalar2=None, op0=ALU.add))
                TT(t1[:], dre[:], Are[:], ALU.mult); TT(t2[:], ai, Aim[:], ALU.mult)
                TT(t1[:], t1[:], t2[:], ALU.add); TT(FF[:, 0, :], t1[:], mag[:], ALU.mult)
                TT(t1[:], ai, Are[:], ALU.mult); TT(t2[:], dre[:], Aim[:], ALU.mult)
                TT(t1[:], t1[:], t2[:], ALU.subtract); TT(FF[:, 1, :], t1[:], mag[:], ALU.mult)
                for k in range(8):
                    cmul(FP[:, k, 0, :], FP[:, k, 1, :], PW[:, k, 0, :], PW[:, k, 1, :], FF[:, 0, :], FF[:, 1, :])
                D_(lambda e: e.memset(ABr[0][:], 0.0)); D_(lambda e: e.memset(ABr[1][:], 0.0))
                for tau in range(8):
                    bk = 7 - tau
                    fr_ = bc(FP, tau * 64, 512, 32, 16); fi_ = bc(FP, tau * 64 + 32, 512, 32, 16)
                    osl = slice(bk * 16, (bk + 1) * 16)
                    TT(T1[:], Bre[:], fr_, ALU.mult); TT(T2[:], Bim[:], fi_, ALU.mult)
                    TT(ABr[0][:, :, osl], T1[:], T2[:], ALU.subtract)
                    TT(T1[:], Bim[:], fr_, ALU.mult); TT(T2[:], Bre[:], fi_, ALU.mult)
                    TT(ABr[1][:, :, osl], T1[:], T2[:], ALU.add)
                for k2, src in enumerate([c_re, c_im]):
                    for blk in range(4):
                        for gh in range(2):
                            g0 = gh * 32 + blk * 8
                            S.dma("sp", Cn[:, gh, :], src.ap()[l][g0:g0 + 8].rearrange("g o p -> (g o) p"), s_cn, writes=[r_cn])
                        pb = 2 + (k2 * 4 + blk) % 2
                        S.op("pe", lambda e: e.transpose(out=PS[pb][:, 0:128], in_=Cn[:].rearrange("p a b -> p (a b)"), identity=ident[:]),
                             reads=[r_cn], writes=[PSR[pb]])
                        D_(lambda e: e.tensor_copy(out=CT[k2][:, blk * 8:(blk + 1) * 8, :].rearrange("p a b -> p (a b)"), in_=PS[pb][:, 0:128]),
                           rd=[PSR[pb]])
                D_(lambda e: e.tensor_scalar(out=CT[2][:], in0=CT[1][:], scalar1=-1.0, scalar2=None, op0=ALU.mult))
                D_(lambda e: e.tensor_copy(out=ABrb[0][:], in_=ABr[0][:]))
                A_(lambda e: e.activation(out=ABrb[1][:], in_=ABr[1][:], func=AF.Copy))
                D_(lambda e: e.tensor_copy(out=CTb[0][:], in_=CT[0][:]))
                D_(lambda e: e.tensor_copy(out=CTb[1][:], in_=CT[2][:]))
                S.op("pe", lambda e: e.matmul(PS[4][:, 0:64], dgi[:].rearrange("p a b -> p (a b)"), ident[0:64, 0:64], start=True, stop=True),
                     reads=[r_p], writes=[PSR[4]])
                D_(lambda e: e.tensor_copy(out=Drep[:], in_=PS[4][:, 0:64]), rd=[PSR[4]])
                for g4 in range(16):
                    pb = 5 + g4 % 2
                    for q in range(4):
                        g = g4 * 4 + q
                        gh, gl = g // 32, g % 32
                        ps_ = slice(gh * 64, (gh + 1) * 64)
                        for t in range(8):
                            wsl = slice((7 - t) * 16, (7 - t) * 16 + 128)
                            osl = slice(q * 128 + t * 16, q * 128 + (t + 1) * 16)
                            S.op("pe", lambda e: e.matmul(PS[pb][:, osl], ABrb[0][ps_, gl, wsl], CTb[0][ps_, gl, :], start=True, stop=False),
                                 reads=[r_p], writes=[PSR[pb]], sig=False)
                            S.op("pe", lambda e: e.matmul(PS[pb][:, osl], ABrb[1][ps_, gl, wsl], CTb[1][ps_, gl, :], start=False, stop=True),
                                 reads=[r_p], writes=[PSR[pb]], sig=(t == 7 and q == 3))
                    for q in range(4):
                        g = g4 * 4 + q
                        S.op("dve", lambda e: e.scalar_tensor_tensor(out=W1b[:, g, :], in0=ident[:], scalar=Drep[:, g:g + 1],
                                                                     in1=PS[pb][:, q * 128:(q + 1) * 128], op0=ALU.mult, op1=ALU.add),
                             reads=[r_p, PSR[pb]], writes=[r_W])
                for k2, dst in enumerate([W2re, W2im]):
                    for g8 in range(8):
                        pb = (k2 * 8 + g8) % 2
                        for q in range(8):
                            g = g8 * 8 + q
                            gh, gl = g // 32, g % 32
                            ps_ = slice(gh * 64, (gh + 1) * 64)
                            S.op("pe", lambda e: e.transpose(out=PS[pb][:, q * 64:(q + 1) * 64], in_=ABr[k2][ps_, gl, 0:128],
                                                             identity=ident[ps_, ps_]), reads=[r_p], writes=[PSR[pb]], sig=(q == 7))
                        S.op("act", lambda e: e.activation(out=dst[:, g8 * 8:(g8 + 1) * 8, :].rearrange("p a b -> p (a b)"), in_=PS[pb][:],
                                                           func=AF.Copy), reads=[PSR[pb]], writes=[r_W])
                for t in range(8):
                    k = t + 1
                    pr_ = bc(PW, k * 64, 576, 32, 16); pi_ = bc(PW, k * 64 + 32, 576, 32, 16)
                    osl = slice(t * 16, (t + 1) * 16)
                    TT(T1[:], CT[0][:], pr_, ALU.mult); TT(T2[:], CT[1][:], pi_, ALU.mult)
                    D_(lambda e: e.tensor_tensor(out=W4re[:, :, osl], in0=T1[:], in1=T2[:], op=ALU.subtract), wr=[r_W])
                    TT(T1[:], CT[2][:], pr_, ALU.mult); TT(T2[:], CT[0][:], pi_, ALU.mult)
                    D_(lambda e: e.tensor_tensor(out=W4im[:, :, osl], in0=T1[:], in1=T2[:], op=ALU.subtract), wr=[r_W])
                for r2 in range(2):
                    D_(lambda e: e.tensor_copy(out=AR2[:, r2, :], in_=PW[:, 8, 0, :]), wr=[r_W])
                D_(lambda e: e.tensor_copy(out=NAI[:, 1, :], in_=PW[:, 8, 1, :]), wr=[r_W])
                D_(lambda e: e.tensor_scalar(out=NAI[:, 0, :], in0=PW[:, 8, 1, :], scalar1=-1.0, scalar2=None, op0=ALU.mult), wr=[r_W])
                S.barrier()
            SH = sb(st, "ssSH", [128, 2, 32, 2, 257], BF16)
            Hst = [sb(st, f"ssH{b}", [128, 2, 32, 2], F32) for b in range(2)]
            Pt = sb(st, "ssP", [128, 2, 32, 2], F32)
            Qt = sb(st, "ssQ", [128, 2, 32, 2], F32)
            r_P = S.res(); r_Q = S.res(); r_SHs = S.res()
            with ExitStack() as st3:
                ub = [sb(st3, f"ssub{i}", [128, 512], BF16) for i in range(4)]
                r_ub = [S.res() for _ in range(4)]; s_ub = [S.dsem() for _ in range(4)]
                for b in range(2):
                    S.op("dve", lambda e: e.memset(SH[:, :, :, b, 0:1], 0.0), writes=[r_SH[b]])
                    S.op("dve", lambda e: e.memset(Hst[b][:], 0.0), writes=[r_H[b]])
                uc = 0
                for gl in range(32):
                    a = 2 * (gl % 2)
                    for gh in range(2):
                        g = gh * 32 + gl
                        ui = uc % 4; uc += 1
                        S.dma("sp", ub[ui][:], U2.ap()[g].rearrange("s i n -> (s i) n"), s_ub[ui], writes=[r_ub[ui]])
                        ps_ = slice(gh * 64, (gh + 1) * 64)
                        S.op("pe", lambda e: e.matmul(PS[a][ps_, :], W2re[:, g, :], ub[ui][:], start=True, stop=True),
                             reads=[r_W, r_ub[ui]], writes=[PSR[a]])
                        S.op("pe", lambda e: e.matmul(PS[a + 1][ps_, :], W2im[:, g, :], ub[ui][:], start=True, stop=True),
                             reads=[r_W, r_ub[ui]], writes=[PSR[a + 1]])
                    S.op("act", lambda e: e.activation(out=SH[:, 0, gl, :, 1:257], in_=PS[a][:].rearrange("p (b c) -> p b c", b=2), func=AF.Copy),
                         reads=[PSR[a]], writes=r_SH)
                    S.op("dve", lambda e: e.tensor_copy(out=SH[:, 1, gl, :, 1:257], in_=PS[a + 1][:].rearrange("p (b c) -> p b c", b=2)),
                         reads=[PSR[a + 1]], writes=r_SH)
                S.barrier()
            state = {"c": 0}

            ARb = bass.AP(AR2.tensor if hasattr(AR2, "tensor") else AR2, 0, [[64, 128], [1, 64], [0, 2]])
            NAb = [bass.AP(NAI.tensor if hasattr(NAI, "tensor") else NAI, r2 * 32, [[64, 128], [1, 32], [0, 2]]) for r2 in range(2)]

            def tick():
                c = state["c"]
                if c >= 256:
                    return
                state["c"] = c + 1
                Hc, Hn = Hst[c % 2], Hst[(c + 1) % 2]
                rHc, rHn = r_H[c % 2], r_H[(c + 1) % 2]
                S.op("pool", lambda e: e.tensor_tensor(out=Pt[:].rearrange("p r g b -> p (r g) b"), in0=ARb,
                                                      in1=Hc[:].rearrange("p r g b -> p (r g) b"), op=ALU.mult),
                     reads=[rHc, r_W], writes=[r_P])
                S.op("pool", lambda e: e.tensor_tensor(out=Qt[:, 0, :, :], in0=NAb[0], in1=Hc[:, 1, :, :], op=ALU.mult),
                     reads=[rHc, r_W], writes=[r_Q])
                S.op("pool", lambda e: e.tensor_tensor(out=Qt[:, 1, :, :], in0=NAb[1], in1=Hc[:, 0, :, :], op=ALU.mult),
                     reads=[rHc, r_W], writes=[r_Q])
                S.op("pool", lambda e: e.tensor_tensor(out=Pt[:], in0=Pt[:], in1=Qt[:], op=ALU.add), reads=[r_P, r_Q], writes=[r_P])
                S.op("pool", lambda e: e.tensor_tensor(out=Hn[:], in0=Pt[:], in1=SH[:, :, :, :, c + 1], op=ALU.add),
                     reads=[r_P] + r_SH, writes=[rHn])
                S.op("pool", lambda e: e.tensor_copy(out=SH[:, :, :, :, c + 1], in_=Hn[:]), reads=[rHn], writes=r_SH)

            attn_fn(l, tick)
            while state["c"] < 256:
                tick()
            with ExitStack() as st4:
                ub = [sb(st4, f"ssub{i}", [128, 512], BF16) for i in range(4)]
                r_ub = [S.res() for _ in range(4)]; s_ub = [S.dsem() for _ in range(4)]
                yo = [sb(st4, f"ssyo{i}", [128, 512], BF16) for i in range(2)]
                r_yo = [S.res() for _ in range(2)]; s_yo = [S.dsem() for _ in range(2)]
                yc = 0; uc = 0
                for g in range(64):
                    gh, gl = g // 32, g % 32
                    ps_ = slice(gh * 64, (gh + 1) * 64)
                    ui = uc % 4; uc += 1
                    S.dma("sp", ub[ui][:], U2.ap()[g].rearrange("s i n -> (s i) n"), s_ub[ui], writes=[r_ub[ui]])
                    pb = 4 + g % 4
                    S.op("pe", lambda e: e.matmul(PS[pb][:], W1b[:, g, :], ub[ui][:], start=True, stop=False),
                         reads=[r_W, r_ub[ui]], writes=[PSR[pb]], sig=False)
                    S.op("pe", lambda e: e.matmul(PS[pb][:], W4re[ps_, gl, :], SH[ps_, 0, gl, :, 0:256], start=False, stop=False),
                         reads=[r_W] + r_SH, writes=[PSR[pb]], sig=False)
                    S.op("pe", lambda e: e.matmul(PS[pb][:], W4im[ps_, gl, :], SH[ps_, 1, gl, :, 0:256], start=False, stop=True),
                         reads=[r_W] + r_SH, writes=[PSR[pb]])
                    yi = yc % 2; yc += 1
                    if g % 2 == 0:
                        S.op("act", lambda e: e.activation(out=yo[yi][:], in_=PS[pb][:], func=AF.Copy), reads=[PSR[pb]], writes=[r_yo[yi]])
                    else:
                        S.op("dve", lambda e: e.tensor_copy(out=yo[yi][:], in_=PS[pb][:]), reads=[PSR[pb]], writes=[r_yo[yi]])
                    S.dma("act", Y2.ap()[g].rearrange("t o n -> (t o) n"), yo[yi][:], s_yo[yi], reads=[r_yo[yi]])
                S.barrier()

    class WStream:
        def __init__(self, st, tag, nkmax, items, nbf=2, dist=1):
            self.items = items
            self.nbf = nbf
            self.dist = dist
            self.wst = [sb(st, f"{tag}ws{i}", [128, nkmax, 128], F32) for i in range(2)]
            self.wbf = [sb(st, f"{tag}wb{i}", [128, nkmax, 128], BF16) for i in range(nbf)]
            self.r_ws = [S.res() for _ in range(2)]; self.r_wb = [S.res() for _ in range(nbf)]
            self.s_ws = [S.dsem() for i in range(2)]
            self.issued = 0
            self.idx = 0

        def _issue(self, k):
            view, nk = self.items[k]
            i = k % 2
            j = k % self.nbf
            S.dma("sp", self.wst[i][:, 0:nk, :], view, self.s_ws[i], writes=[self.r_ws[i]])
            S.op("pool", lambda e: e.tensor_copy(out=self.wbf[j][:, 0:nk, :], in_=self.wst[i][:, 0:nk, :]),
                 reads=[self.r_ws[i]], writes=[self.r_wb[j]])

        def next(self):
            while self.issued < min(len(self.items), self.idx + self.dist + 1):
                self._issue(self.issued)
                self.issued += 1
            j = self.idx % self.nbf
            self.idx += 1
            return self.wbf[j], self.r_wb[j]

    def mm_acc(pb, wb, r_wb, X, r_X, nk, tsl, first=True, last=True):
        for c in range(nk):
            S.op("pe", lambda e: e.matmul(PS[pb][:], wb[:, c, :], X[:, c, tsl], start=(first and c == 0), stop=(last and c == nk - 1)),
                 reads=[r_wb, r_X], writes=[PSR[pb]], sig=(c == nk - 1))

    def phase_mix(l):
        with ExitStack() as st:
            zT = sb(st, "mxz", [128, 8, NT], BF16)
            r_z = S.res()
            glb = sb(st, "mxgb", [128, 8], F32); r_gb = S.res(); s_gb = S.dsem("mxsgb")
            S.dma("sp", glb[:], glu_b.ap()[l].rearrange("(c p) -> p c", p=128), s_gb, writes=[r_gb], allow_slow_non_contiguous=True)
            with ExitStack() as st2:
                yT = sb(st2, "mxy", [128, 8, NT], BF16); r_y = S.res()
                yl = [sb(st2, f"mxyl{i}", [128, 8, 512], BF16) for i in range(2)]
                r_yl = [S.res() for _ in range(2)]; s_yl = [S.dsem(f"mxsyl{i}") for i in range(2)]
                sgt = [sb(st2, f"mxsg{i}", [128, 512], BF16) for i in range(2)]; r_sg = [S.res() for _ in range(2)]
                wgv = glu_w.ap()[l].rearrange("(c p) n -> p c n", p=128)
                ws = WStream(st2, "mxa", 8, [(wgv[:, :, m * 128:(m + 1) * 128], 8) for m in range(8)])
                for mu in range(8):
                    i = mu % 2
                    for gl in range(8):
                        S.dma("sp", yl[i][gl * 16:(gl + 1) * 16, :, :], Y2.ap()[mu * 8 + gl].rearrange("t o n -> o t n"),
                              s_yl[i], writes=[r_yl[i]])
                    S.op("act", lambda e: e.activation(out=yT[:, mu, :].rearrange("p (n t) -> p t n", t=8), in_=yl[i][:],
                                                       func=AF.Gelu_apprx_tanh), reads=[r_yl[i]], writes=[r_y])
                cnt = 0
                for m in range(8):
                    wb, r_wb = ws.next()
                    for j in range(8):
                        pb = cnt % 4; k = cnt % 2; cnt += 1
                        tsl = slice(j * 512, (j + 1) * 512)
                        mm_acc(pb, wb, r_wb, yT, r_y, 8, tsl)
                        S.op("act", lambda e: e.activation(out=sgt[k][:], in_=PS[pb][:], func=AF.Sigmoid, bias=glb[:, m:m + 1], scale=1.0),
                             reads=[PSR[pb], r_gb], writes=[r_sg[k]])
                        S.op("dve", lambda e: e.tensor_tensor(out=zT[:, m, tsl], in0=yT[:, m, tsl], in1=sgt[k][:], op=ALU.mult),
                             reads=[r_sg[k], r_y], writes=[r_z])
                S.barrier()
            with ExitStack() as st2:
                oT = sb(st2, "mxo", [128, 8, NT], BF16); r_o = S.res(); s_o = S.dsem("mxso")
                tmp = sb(st2, "mxt", [128, NT], F32); r_t = S.res()
                gt = [sb(st2, f"mxg{i}", [128, 512], BF16) for i in range(4)]; r_g = [S.res() for _ in range(4)]
                s_g = [S.dsem(f"mxsg{i}") for i in range(4)]
                mo = [sb(st2, f"mxmo{i}", [128, 512], BF16) for i in range(2)]; r_mo = [S.res() for _ in range(2)]
                s_mo = [S.dsem(f"mxsmo{i}") for i in range(2)]
                t2 = [sb(st2, f"mxt2{i}", [128, 512], F32) for i in range(2)]; r_t2 = [S.res() for _ in range(2)]
                wav = w_ba.ap()[l].rearrange("(c p) n -> p c n", p=128)
                wsv = w_bs.ap()[l].rearrange("(c p) n -> p c n", p=128)
                its = []
                for m in range(16):
                    its.append((wav[:, :, m * 128:(m + 1) * 128], 8))
                    its.append((wsv[:, :, m * 128:(m + 1) * 128], 8))
                ws = WStream(st2, "mxb", 8, its)
                OTv = OT.ap().rearrange("c p n -> p c n")
                for j in range(8):
                    S.dma("sp", oT[:, :, j * 512:(j + 1) * 512], OTv[:, :, j * 512:(j + 1) * 512], s_o, writes=[r_o])
                cnt = 0; gc = 0
                for m in range(16):
                    wb, r_wb = ws.next()
                    for j in range(8):
                        pb = cnt % 4; cnt += 1
                        tsl = slice(j * 512, (j + 1) * 512)
                        gi = gc % 4; gc += 1
                        S.dma("sp", gt[gi][:], GA.ap()[m][:, tsl], s_g[gi], writes=[r_g[gi]])
                        mm_acc(pb, wb, r_wb, oT, r_o, 8, tsl)
                        S.op("dve", lambda e: e.tensor_tensor(out=tmp[:, tsl], in0=PS[pb][:], in1=gt[gi][:], op=ALU.mult),
                             reads=[PSR[pb], r_g[gi]], writes=[r_t])
                    wb, r_wb = ws.next()
                    for j in range(8):
                        pb = cnt % 4; k = cnt % 2; cnt += 1
                        tsl = slice(j * 512, (j + 1) * 512)
                        gi = gc % 4; gc += 1
                        S.dma("sp", gt[gi][:], GS.ap()[m][:, tsl], s_g[gi], writes=[r_g[gi]])
                        mm_acc(pb, wb, r_wb, zT, r_z, 8, tsl)
                        S.op("dve", lambda e: e.tensor_tensor(out=t2[k][:], in0=PS[pb][:], in1=gt[gi][:], op=ALU.mult),
                             reads=[PSR[pb], r_g[gi]], writes=[r_t2[k]])
                        S.op("pool", lambda e: e.tensor_tensor(out=mo[k][:], in0=t2[k][:], in1=tmp[:, tsl], op=ALU.add),
                             reads=[r_t2[k], r_t], writes=[r_mo[k]])
                        S.dma("act", H.ap()[m][:, tsl], mo[k][:], s_mo[k], reads=[r_mo[k]])
                S.barrier()
        with ExitStack() as st:
            Ms = sb(st, "mxM", [128, 16, NT], BF16); r_M = S.res(); s_M = S.dsem("mxsM")
            xt = [sb(st, f"mxx{i}", [128, 512], F32) for i in range(4)]; r_x = [S.res() for _ in range(4)]
            s_x = [S.dsem(f"mxsx{i}") for i in range(4)]
            wov = w_out.ap()[l].rearrange("(c p) n -> p c n", p=128)
            ws = WStream(st, "mxc", 16, [(wov[:, :, m * 128:(m + 1) * 128], 16) for m in range(16)])
            Hv = H.ap().rearrange("c p n -> p c n")
            for j in range(8):
                S.dma("sp", Ms[:, :, j * 512:(j + 1) * 512], Hv[:, :, j * 512:(j + 1) * 512], s_M, writes=[r_M])
            cnt = 0
            for m in range(16):
                wb, r_wb = ws.next()
                for j in range(8):
                    pb = cnt % 4; xi = cnt % 4; cnt += 1
                    tsl = slice(j * 512, (j + 1) * 512)
                    S.dma("sp", xt[xi][:], XT.ap()[m][:, tsl], s_x[xi], writes=[r_x[xi]])
                    mm_acc(pb, wb, r_wb, Ms, r_M, 16, tsl)
                    S.op("dve", lambda e: e.tensor_tensor(out=xt[xi][:], in0=PS[pb][:], in1=xt[xi][:], op=ALU.add),
                         reads=[PSR[pb], r_x[xi]], writes=[r_x[xi]])
                    S.dma("act", XT.ap()[m][:, tsl], xt[xi][:], s_x[xi], reads=[r_x[xi]])
            S.barrier()

    def phase_ffn_up(l):
        with ExitStack() as st:
            Hs = sb(st, "fuH", [128, 16, NT], BF16); r_H = S.res()
            phase_norm(norm2_g, l, f"n2{l}", Hs, r_H)
            cw = sb(st, "fucw", [128, 3, 86], F32); cb = sb(st, "fucb", [128, 86], F32); r_cw = S.res(); s_cw = S.dsem("fuscw")
            up = [[sb(st, f"fuu{a}{i}", [128, 514], F32) for i in range(2)] for a in range(2)]
            r_up = [[S.res() for i in range(2)] for a in range(2)]
            cc = [[sb(st, f"fuc{a}{i}", [128, 512], F32) for i in range(2)] for a in range(2)]
            r_cc = [[S.res() for i in range(2)] for a in range(2)]
            ga = [sb(st, f"fug{i}", [128, 512], F32) for i in range(2)]; r_ga = [S.res() for _ in range(2)]
            ao = [sb(st, f"fuo{i}", [128, 512], BF16) for i in range(2)]; r_ao = [S.res() for _ in range(2)]
            s_ao = [S.dsem(f"fusao{i}") for i in range(2)]
            wuv = w_up.ap()[l].rearrange("(c p) n -> p c n", p=128)
            its = []
            for m in range(NFT):
                its.append((wuv[:, :, m * 128:(m + 1) * 128], 16))
                its.append((wuv[:, :, (NFT + m) * 128:(NFT + m + 1) * 128], 16))
            ws = WStream(st, "fu", 16, its, nbf=4, dist=2)
            for k3 in range(3):
                S.dma("sp", cw[:, k3, :], conv_w.ap()[l][k3].rearrange("(m p) -> p m", p=128), s_cw, writes=[r_cw],
                      allow_slow_non_contiguous=True)
            S.dma("sp", cb[:], conv_b.ap()[l].rearrange("(m p) -> p m", p=128), s_cw, writes=[r_cw], allow_slow_non_contiguous=True)
            cnt = 0
            for m in range(NFT):
                wba, r_wba = ws.next()
                wbv, r_wbv = ws.next()
                for j in range(8):
                    k = cnt % 2; k4 = cnt % 4; cnt += 1
                    tsl = slice(j * 512, (j + 1) * 512)
                    pbs = (2 * k4, 2 * k4 + 1)
                    mm_acc(pbs[0], wba, r_wba, Hs, r_H, 16, tsl)
                    mm_acc(pbs[1], wbv, r_wbv, Hs, r_H, 16, tsl)
                    for a in range(2):
                        fm = m + a * NFT
                        pb = pbs[a]
                        u_, ru_ = up[a][k], r_up[a][k]
                        if j % 4 == 0:
                            S.op("pool", lambda e: e.memset(u_[:, 0:2], 0.0), writes=[ru_])
                        else:
                            S.op("pool", lambda e: e.tensor_copy(out=u_[:, 0:2], in_=up[a][1 - k][:, 512:514]),
                                 reads=[r_up[a][1 - k]], writes=[ru_])
                        S.op("act", lambda e: e.activation(out=u_[:, 2:514], in_=PS[pb][:], func=AF.Copy), reads=[PSR[pb]], writes=[ru_])
                        S.op("act", lambda e: e.activation(out=cc[a][k][:], in_=PS[pb][:], func=AF.Identity, bias=cb[:, fm:fm + 1],
                                                           scale=cw[:, 2, fm:fm + 1]), reads=[PSR[pb], r_cw], writes=[r_cc[a][k]])
                        S.op("dve", lambda e: e.scalar_tensor_tensor(out=cc[a][k][:], in0=u_[:, 1:513], scalar=cw[:, 1, fm:fm + 1],
                                                                     in1=cc[a][k][:], op0=ALU.mult, op1=ALU.add),
                             reads=[ru_, r_cw, r_cc[a][k]], writes=[r_cc[a][k]])
                        S.op("dve", lambda e: e.scalar_tensor_tensor(out=cc[a][k][:], in0=u_[:, 0:512], scalar=cw[:, 0, fm:fm + 1],
                                                                     in1=cc[a][k][:], op0=ALU.mult, op1=ALU.add),
                             reads=[ru_, r_cw, r_cc[a][k]], writes=[r_cc[a][k]])
                    S.op("act", lambda e: e.activation(out=ga[k][:], in_=cc[0][k][:], func=AF.Gelu_apprx_tanh),
                         reads=[r_cc[0][k]], writes=[r_ga[k]])
                    S.op("dve", lambda e: e.tensor_tensor(out=ao[k][:], in0=ga[k][:], in1=cc[1][k][:], op=ALU.mult),
                         reads=[r_ga[k], r_cc[1][k]], writes=[r_ao[k]])
                    S.dma("act", AT.ap()[m][:, tsl], ao[k][:], s_ao[k], reads=[r_ao[k]])
            S.barrier()

    def phase_ffn_down(l):
        for (k0, k1) in KGROUPS:
            nk = k1 - k0
            with ExitStack() as st:
                As = sb(st, "fdA", [128, 15, NT], BF16); r_A = S.res(); s_A = S.dsem(f"fdsA{k0}")
                xt = [sb(st, f"fdx{i}", [128, 512], F32) for i in range(4)]; r_x = [S.res() for _ in range(4)]
                s_x = [S.dsem(f"fdsx{k0}_{i}") for i in range(4)]
                wdv = w_down.ap()[l].rearrange("(c p) n -> p c n", p=128)
                ws = WStream(st, f"fd{k0}", 15, [(wdv[:, k0:k1, m * 128:(m + 1) * 128], nk) for m in range(16)])
                ATv = AT.ap().rearrange("c p n -> p c n")
                for j in range(8):
                    S.dma("sp", As[:, 0:nk, j * 512:(j + 1) * 512], ATv[:, k0:k1, j * 512:(j + 1) * 512], s_A, writes=[r_A])
                cnt = 0
                for m in range(16):
                    wb, r_wb = ws.next()
                    for j in range(8):
                        pb = cnt % 4; xi = cnt % 4; cnt += 1
                        tsl = slice(j * 512, (j + 1) * 512)
                        S.dma("sp", xt[xi][:], XT.ap()[m][:, tsl], s_x[xi], writes=[r_x[xi]])
                        mm_acc(pb, wb, r_wb, As, r_A, nk, tsl)
                        S.op("dve", lambda e: e.tensor_tensor(out=xt[xi][:], in0=PS[pb][:], in1=xt[xi][:], op=ALU.add),
                             reads=[PSR[pb], r_x[xi]], writes=[r_x[xi]])
                        S.dma("act", XT.ap()[m][:, tsl], xt[xi][:], s_x[xi], reads=[r_x[xi]])
                S.barrier()

    def phase_transpose_out():
        with ExitStack() as st:
            xs = [sb(st, f"pox{i}", [128, 16, 512], F32) for i in range(2)]
            yo = [sb(st, f"poy{i}", [128, 4, D], F32) for i in range(2)]
            r_x = [S.res() for _ in range(2)]; r_y = [S.res() for _ in range(2)]
            s_x = [S.dsem(f"posx{i}") for i in range(2)]; s_y = [S.dsem(f"posy{i}") for i in range(2)]
            XTv = XT.ap().rearrange("c p n -> p c n")
            yv = y_out.ap().rearrange("(n s p) d -> n p s d", s=4, p=128)
            cnt = 0
            for j in range(8):
                b = j % 2
                S.dma("sp", xs[b][:], XTv[:, :, j * 512:(j + 1) * 512], s_x[b], writes=[r_x[b]])
                for s4 in range(4):
                    for c4 in range(4):
                        pb = cnt % 8; cnt += 1
                        for cc_ in range(4):
                            c = c4 * 4 + cc_
                            S.op("pe", lambda e: e.transpose(out=PS[pb][:, cc_ * 128:(cc_ + 1) * 128],
                                                             in_=xs[b][:, c, s4 * 128:(s4 + 1) * 128], identity=ident[:]),
                                 reads=[r_x[b]], writes=[PSR[pb]], sig=(cc_ == 3))
                        if cnt % 2 == 0:
                            S.op("act", lambda e: e.activation(out=yo[b][:, s4, c4 * 512:(c4 + 1) * 512], in_=PS[pb][:], func=AF.Copy),
                                 reads=[PSR[pb]], writes=[r_y[b]])
                        else:
                            S.op("dve", lambda e: e.tensor_copy(out=yo[b][:, s4, c4 * 512:(c4 + 1) * 512], in_=PS[pb][:]),
                                 reads=[PSR[pb]], writes=[r_y[b]])
                S.dma("act", yv[j], yo[b][:], s_y[b], reads=[r_y[b]])
            S.barrier()

    phase_transpose_in()
    for l in range(nlayers):
        S.rotate()
        if "ip" in PH: phase_inproj(l)
        if "ss" in PH: phase_ssm(l, phase_attn)
        if "mx" in PH: phase_mix(l)
        if "fu" in PH: phase_ffn_up(l)
        if "fd" in PH: phase_ffn_down(l)
    phase_transpose_out()
    stack.close()
    return nc


def consts():
    ident = np.eye(128, dtype=np.float32)
    p = np.arange(128)[:, None].astype(np.float64)
    f = np.arange(512)[None, :].astype(np.float64)
    al = np.zeros((128, 5, 512), np.float32)
    al[:, 0, :] = (f - p)
    for m in range(4):
        d = f - p - 128.0 * m
        vis = (p + 128 * m) < (np.floor(f / 64) + 1) * 64
        al[:, 1 + m, :] = np.where(vis, np.abs(d), 1.0e9)
    blk = np.zeros((128, 128), np.float32)
    blk[:64, :64] = 1.0
    blk[64:, 64:] = 1.0
    pf = np.zeros((128, 513), np.float32)
    pf[:, 0] = np.arange(128)
    pf[:, 1:] = np.arange(512)[None, :]
    import ml_dtypes
    t = np.arange(SEQ) % 512
    aug = np.zeros((8, 2, SEQ), np.float32)
    for h in range(8):
        sl = 2.0 ** -(h + 1)
        aug[h, 0] = -8.0 * sl * 16.0 * (t // 16)
        aug[h, 1] = -8.0 * sl * (t % 16)
    return {"c_ident": ident, "c_alibi": al, "c_blk": blk, "c_pf": pf, "c_aug": aug.astype(ml_dtypes.bfloat16)}


PARAM_NAMES = ["norm1_g", "w_in", "q_norm_g", "k_norm_g", "lambda_q1", "lambda_k1", "lambda_q2", "lambda_k2",
               "subln_g", "ssm_a_re", "ssm_a_im", "ssm_log_dt", "ssm_b_re", "ssm_b_im", "ssm_c_re", "ssm_c_im",
               "ssm_d", "ssm_glu_w", "ssm_glu_b", "w_branch_attn", "w_branch_ssm", "w_out", "norm2_g",
               "ffn_w_up", "ffn_conv_w", "ffn_conv_b", "ffn_w_down"]


def make_in_maps(inputs):
    x = np.ascontiguousarray(np.asarray(inputs["x"], dtype=np.float32))
    shared = {k: np.ascontiguousarray(np.asarray(inputs[k], dtype=np.float32)) for k in PARAM_NAMES}
    shared.update(consts())
    maps = []
    for c in range(NCORES):
        m = dict(shared)
        m["x"] = x[2 * c:2 * c + 2].reshape(NT, D)
        maps.append(m)
    return maps


def kernel(**inputs):
    nc = build_program()
    res = run_bass_kernel_spmd(nc, make_in_maps(inputs), core_ids=list(range(NCORES)))
    out = np.stack([r["y"].reshape(2, SEQ, D) for r in res.results], axis=0).reshape(16, SEQ, D)
    return out.astype(np.float32)
```

```python
import math
from contextlib import ExitStack
import numpy as np
import concourse.bass as bass
import concourse.mybir as mybir
from concourse.bass_utils import run_bass_kernel_spmd

F32 = mybir.dt.float32
BF16 = mybir.dt.bfloat16
AF = mybir.ActivationFunctionType
ALU = mybir.AluOpType
AX = mybir.AxisListType

NCORES = 8
D = 2048
NT = 4096
SEQ = 2048
DEPTH = 4
DFF = 5504
EPS = 1e-6
NFT = DFF // 128
KGROUPS = [(0, 15), (15, 29), (29, 43)]


class Res:
    __slots__ = ("name", "w", "rs")

    def __init__(self, name):
        self.name = name
        self.w = None
        self.rs = []


class Sched:
    def __init__(self, nc, stack):
        self.nc = nc
        self.stack = stack
        self.eng = {"pe": nc.tensor, "act": nc.scalar, "dve": nc.vector, "pool": nc.gpsimd, "sp": nc.sync}
        self.sem = {}
        self.cnt = {}
        self.waited = {e: {} for e in self.eng}
        self.pending = {e: [] for e in self.eng}
        self.dma_sems = []
        self.free_dsems = []
        self.ndsem = 0
        self.cur = {}
        self.gen = 0
        self.rotate()
        self.nres = 0

    def rotate(self):
        self.gen += 1
        for e in ("pe", "act", "dve", "pool"):
            n = f"c_{e}_{self.gen}"
            self._newsem(n)
            self.cur[e] = n

    def _newsem(self, name):
        self.sem[name] = self.stack.enter_context(self.nc.semaphore(name))
        self.cnt[name] = 0

    def res(self, name=None):
        self.nres += 1
        return Res(name or f"r{self.nres}")

    def dsem(self, name=None):
        if self.free_dsems:
            n = self.free_dsems.pop()
        else:
            self.ndsem += 1
            n = f"d{self.ndsem}"
            self._newsem(n)
        self.dma_sems.append(n)
        return n

    def _wait(self, e, ev):
        if ev is None:
            return
        s, v = ev
        if e == "pe" and s.startswith("c_pe"):
            return
        if self.waited[e].get(s, 0) >= v:
            return
        self.eng[e].wait_ge(self.sem[s], v)
        self.waited[e][s] = v

    def _deps(self, e, reads, writes):
        for r in reads:
            self._wait(e, r.w)
        for w in writes:
            self._wait(e, w.w)
            for ev in w.rs:
                self._wait(e, ev)

    def op(self, e, fn, reads=(), writes=(), sig=True):
        self._deps(e, reads, writes)
        ins = fn(self.eng[e])
        if not sig:
            self.pending[e].append((tuple(reads), tuple(writes)))
            return None
        s = self.cur[e]
        self.cnt[s] += 1
        ins.then_inc(self.sem[s], 1)
        ev = (s, self.cnt[s])
        for (prs, pws) in self.pending[e]:
            for r in prs:
                r.rs.append(ev)
            for w in pws:
                w.w = ev
                w.rs = []
        self.pending[e] = []
        for r in reads:
            r.rs.append(ev)
        for w in writes:
            w.w = ev
            w.rs = []
        return ev

    def dma(self, q, out, in_, sem, reads=(), writes=(), **kw):
        self._deps(q, reads, writes)
        ins = self.eng[q].dma_start(out=out, in_=in_, **kw)
        self.cnt[sem] += 16
        ins.then_inc(self.sem[sem], 16)
        ev = (sem, self.cnt[sem])
        for r in reads:
            r.rs.append(ev)
        for w in writes:
            w.w = ev
            w.rs = []
        return ev

    def barrier(self):
        for s in self.dma_sems:
            if self.cnt[s] > self.waited["sp"].get(s, 0):
                self.eng["sp"].wait_ge(self.sem[s], self.cnt[s])
        self.nc.all_engine_barrier()
        for e in self.eng:
            for s in self.cnt:
                self.waited[e][s] = self.cnt[s]
        self.free_dsems.extend(self.dma_sems)
        self.dma_sems = []


def dap(t, off, pat):
    return bass.AP(t, off, [list(p) for p in pat])


def build_program(nlayers=DEPTH, dbg=None, PH=("n1", "ip", "at", "ss", "mx", "n2", "fu", "fd")):
    nc = bass.Bass("TRN2", target_bir_lowering=False)
    stack = ExitStack()
    S = Sched(nc, stack)
    L = DEPTH

    def din(name, shape, dt=F32):
        return nc.dram_tensor(name, list(shape), dt, kind="ExternalInput")

    def dscr(name, shape, dt, out=False):
        return nc.dram_tensor(name, list(shape), dt, kind=("ExternalOutput" if out else "Internal"))

    x_in = din("x", [NT, D])
    y_out = nc.dram_tensor("y", [NT, D], F32, kind="ExternalOutput")
    norm1_g = din("norm1_g", [L, D]); norm2_g = din("norm2_g", [L, D])
    w_in = din("w_in", [L, D, 8192])
    q_norm_g = din("q_norm_g", [L, 64]); k_norm_g = din("k_norm_g", [L, 64])
    lam_q1 = din("lambda_q1", [L, 64]); lam_k1 = din("lambda_k1", [L, 64])
    lam_q2 = din("lambda_q2", [L, 64]); lam_k2 = din("lambda_k2", [L, 64])
    subln_g = din("subln_g", [L, 128])
    a_re = din("ssm_a_re", [L, 64, 64]); a_im = din("ssm_a_im", [L, 64, 64])
    log_dt = din("ssm_log_dt", [L, 64])
    b_re = din("ssm_b_re", [L, 64, 64, 16]); b_im = din("ssm_b_im", [L, 64, 64, 16])
    c_re = din("ssm_c_re", [L, 64, 16, 64]); c_im = din("ssm_c_im", [L, 64, 16, 64])
    ssm_d = din("ssm_d", [L, 1024])
    glu_w = din("ssm_glu_w", [L, 1024, 1024]); glu_b = din("ssm_glu_b", [L, 1024])
    w_ba = din("w_branch_attn", [L, 1024, D]); w_bs = din("w_branch_ssm", [L, 1024, D])
    w_out = din("w_out", [L, D, D])
    w_up = din("ffn_w_up", [L, D, 2 * DFF]); conv_w = din("ffn_conv_w", [L, 3, 2 * DFF])
    conv_b = din("ffn_conv_b", [L, 2 * DFF]); w_down = din("ffn_w_down", [L, DFF, D])
    c_ident = din("c_ident", [128, 128])
    c_alibi = din("c_alibi", [128, 5, 512])
    c_blk = din("c_blk", [128, 128])
    c_pf = din("c_pf", [128, 513])
    c_aug = din("c_aug", [8, 2, SEQ], BF16)

    isdbg = lambda n: dbg is not None and n in dbg
    XT = dscr("XT", [16, 128, NT], F32, isdbg("XT"))
    H = dscr("H", [16, 128, NT], BF16, isdbg("H"))
    QT = dscr("QT", [8, 128, NT], BF16, isdbg("QT"))
    KT = dscr("KT", [8, 128, NT], BF16, isdbg("KT"))
    V = dscr("V", [NT, 1024], BF16, isdbg("V"))
    U2 = dscr("U2", [64, 8, 16, 512], BF16, isdbg("U2"))
    GA = dscr("GA", [16, 128, NT], BF16, isdbg("GA"))
    GS = dscr("GS", [16, 128, NT], BF16, isdbg("GS"))
    OT = dscr("OT", [8, 128, NT], BF16, isdbg("OT"))
    Y2 = dscr("Y2", [64, 8, 16, 512], BF16, isdbg("Y2"))
    AT = dscr("AT", [NFT, 128, NT], BF16, isdbg("AT"))

    PS = [nc.alloc_psum_tensor(f"ps{i}", [128, 512], F32) for i in range(8)]
    PSR = [S.res(f"ps{i}") for i in range(8)]

    _uid = [0]

    def sb(st, name, shape, dt):
        _uid[0] += 1
        return st.enter_context(nc.sbuf_tensor(f"{name}_{_uid[0]}", list(shape), dt))

    ident = sb(stack, "ident", [128, 128], F32)
    identb = sb(stack, "identb", [128, 128], BF16)
    onesb = sb(stack, "onesb", [128, 128], BF16)
    blkb = sb(stack, "blkb", [128, 128], BF16)
    blkf = sb(stack, "blkf", [128, 128], F32)
    epsc = sb(stack, "epsc", [128, 1], F32)
    r_const = S.res("const")
    s_const = S.dsem("d_const")
    S.dma("sp", ident[:], c_ident.ap(), s_const, writes=[r_const])
    S.dma("sp", blkf[:], c_blk.ap(), s_const, writes=[r_const])
    S.op("dve", lambda e: e.tensor_copy(out=identb[:], in_=ident[:]), reads=[r_const], writes=[r_const])
    S.op("dve", lambda e: e.tensor_copy(out=blkb[:], in_=blkf[:]), reads=[r_const], writes=[r_const])
    S.op("dve", lambda e: e.memset(onesb[:], 1.0), writes=[r_const])
    S.op("dve", lambda e: e.memset(epsc[:], EPS), writes=[r_const])
    S.barrier()

    def phase_transpose_in():
        with ExitStack() as st:
            xin = [sb(st, f"p0x{i}", [128, 4, D], F32) for i in range(2)]
            xo = [sb(st, f"p0o{i}", [128, 16, 512], F32) for i in range(2)]
            r_in = [S.res() for _ in range(2)]; r_o = [S.res() for _ in range(2)]
            s_in = [S.dsem(f"p0si{i}") for i in range(2)]; s_o = [S.dsem(f"p0so{i}") for i in range(2)]
            xv = x_in.ap().rearrange("(n s p) d -> n p s d", s=4, p=128)
            XTv = XT.ap().rearrange("c p n -> p c n")
            for j in range(8):
                b = j % 2
                S.dma("sp", xin[b][:], xv[j], s_in[b], writes=[r_in[b]])
                for c in range(16):
                    pb = c % 8
                    for s4 in range(4):
                        last = s4 == 3
                        S.op("pe", lambda e, s4=s4, c=c, pb=pb: e.transpose(
                            out=PS[pb][:, s4 * 128:(s4 + 1) * 128], in_=xin[b][:, s4, c * 128:(c + 1) * 128],
                            identity=ident[:]), reads=[r_in[b]], writes=[PSR[pb]], sig=last)
                    ee = "act" if c % 2 == 0 else "dve"
                    if ee == "act":
                        S.op("act", lambda e, c=c, pb=pb: e.activation(out=xo[b][:, c, :], in_=PS[pb][:], func=AF.Copy),
                             reads=[PSR[pb]], writes=[r_o[b]])
                    else:
                        S.op("dve", lambda e, c=c, pb=pb: e.tensor_copy(out=xo[b][:, c, :], in_=PS[pb][:]),
                             reads=[PSR[pb]], writes=[r_o[b]])
                S.dma("act", XTv[:, :, j * 512:(j + 1) * 512], xo[b][:], s_o[b], reads=[r_o[b]])
            S.barrier()

    def phase_norm(gain_dram, l, tag, Hs, r_H):
        with ExitStack() as st:
            g = sb(st, tag + "g", [128, 16], F32)
            xs = [sb(st, f"{tag}x{i}", [128, 16, 256], F32) for i in range(2)]
            sq = sb(st, tag + "sq", [128, 16, 256], BF16)
            rs = [sb(st, f"{tag}rs{i}", [128, 256], F32) for i in range(2)]
            r_g = S.res(); r_x = [S.res() for _ in range(2)]; r_sq = S.res(); r_rs = [S.res() for _ in range(2)]
            s_g = S.dsem(); s_x = [S.dsem() for i in range(2)]
            S.dma("sp", g[:], gain_dram.ap()[l].rearrange("(c p) -> p c", p=128), s_g, writes=[r_g],
                  allow_slow_non_contiguous=True)
            XTv = XT.ap().rearrange("c p n -> p c n")
            for j in range(16):
                b = j % 2
                tsl = slice(j * 256, (j + 1) * 256)
                S.dma("sp", xs[b][:], XTv[:, :, tsl], s_x[b], writes=[r_x[b]])
                S.op("act", lambda e: e.activation(out=sq[:], in_=xs[b][:], func=AF.Square), reads=[r_x[b]], writes=[r_sq])
                pb = j % 8
                for c in range(16):
                    S.op("pe", lambda e, c=c: e.matmul(PS[pb][:, 0:256], onesb[:], sq[:, c, :], start=(c == 0), stop=(c == 15)),
                         reads=[r_sq], writes=[PSR[pb]], sig=(c == 15))
                S.op("act", lambda e: e.activation(out=rs[b][:], in_=PS[pb][:, 0:256], func=AF.Ln, bias=epsc[:], scale=1.0 / D),
                     reads=[PSR[pb]], writes=[r_rs[b]])
                S.op("act", lambda e: e.activation(out=rs[b][:], in_=rs[b][:], func=AF.Exp, scale=-0.5), reads=[r_rs[b]], writes=[r_rs[b]])
                for c in range(16):
                    S.op("dve", lambda e, c=c: e.scalar_tensor_tensor(out=Hs[:, c, tsl], in0=xs[b][:, c, :], scalar=g[:, c:c + 1],
                                                                 in1=rs[b][:], op0=ALU.mult, op1=ALU.mult),
                         reads=[r_x[b], r_rs[b], r_g], writes=[r_H])
            S.barrier()

    def phase_inproj(l):
        with ExitStack() as st:
            Hs = sb(st, "p2H", [128, 16, NT], BF16)
            r_H = S.res()
            phase_norm(norm1_g, l, f"n1{l}", Hs, r_H)
            wv = w_in.ap()[l].rearrange("(c p) n -> p c n", p=128)
            ws = WStream(st, "p2", 16, [(wv[:, :, m * 128:(m + 1) * 128], 16) for m in range(64)])
            qg = sb(st, "p2qg", [128, 2], F32)
            sqb = [sb(st, f"p2sq{i}", [128, 512], BF16) for i in range(2)]
            rsb = [sb(st, f"p2rs{i}", [128, 512], F32) for i in range(2)]
            NO = 4
            ob = [sb(st, f"p2o{i}", [128, 512], BF16) for i in range(NO)]
            vb = [sb(st, f"p2v{i}", [128, 512], BF16) for i in range(2)]
            r_qg = S.res(); r_sq = [S.res() for _ in range(2)]; r_rs = [S.res() for _ in range(2)]
            r_o = [S.res() for _ in range(NO)]; r_v = [S.res() for _ in range(2)]
            s_H = S.dsem("p2sH")
            s_qg = S.dsem("p2sqg"); s_o = [S.dsem(f"p2so{i}") for i in range(NO)]
            for hh in range(2):
                S.dma("sp", qg[hh * 64:(hh + 1) * 64, 0:1], dap(q_norm_g, l * 64, [[1, 64], [1, 1]]), s_qg, writes=[r_qg])
                S.dma("sp", qg[hh * 64:(hh + 1) * 64, 1:2], dap(k_norm_g, l * 64, [[1, 64], [1, 1]]), s_qg, writes=[r_qg])
            Vv = V.ap().rearrange("(n s p) f -> n p s f", s=4, p=128)
            U2v = U2.ap()
            cnt = 0
            oc = 0
            pend = [None]
            for m in range(64):
                wbm, r_wbm = ws.next()
                for j in range(8):
                    pb = cnt % 6
                    cnt += 1
                    for c in range(16):
                        S.op("pe", lambda e, c=c: e.matmul(PS[pb][:], wbm[:, c, :], Hs[:, c, j * 512:(j + 1) * 512],
                                                           start=(c == 0), stop=(c == 15)),
                             reads=[r_wbm, r_H], writes=[PSR[pb]], sig=(c == 15))
                    if pend[0] is not None:
                        pend[0]()
                        pend[0] = None
                    oi = oc % NO
                    oc += 1
                    tsl = slice(j * 512, (j + 1) * 512)
                    if m < 16:
                        kq = 0 if m < 8 else 1
                        hh = m % 8
                        qi = cnt % 2
                        pb2 = 6 + (cnt % 2)
                        S.op("act", lambda e: e.activation(out=sqb[qi][:], in_=PS[pb][:], func=AF.Square),
                             reads=[PSR[pb]], writes=[r_sq[qi]])
                        def post(pb=pb, pb2=pb2, qi=qi, oi=oi, kq=kq, hh=hh, tsl=tsl):
                            S.op("pe", lambda e: e.matmul(PS[pb2][:], blkb[:], sqb[qi][:], start=True, stop=True),
                                 reads=[r_sq[qi]], writes=[PSR[pb2]])
                            S.op("act", lambda e: e.activation(out=rsb[qi][:], in_=PS[pb2][:], func=AF.Ln, bias=epsc[:], scale=1.0 / 64),
                                 reads=[PSR[pb2]], writes=[r_rs[qi]])
                            S.op("act", lambda e: e.activation(out=rsb[qi][:], in_=rsb[qi][:], func=AF.Exp, scale=-0.5),
                                 reads=[r_rs[qi]], writes=[r_rs[qi]])
                            S.op("dve", lambda e: e.scalar_tensor_tensor(out=ob[oi][:], in0=PS[pb][:], scalar=qg[:, kq:kq + 1],
                                                                         in1=rsb[qi][:], op0=ALU.mult, op1=ALU.mult),
                                 reads=[PSR[pb], r_rs[qi], r_qg], writes=[r_o[oi]])
                            dst = (QT if kq == 0 else KT).ap()[hh][:, tsl]
                            S.dma("act", dst, ob[oi][:], s_o[oi], reads=[r_o[oi]])
                        pend[0] = post
                    elif m < 24:
                        hh = m - 16
                        vi = cnt % 2
                        pb2 = 6 + (cnt % 2)
                        S.op("act", lambda e: e.activation(out=vb[vi][:], in_=PS[pb][:], func=AF.Copy),
                             reads=[PSR[pb]], writes=[r_v[vi]])
                        def postv(pb2=pb2, vi=vi, oi=oi, hh=hh, j=j):
                            pv = PS[pb2].bitcast(BF16)
                            for s4 in range(4):
                                S.op("pe", lambda e, s4=s4: e.transpose(out=pv[:, s4 * 128:(s4 + 1) * 128],
                                                                        in_=vb[vi][:, s4 * 128:(s4 + 1) * 128], identity=identb[:]),
                                     reads=[r_v[vi]], writes=[PSR[pb2]], sig=(s4 == 3))
                            S.op("dve", lambda e: e.tensor_copy(out=ob[oi][:], in_=pv[:, 0:512]), reads=[PSR[pb2]], writes=[r_o[oi]])
                            S.dma("act", Vv[j][:, :, hh * 128:(hh + 1) * 128], ob[oi][:].rearrange("p (s f) -> p s f", s=4),
                                  s_o[oi], reads=[r_o[oi]])
                        pend[0] = postv
                    elif m < 32:
                        mu = m - 24
                        S.op("act", lambda e: e.activation(out=ob[oi][:].rearrange("p (s c) -> p c s", s=8),
                                                           in_=PS[pb][:].rearrange("p (c s) -> p c s", s=8), func=AF.Copy),
                             reads=[PSR[pb]], writes=[r_o[oi]])
                        for gl in range(8):
                            gidx = mu * 8 + gl
                            S.dma("act", U2v[gidx].rearrange("s i n -> i s n")[:, :, j * 64:(j + 1) * 64],
                                  ob[oi][gl * 16:(gl + 1) * 16, :].rearrange("p (s c) -> p s c", s=8),
                                  s_o[oi], reads=[r_o[oi]])
                    else:
                        mg = m - 32
                        S.op("act", lambda e: e.activation(out=ob[oi][:], in_=PS[pb][:], func=AF.Sigmoid),
                             reads=[PSR[pb]], writes=[r_o[oi]])
                        dst = (GA if mg < 16 else GS).ap()[mg % 16][:, tsl]
                        S.dma("act", dst, ob[oi][:], s_o[oi], reads=[r_o[oi]])
            if pend[0] is not None:
                pend[0]()
            S.barrier()


    def phase_attn(l, tick=None):
        lam_init = 0.8 - 0.6 * math.exp(-0.3 * l)
        with ExitStack() as st:
            alib = sb(st, "atal", [128, 5, 512], F32)
            lamt = sb(st, "atlam", [128, 4, 64], F32)
            ltmp = sb(st, "atlt", [128, 64], F32)
            lsc = sb(st, "atls", [128, 8], F32)
            sg = sb(st, "atsg", [128, 1], F32)
            btab = sb(st, "atbt", [128, 8, 13], F32)
            btab2 = sb(st, "atbt2", [128, 8, 13], F32)
            pcol = sb(st, "atpc", [128, 8], F32)
            cpf = sb(st, "atcpf", [128, 513], F32)
            qT = [sb(st, f"atq{i}", [128, 2, SEQ], BF16) for i in range(2)]
            kT = [sb(st, f"atk{i}", [128, 2, SEQ], BF16) for i in range(2)]
            vt = [sb(st, f"atv{i}", [128, 16, 128], BF16) for i in range(2)]
            NB = 6
            sbias = [sb(st, f"atsb{i}", [128, 512], F32) for i in range(NB)]
            Eb = [sb(st, f"atE{i}", [128, 512], BF16) for i in range(NB)]
            fr2 = sb(st, "atfr2", [128, 512], F32); r_fr2 = S.res()
            o1s = sb(st, "ato1s", [128, 512], F32); r_o1s = S.res()
            o2s = sb(st, "ato2s", [128, 512], F32); r_o2s = S.res()
            fr = sb(st, "atfr", [128, 512], F32)
            ft1 = sb(st, "atft1", [128, 512], F32)
            ft2 = sb(st, "atft2", [128, 512], F32)
            fo = sb(st, "atfo", [128, 512], F32)
            fsq = sb(st, "atfsq", [128, 512], BF16)
            frs = sb(st, "atfrs", [128, 512], F32)
            fon = [sb(st, f"atfon{i}", [128, 512], BF16) for i in range(2)]
            r_c = S.res(); r_q = [S.res() for _ in range(2)]
            r_sb = [S.res() for _ in range(NB)]; r_E = [S.res() for _ in range(NB)]
            r_fr = S.res(); r_ft1 = S.res(); r_ft2 = S.res(); r_fo = S.res(); r_fsq = S.res(); r_frs = S.res()
            r_fon = [S.res() for _ in range(2)]
            s_c = S.dsem("atsc"); s_q = [S.dsem(f"atsq{i}") for i in range(2)]
            s_on = [S.dsem(f"atson{i}") for i in range(2)]
            S.dma("sp", alib[:], c_alibi.ap(), s_c, writes=[r_c])
            for idx, t in enumerate([lam_q1, lam_k1, lam_q2, lam_k2]):
                S.dma("sp", lamt[:, idx, :], dap(t, l * 64, [[0, 128], [1, 64]]), s_c, writes=[r_c])
            S.dma("sp", sg[:], dap(subln_g, l * 128, [[1, 128], [1, 1]]), s_c, writes=[r_c])
            for k2 in range(2):
                S.op("dve", lambda e: e.tensor_tensor(out=ltmp[:], in0=lamt[:, 2 * k2, :], in1=lamt[:, 2 * k2 + 1, :], op=ALU.mult),
                     reads=[r_c], writes=[r_c])
                S.op("dve", lambda e: e.tensor_reduce(out=lsc[:, k2:k2 + 1], in_=ltmp[:], axis=AX.X, op=ALU.add),
                     reads=[r_c], writes=[r_c])
            S.op("act", lambda e: e.activation(out=lsc[:, 2:4], in_=lsc[:, 0:2], func=AF.Exp), reads=[r_c], writes=[r_c])
            S.op("dve", lambda e: e.tensor_tensor(out=lsc[:, 4:5], in0=lsc[:, 3:4], in1=lsc[:, 2:3], op=ALU.subtract),
                 reads=[r_c], writes=[r_c])
            S.op("dve", lambda e: e.tensor_scalar(out=lsc[:, 5:6], in0=lsc[:, 4:5], scalar1=-lam_init, scalar2=None, op0=ALU.add),
                 reads=[r_c], writes=[r_c])
            S.op("dve", lambda e: e.tensor_scalar(out=sg[:], in0=sg[:], scalar1=(1.0 - lam_init), scalar2=None, op0=ALU.mult),
                 reads=[r_c], writes=[r_c])
            for hh in range(8):
                for rel in range(13):
                    S.op("pool", lambda e: e.memset(btab[:, hh, rel:rel + 1], -(2.0 ** -(hh + 1)) * 128.0 * rel), writes=[r_c])
            S.dma("sp", cpf[:], c_pf.ap(), s_c, writes=[r_c])
            for hh in range(8):
                sl_ = 2.0 ** -(hh + 1)
                S.op("dve", lambda e: e.tensor_scalar(out=pcol[:, hh:hh + 1], in0=cpf[:, 0:1], scalar1=sl_, scalar2=None, op0=ALU.mult),
                     reads=[r_c], writes=[r_c])
                S.op("dve", lambda e: e.tensor_scalar(out=btab2[:, hh, :], in0=btab[:, hh, :], scalar1=pcol[:, hh:hh + 1], scalar2=None,
                                                      op0=ALU.add), reads=[r_c], writes=[r_c])
            nlam = lsc[:, 5:6]
            for qi in range(2):
                S.op("dve", lambda e: e.memset(kT[qi][64:66, :, :], 1.0), writes=[r_q[qi]])
            it = 0
            bh = 0
            pending = [None]
            for b in range(2):
                for hh in range(8):
                    qi = bh % 2
                    bh += 1
                    tok = slice(b * SEQ, (b + 1) * SEQ)
                    for c2 in range(2):
                        S.dma("sp", qT[qi][0:64, c2, :], QT.ap()[hh][c2 * 64:(c2 + 1) * 64, tok], s_q[qi], writes=[r_q[qi]])
                        S.dma("sp", kT[qi][0:64, c2, :], KT.ap()[hh][c2 * 64:(c2 + 1) * 64, tok], s_q[qi], writes=[r_q[qi]])
                        S.dma("sp", qT[qi][64:66, c2, :], c_aug.ap()[hh], s_q[qi], writes=[r_q[qi]])
                    S.dma("sp", vt[qi][:], V.ap()[tok, hh * 128:(hh + 1) * 128].rearrange("(t p) f -> p t f", p=128),
                          s_q[qi], writes=[r_q[qi]])
                    slope = 2.0 ** -(hh + 1)
                    for j in range(4):
                        nk = 4 * (j + 1)
                        slots = {}

                        def keep(i):
                            rel = 4 * j - i
                            return rel <= 0 or slope * (128 * rel - 127) <= 40.0
                        tiles = [i for i in range(nk) if keep(i)]
                        nkk = len(tiles)

                        def emit_S(n):
                            nonlocal it
                            i = tiles[n]
                            a = 2 * (n % 2)
                            ksl = slice(i * 128, (i + 1) * 128)
                            rel = 4 * j - i
                            c0 = 0 if rel >= 1 else -rel * 128
                            pidx = 0 if rel >= 1 else 1 - rel
                            cs_ = slice(c0, 512)
                            qs_ = slice(j * 512 + c0, (j + 1) * 512)
                            kk = 66 if rel >= 1 else 64
                            xs2 = []
                            for c2 in range(2):
                                S.op("pe", lambda e: e.matmul(PS[a + c2][:, cs_], kT[qi][0:kk, c2, ksl], qT[qi][0:kk, c2, qs_], start=True, stop=True),
                                     reads=[r_q[qi]], writes=[PSR[a + c2]])
                            for c2 in range(2):
                                x = it % NB
                                it += 1
                                xs2.append(x)
                                if rel >= 1:
                                    S.op("act", lambda e: e.activation(out=Eb[x][:], in_=PS[a + c2][:], func=AF.Exp,
                                                                       bias=btab2[:, hh, rel:rel + 1], scale=0.125),
                                         reads=[PSR[a + c2], r_c], writes=[r_E[x]])
                                else:
                                    S.op("dve", lambda e: e.scalar_tensor_tensor(out=sbias[x][:, cs_], in0=alib[:, pidx, cs_], scalar=-8.0 * slope,
                                                                                 in1=PS[a + c2][:, cs_], op0=ALU.mult, op1=ALU.add),
                                         reads=[r_c, PSR[a + c2]], writes=[r_sb[x]])
                                    S.op("act", lambda e: e.activation(out=Eb[x][:, cs_], in_=sbias[x][:, cs_], func=AF.Exp, scale=0.125),
                                         reads=[r_sb[x]], writes=[r_E[x]])
                            if tick is not None:
                                tick()
                            slots[n] = (xs2, cs_)

                        def emit_OZ(n):
                            i = tiles[n]
                            (x1, x2), cs_ = slots[n]
                            st_, sp_ = (n == 0), (n == nkk - 1)
                            S.op("pe", lambda e: e.matmul(PS[4][:, cs_], vt[qi][:, i, :], Eb[x1][:, cs_], start=st_, stop=sp_),
                                 reads=[r_q[qi], r_E[x1]], writes=[PSR[4]])
                            S.op("pe", lambda e: e.matmul(PS[5][:, cs_], onesb[:], Eb[x1][:, cs_], start=st_, stop=sp_),
                                 reads=[r_E[x1]], writes=[PSR[5]])
                            S.op("pe", lambda e: e.matmul(PS[6][:, cs_], vt[qi][:, i, :], Eb[x2][:, cs_], start=st_, stop=sp_),
                                 reads=[r_q[qi], r_E[x2]], writes=[PSR[6]])
                            S.op("pe", lambda e: e.matmul(PS[7][:, cs_], onesb[:], Eb[x2][:, cs_], start=st_, stop=sp_),
                                 reads=[r_E[x2]], writes=[PSR[7]])

                        for n in range(nkk + 2):
                            if n < nkk:
                                emit_S(n)
                            if n == 1 and pending[0] is not None:
                                pending[0]()
                                pending[0] = None
                            if n >= 2:
                                emit_OZ(n - 2)

                        S.op("act", lambda e: e.activation(out=fr[:], in_=PS[5][:], func=AF.Ln), reads=[PSR[5]], writes=[r_fr])
                        S.op("act", lambda e: e.activation(out=fr2[:], in_=PS[7][:], func=AF.Ln), reads=[PSR[7]], writes=[r_fr2])
                        S.op("dve", lambda e: e.tensor_copy(out=o1s[:], in_=PS[4][:]), reads=[PSR[4]], writes=[r_o1s])
                        S.op("dve", lambda e: e.tensor_copy(out=o2s[:], in_=PS[6][:]), reads=[PSR[6]], writes=[r_o2s])

                        def finalize(b=b, hh=hh, j=j, oi=(bh * 4 + j) % 2):
                            S.op("act", lambda e: e.activation(out=fr[:], in_=fr[:], func=AF.Exp, scale=-1.0), reads=[r_fr], writes=[r_fr])
                            S.op("dve", lambda e: e.tensor_tensor(out=ft1[:], in0=o1s[:], in1=fr[:], op=ALU.mult),
                                 reads=[r_o1s, r_fr], writes=[r_ft1])
                            S.op("act", lambda e: e.activation(out=fr2[:], in_=fr2[:], func=AF.Exp, scale=-1.0), reads=[r_fr2], writes=[r_fr2])
                            S.op("dve", lambda e: e.tensor_tensor(out=ft2[:], in0=o2s[:], in1=fr2[:], op=ALU.mult),
                                 reads=[r_o2s, r_fr2], writes=[r_ft2])
                            S.op("dve", lambda e: e.scalar_tensor_tensor(out=fo[:], in0=ft2[:], scalar=nlam, in1=ft1[:],
                                                                         op0=ALU.mult, op1=ALU.add),
                                 reads=[r_ft1, r_ft2, r_c], writes=[r_fo])
                            S.op("dve", lambda e: e.tensor_tensor(out=fsq[:], in0=fo[:], in1=fo[:], op=ALU.mult), reads=[r_fo], writes=[r_fsq])
                            S.op("pe", lambda e: e.matmul(PS[5][:], onesb[:], fsq[:], start=True, stop=True), reads=[r_fsq], writes=[PSR[5]])
                            S.op("act", lambda e: e.activation(out=frs[:], in_=PS[5][:], func=AF.Ln, bias=epsc[:], scale=1.0 / 128),
                                 reads=[PSR[5]], writes=[r_frs])
                            S.op("act", lambda e: e.activation(out=frs[:], in_=frs[:], func=AF.Exp, scale=-0.5), reads=[r_frs], writes=[r_frs])
                            S.op("dve", lambda e: e.scalar_tensor_tensor(out=fon[oi][:], in0=fo[:], scalar=sg[:, 0:1], in1=frs[:],
                                                                         op0=ALU.mult, op1=ALU.mult),
                                 reads=[r_fo, r_frs, r_c], writes=[r_fon[oi]])
                            S.dma("act", OT.ap()[hh][:, b * SEQ + j * 512:b * SEQ + (j + 1) * 512], fon[oi][:], s_on[oi], reads=[r_fon[oi]])
                        pending[0] = finalize
            pending[0]()
            S.barrier()

    def phase_ssm(l, attn_fn):
        def bc(t, off, rowlen, n1, n2):
            return bass.AP(t, off, [[rowlen, 128], [1, n1], [0, n2]])
        with ExitStack() as st:
            W1b = sb(st, "ssW1", [128, 64, 128], BF16)
            W4re = sb(st, "ssW4r", [128, 32, 128], BF16); W4im = sb(st, "ssW4i", [128, 32, 128], BF16)
            AR2 = sb(st, "ssAR2", [128, 2, 32], F32); NAI = sb(st, "ssNAI", [128, 2, 32], F32)
            W2re = sb(st, "ssW2r", [128, 64, 64], BF16); W2im = sb(st, "ssW2i", [128, 64, 64], BF16)
            r_W = S.res()
            r_SH = [S.res() for _ in range(2)]; r_H = [S.res() for _ in range(2)]
            with ExitStack() as st2:
                An = sb(st2, "ssAn", [32, 2, 128], F32)
                Are = sb(st2, "ssAre", [128, 32], F32); Aim = sb(st2, "ssAim", [128, 32], F32)
                Ldt = sb(st2, "ssLdt", [128, 32], F32)
                dre = sb(st2, "ssdre", [128, 32], F32); dim = sb(st2, "ssdim", [128, 32], F32)
                mag = sb(st2, "ssmag", [128, 32], F32)
                cs = sb(st2, "sscs", [128, 2, 32], F32)
                t1 = sb(st2, "sst1", [128, 32], F32); t2 = sb(st2, "sst2", [128, 32], F32)
                PW = sb(st2, "ssPW", [128, 9, 2, 32], F32)
                FF = sb(st2, "ssFF", [128, 2, 32], F32)
                FP = sb(st2, "ssFP", [128, 8, 2, 32], F32)
                hpi = sb(st2, "sshpi", [128, 1], F32)
                Bre = sb(st2, "ssBre", [128, 32, 16], F32); Bim = sb(st2, "ssBim", [128, 32, 16], F32)
                T1 = sb(st2, "ssT1", [128, 32, 16], F32); T2 = sb(st2, "ssT2", [128, 32, 16], F32)
                ABr = [sb(st2, f"ssAB{i}", [128, 32, 240], F32) for i in range(2)]
                ABrb = [sb(st2, f"ssABb{i}", [128, 32, 240], BF16) for i in range(2)]
                CTb = [sb(st2, f"ssCTb{i}", [128, 32, 16], BF16) for i in range(2)]
                Cn = sb(st2, "ssCn", [128, 2, 64], F32)
                CT = [sb(st2, f"ssCT{i}", [128, 32, 16], F32) for i in range(3)]
                dgi = sb(st2, "ssdgi", [64, 8, 16], F32)
                Drep = sb(st2, "ssDrep", [128, 64], F32)
                r_p = S.res(); s_p = S.dsem(); r_cn = S.res(); s_cn = S.dsem()
                D_ = lambda fn, rd=(), wr=(): S.op("dve", fn, reads=[r_p] + list(rd), writes=[r_p] + list(wr))
                A_ = lambda fn, rd=(), wr=(): S.op("act", fn, reads=[r_p] + list(rd), writes=[r_p] + list(wr))
                TT = lambda o, a, b_, op: D_(lambda e: e.tensor_tensor(out=o, in0=a, in1=b_, op=op))
                for k2, src in enumerate([a_re, a_im]):
                    for gh in range(2):
                        S.dma("sp", An[:, k2, gh * 64:(gh + 1) * 64], src.ap()[l][gh * 32:(gh + 1) * 32, :], s_p, writes=[r_p])
                for gh in range(2):
                    S.dma("sp", Ldt[gh * 64:(gh + 1) * 64, :], dap(log_dt, l * 64 + gh * 32, [[0, 64], [1, 32]]), s_p, writes=[r_p])
                    S.dma("sp", Bre[gh * 64:(gh + 1) * 64, :, :], b_re.ap()[l][gh * 32:(gh + 1) * 32].rearrange("g p i -> p g i"), s_p, writes=[r_p])
                    S.dma("sp", Bim[gh * 64:(gh + 1) * 64, :, :], b_im.ap()[l][gh * 32:(gh + 1) * 32].rearrange("g p i -> p g i"), s_p, writes=[r_p])
                S.dma("sp", dgi[:], dap(ssm_d, l * 1024, [[16, 64], [0, 8], [1, 16]]), s_p, writes=[r_p])
                D_(lambda e: e.memset(hpi[:], math.pi / 2))
                for k2, dst in enumerate([Are, Aim]):
                    S.op("pe", lambda e: e.transpose(out=PS[k2][:, 0:32], in_=An[:, k2, :], identity=ident[0:32, 0:32]),
                         reads=[r_p], writes=[PSR[k2]])
                    D_(lambda e: e.tensor_copy(out=dst[:], in_=PS[k2][:, 0:32]), rd=[PSR[k2]])
                A_(lambda e: e.activation(out=Ldt[:], in_=Ldt[:], func=AF.Exp))
                TT(dre[:], Ldt[:], Are[:], ALU.mult)
                TT(dim[:], Ldt[:], Aim[:], ALU.mult)
                A_(lambda e: e.activation(out=mag[:], in_=dre[:], func=AF.Exp))
                A_(lambda e: e.activation(out=cs[:, 1, :], in_=dim[:], func=AF.Sin, scale=1.0 / 16))
                A_(lambda e: e.activation(out=cs[:, 0, :], in_=dim[:], func=AF.Sin, bias=hpi[:], scale=-1.0 / 16))
                for _ in range(4):
                    TT(t1[:], cs[:, 0, :], cs[:, 0, :], ALU.mult)
                    TT(t2[:], cs[:, 1, :], cs[:, 1, :], ALU.mult)
                    D_(lambda e: e.scalar_tensor_tensor(out=cs[:, 1, :], in0=cs[:, 0, :], scalar=2.0, in1=cs[:, 1, :],
                                                        op0=ALU.mult, op1=ALU.mult))
                    TT(cs[:, 0, :], t1[:], t2[:], ALU.subtract)
                D_(lambda e: e.memset(PW[:, 0, 0, :], 1.0))
                D_(lambda e: e.memset(PW[:, 0, 1, :], 0.0))
                TT(PW[:, 1, 0, :], mag[:], cs[:, 0, :], ALU.mult)
                TT(PW[:, 1, 1, :], mag[:], cs[:, 1, :], ALU.mult)
                ar, ai = PW[:, 1, 0, :], PW[:, 1, 1, :]

                def cmul(o_re, o_im, x_re, x_im, y_re, y_im):
                    TT(t1[:], x_re, y_re, ALU.mult); TT(t2[:], x_im, y_im, ALU.mult)
                    TT(o_re, t1[:], t2[:], ALU.subtract)
                    TT(t1[:], x_re, y_im, ALU.mult); TT(t2[:], x_im, y_re, ALU.mult)
                    TT(o_im, t1[:], t2[:], ALU.add)
                for k in range(2, 9):
                    cmul(PW[:, k, 0, :], PW[:, k, 1, :], PW[:, k - 1, 0, :], PW[:, k - 1, 1, :], ar, ai)
                TT(t1[:], Are[:], Are[:], ALU.mult); TT(t2[:], Aim[:], Aim[:], ALU.mult)
                TT(mag[:], t1[:], t2[:], ALU.add)
                D_(lambda e: e.reciprocal(out=mag[:], in_=mag[:]))
                D_(lambda e: e.tensor_scalar(out=dre[:], in0=ar, scalar1=-1.0, scalar2=None, op0=ALU.add))
                TT(t1[:], dre[:], Are[:], ALU.mult); TT(t2[:], ai, Aim[:], ALU.mult)
                TT(t1[:], t1[:], t2[:], ALU.add); TT(FF[:, 0, :], t1[:], mag[:], ALU.mult)
                TT(t1[:], ai, Are[:], ALU.mult); TT(t2[:], dre[:], Aim[:], ALU.mult)
                TT(t1[:], t1[:], t2[:], ALU.subtract); TT(FF[:, 1, :], t1[:], mag[:], ALU.mult)
                for k in range(8):
                    cmul(FP[:, k, 0, :], FP[:, k, 1, :], PW[:, k, 0, :], PW[:, k, 1, :], FF[:, 0, :], FF[:, 1, :])
                D_(lambda e: e.memset(ABr[0][:], 0.0)); D_(lambda e: e.memset(ABr[1][:], 0.0))
                for tau in range(8):
                    bk = 7 - tau
                    fr_ = bc(FP, tau * 64, 512, 32, 16); fi_ = bc(FP, tau * 64 + 32, 512, 32, 16)
                    osl = slice(bk * 16, (bk + 1) * 16)
                    TT(T1[:], Bre[:], fr_, ALU.mult); TT(T2[:], Bim[:], fi_, ALU.mult)
                    TT(ABr[0][:, :, osl], T1[:], T2[:], ALU.subtract)
                    TT(T1[:], Bim[:], fr_, ALU.mult); TT(T2[:], Bre[:], fi_, ALU.mult)
                    TT(ABr[1][:, :, osl], T1[:], T2[:], ALU.add)
                for k2, src in enumerate([c_re, c_im]):
                    for blk in range(4):
                        for gh in range(2):
                            g0 = gh * 32 + blk * 8
                            S.dma("sp", Cn[:, gh, :], src.ap()[l][g0:g0 + 8].rearrange("g o p -> (g o) p"), s_cn, writes=[r_cn])
                        pb = 2 + (k2 * 4 + blk) % 2
                        S.op("pe", lambda e: e.transpose(out=PS[pb][:, 0:128], in_=Cn[:].rearrange("p a b -> p (a b)"), identity=ident[:]),
                             reads=[r_cn], writes=[PSR[pb]])
                        D_(lambda e: e.tensor_copy(out=CT[k2][:, blk * 8:(blk + 1) * 8, :].rearrange("p a b -> p (a b)"), in_=PS[pb][:, 0:128]),
                           rd=[PSR[pb]])
                D_(lambda e: e.tensor_scalar(out=CT[2][:], in0=CT[1][:], scalar1=-1.0, scalar2=None, op0=ALU.mult))
                D_(lambda e: e.tensor_copy(out=ABrb[0][:], in_=ABr[0][:]))
                A_(lambda e: e.activation(out=ABrb[1][:], in_=ABr[1][:], func=AF.Copy))
                D_(lambda e: e.tensor_copy(out=CTb[0][:], in_=CT[0][:]))
                D_(lambda e: e.tensor_copy(out=CTb[1][:], in_=CT[2][:]))
                S.op("pe", lambda e: e.matmul(PS[4][:, 0:64], dgi[:].rearrange("p a b -> p (a b)"), ident[0:64, 0:64], start=True, stop=True),
                     reads=[r_p], writes=[PSR[4]])
                D_(lambda e: e.tensor_copy(out=Drep[:], in_=PS[4][:, 0:64]), rd=[PSR[4]])
                for g4 in range(16):
                    pb = 5 + g4 % 2
                    for q in range(4):
                        g = g4 * 4 + q
                        gh, gl = g // 32, g % 32
                        ps_ = slice(gh * 64, (gh + 1) * 64)
                        for t in range(8):
                            wsl = slice((7 - t) * 16, (7 - t) * 16 + 128)
                            osl = slice(q * 128 + t * 16, q * 128 + (t + 1) * 16)
                            S.op("pe", lambda e: e.matmul(PS[pb][:, osl], ABrb[0][ps_, gl, wsl], CTb[0][ps_, gl, :], start=True, stop=False),
                                 reads=[r_p], writes=[PSR[pb]], sig=False)
                            S.op("pe", lambda e: e.matmul(PS[pb][:, osl], ABrb[1][ps_, gl, wsl], CTb[1][ps_, gl, :], start=False, stop=True),
                                 reads=[r_p], writes=[PSR[pb]], sig=(t == 7 and q == 3))
                    for q in range(4):
                        g = g4 * 4 + q
                        S.op("dve", lambda e: e.scalar_tensor_tensor(out=W1b[:, g, :], in0=ident[:], scalar=Drep[:, g:g + 1],
                                                                     in1=PS[pb][:, q * 128:(q + 1) * 128], op0=ALU.mult, op1=ALU.add),
                             reads=[r_p, PSR[pb]], writes=[r_W])
                for k2, dst in enumerate([W2re, W2im]):
                    for g8 in range(8):
                        pb = (k2 * 8 + g8) % 2
                        for q in range(8):
                            g = g8 * 8 + q
                            gh, gl = g // 32, g % 32
                            ps_ = slice(gh * 64, (gh + 1) * 64)
                            S.op("pe", lambda e: e.transpose(out=PS[pb][:, q * 64:(q + 1) * 64], in_=ABr[k2][ps_, gl, 0:128],
                                                             identity=ident[ps_, ps_]), reads=[r_p], writes=[PSR[pb]], sig=(q == 7))
                        S.op("act", lambda e: e.activation(out=dst[:, g8 * 8:(g8 + 1) * 8, :].rearrange("p a b -> p (a b)"), in_=PS[pb][:],
                                                           func=AF.Copy), reads=[PSR[pb]], writes=[r_W])
                for t in range(8):
                    k = t + 1
                    pr_ = bc(PW, k * 64, 576, 32, 16); pi_ = bc(PW, k * 64 + 32, 576, 32, 16)
                    osl = slice(t * 16, (t + 1) * 16)
                    TT(T1[:], CT[0][:], pr_, ALU.mult); TT(T2[:], CT[1][:], pi_, ALU.mult)
                    D_(lambda e: e.tensor_tensor(out=W4re[:, :, osl], in0=T1[:], in1=T2[:], op=ALU.subtract), wr=[r_W])
                    TT(T1[:], CT[2][:], pr_, ALU.mult); TT(T2[:], CT[0][:], pi_, ALU.mult)
                    D_(lambda e: e.tensor_tensor(out=W4im[:, :, osl], in0=T1[:], in1=T2[:], op=ALU.subtract), wr=[r_W])
                for r2 in range(2):
                    D_(lambda e: e.tensor_copy(out=AR2[:, r2, :], in_=PW[:, 8, 0, :]), wr=[r_W])
                D_(lambda e: e.tensor_copy(out=NAI[:, 1, :], in_=PW[:, 8, 1, :]), wr=[r_W])
                D_(lambda e: e.tensor_scalar(out=NAI[:, 0, :], in0=PW[:, 8, 1, :], scalar1=-1.0, scalar2=None, op0=ALU.mult), wr=[r_W])
                S.barrier()
            SH = sb(st, "ssSH", [128, 2, 32, 2, 257], BF16)
            Hst = [sb(st, f"ssH{b}", [128, 2, 32, 2], F32) for b in range(2)]
            Pt = sb(st, "ssP", [128, 2, 32, 2], F32)
            Qt = sb(st, "ssQ", [128, 2, 32, 2], F32)
            r_P = S.res(); r_Q = S.res(); r_SHs = S.res()
            with ExitStack() as st3:
                ub = [sb(st3, f"ssub{i}", [128, 512], BF16) for i in range(4)]
                r_ub = [S.res() for _ in range(4)]; s_ub = [S.dsem() for _ in range(4)]
                for b in range(2):
                    S.op("dve", lambda e: e.memset(SH[:, :, :, b, 0:1], 0.0), writes=[r_SH[b]])
                    S.op("dve", lambda e: e.memset(Hst[b][:], 0.0), writes=[r_H[b]])
                uc = 0
                for gl in range(32):
                    a = 2 * (gl % 2)
                    for gh in range(2):
                        g = gh * 32 + gl
                        ui = uc % 4; uc += 1
                        S.dma("sp", ub[ui][:], U2.ap()[g].rearrange("s i n -> (s i) n"), s_ub[ui], writes=[r_ub[ui]])
                        ps_ = slice(gh * 64, (gh + 1) * 64)
                        S.op("pe", lambda e: e.matmul(PS[a][ps_, :], W2re[:, g, :], ub[ui][:], start=True, stop=True),
                             reads=[r_W, r_ub[ui]], writes=[PSR[a]])
                        S.op("pe", lambda e: e.matmul(PS[a + 1][ps_, :], W2im[:, g, :], ub[ui][:], start=True, stop=True),
                             reads=[r_W, r_ub[ui]], writes=[PSR[a + 1]])
                    S.op("act", lambda e: e.activation(out=SH[:, 0, gl, :, 1:257], in_=PS[a][:].rearrange("p (b c) -> p b c", b=2), func=AF.Copy),
                         reads=[PSR[a]], writes=r_SH)
                    S.op("dve", lambda e: e.tensor_copy(out=SH[:, 1, gl, :, 1:257], in_=PS[a + 1][:].rearrange("p (b c) -> p b c", b=2)),
                         reads=[PSR[a + 1]], writes=r_SH)
                S.barrier()
            state = {"c": 0}

            ARb = bass.AP(AR2.tensor if hasattr(AR2, "tensor") else AR2, 0, [[64, 128], [1, 64], [0, 2]])
            NAb = [bass.AP(NAI.tensor if hasattr(NAI, "tensor") else NAI, r2 * 32, [[64, 128], [1, 32], [0, 2]]) for r2 in range(2)]

            def tick():
                c = state["c"]
                if c >= 256:
                    return
                state["c"] = c + 1
                Hc, Hn = Hst[c % 2], Hst[(c + 1) % 2]
                rHc, rHn = r_H[c % 2], r_H[(c + 1) % 2]
                S.op("pool", lambda e: e.tensor_tensor(out=Pt[:].rearrange("p r g b -> p (r g) b"), in0=ARb,
                                                      in1=Hc[:].rearrange("p r g b -> p (r g) b"), op=ALU.mult),
                     reads=[rHc, r_W], writes=[r_P])
                S.op("pool", lambda e: e.tensor_tensor(out=Qt[:, 0, :, :], in0=NAb[0], in1=Hc[:, 1, :, :], op=ALU.mult),
                     reads=[rHc, r_W], writes=[r_Q])
                S.op("pool", lambda e: e.tensor_tensor(out=Qt[:, 1, :, :], in0=NAb[1], in1=Hc[:, 0, :, :], op=ALU.mult),
                     reads=[rHc, r_W], writes=[r_Q])
                S.op("pool", lambda e: e.tensor_tensor(out=Pt[:], in0=Pt[:], in1=Qt[:], op=ALU.add), reads=[r_P, r_Q], writes=[r_P])
                S.op("pool", lambda e: e.tensor_tensor(out=Hn[:], in0=Pt[:], in1=SH[:, :, :, :, c + 1], op=ALU.add),
                     reads=[r_P] + r_SH, writes=[rHn])
                S.op("pool", lambda e: e.tensor_copy(out=SH[:, :, :, :, c + 1], in_=Hn[:]), reads=[rHn], writes=r_SH)

            attn_fn(l, tick)
            while state["c"] < 256:
                tick()
            with ExitStack() as st4:
                ub = [sb(st4, f"ssub{i}", [128, 512], BF16) for i in range(4)]
                r_ub = [S.res() for _ in range(4)]; s_ub = [S.dsem() for _ in range(4)]
                yo = [sb(st4, f"ssyo{i}", [128, 512], BF16) for i in range(2)]
                r_yo = [S.res() for _ in range(2)]; s_yo = [S.dsem() for _ in range(2)]
                yc = 0; uc = 0
                for g in range(64):
                    gh, gl = g // 32, g % 32
                    ps_ = slice(gh * 64, (gh + 1) * 64)
                    ui = uc % 4; uc += 1
                    S.dma("sp", ub[ui][:], U2.ap()[g].rearrange("s i n -> (s i) n"), s_ub[ui], writes=[r_ub[ui]])
                    pb = 4 + g % 4
                    S.op("pe", lambda e: e.matmul(PS[pb][:], W1b[:, g, :], ub[ui][:], start=True, stop=False),
                         reads=[r_W, r_ub[ui]], writes=[PSR[pb]], sig=False)
                    S.op("pe", lambda e: e.matmul(PS[pb][:], W4re[ps_, gl, :], SH[ps_, 0, gl, :, 0:256], start=False, stop=False),
                         reads=[r_W] + r_SH, writes=[PSR[pb]], sig=False)
                    S.op("pe", lambda e: e.matmul(PS[pb][:], W4im[ps_, gl, :], SH[ps_, 1, gl, :, 0:256], start=False, stop=True),
                         reads=[r_W] + r_SH, writes=[PSR[pb]])
                    yi = yc % 2; yc += 1
                    if g % 2 == 0:
                        S.op("act", lambda e: e.activation(out=yo[yi][:], in_=PS[pb][:], func=AF.Copy), reads=[PSR[pb]], writes=[r_yo[yi]])
                    else:
                        S.op("dve", lambda e: e.tensor_copy(out=yo[yi][:], in_=PS[pb][:]), reads=[PSR[pb]], writes=[r_yo[yi]])
                    S.dma("act", Y2.ap()[g].rearrange("t o n -> (t o) n"), yo[yi][:], s_yo[yi], reads=[r_yo[yi]])
                S.barrier()

    class WStream:
        def __init__(self, st, tag, nkmax, items, nbf=2, dist=1):
            self.items = items
            self.nbf = nbf
            self.dist = dist
            self.wst = [sb(st, f"{tag}ws{i}", [128, nkmax, 128], F32) for i in range(2)]
            self.wbf = [sb(st, f"{tag}wb{i}", [128, nkmax, 128], BF16) for i in range(nbf)]
            self.r_ws = [S.res() for _ in range(2)]; self.r_wb = [S.res() for _ in range(nbf)]
            self.s_ws = [S.dsem() for i in range(2)]
            self.issued = 0
            self.idx = 0

        def _issue(self, k):
            view, nk = self.items[k]
            i = k % 2
            j = k % self.nbf
            S.dma("sp", self.wst[i][:, 0:nk, :], view, self.s_ws[i], writes=[self.r_ws[i]])
            S.op("pool", lambda e: e.tensor_copy(out=self.wbf[j][:, 0:nk, :], in_=self.wst[i][:, 0:nk, :]),
                 reads=[self.r_ws[i]], writes=[self.r_wb[j]])

        def next(self):
            while self.issued < min(len(self.items), self.idx + self.dist + 1):
                self._issue(self.issued)
                self.issued += 1
            j = self.idx % self.nbf
            self.idx += 1
            return self.wbf[j], self.r_wb[j]

    def mm_acc(pb, wb, r_wb, X, r_X, nk, tsl, first=True, last=True):
        for c in range(nk):
            S.op("pe", lambda e: e.matmul(PS[pb][:], wb[:, c, :], X[:, c, tsl], start=(first and c == 0), stop=(last and c == nk - 1)),
                 reads=[r_wb, r_X], writes=[PSR[pb]], sig=(c == nk - 1))

    def phase_mix(l):
        with ExitStack() as st:
            zT = sb(st, "mxz", [128, 8, NT], BF16)
            r_z = S.res()
            glb = sb(st, "mxgb", [128, 8], F32); r_gb = S.res(); s_gb = S.dsem("mxsgb")
            S.dma("sp", glb[:], glu_b.ap()[l].rearrange("(c p) -> p c", p=128), s_gb, writes=[r_gb], allow_slow_non_contiguous=True)
            with ExitStack() as st2:
                yT = sb(st2, "mxy", [128, 8, NT], BF16); r_y = S.res()
                yl = [sb(st2, f"mxyl{i}", [128, 8, 512], BF16) for i in range(2)]
                r_yl = [S.res() for _ in range(2)]; s_yl = [S.dsem(f"mxsyl{i}") for i in range(2)]
                sgt = [sb(st2, f"mxsg{i}", [128, 512], BF16) for i in range(2)]; r_sg = [S.res() for _ in range(2)]
                wgv = glu_w.ap()[l].rearrange("(c p) n -> p c n", p=128)
                ws = WStream(st2, "mxa", 8, [(wgv[:, :, m * 128:(m + 1) * 128], 8) for m in range(8)])
                for mu in range(8):
                    i = mu % 2
                    for gl in range(8):
                        S.dma("sp", yl[i][gl * 16:(gl + 1) * 16, :, :], Y2.ap()[mu * 8 + gl].rearrange("t o n -> o t n"),
                              s_yl[i], writes=[r_yl[i]])
                    S.op("act", lambda e: e.activation(out=yT[:, mu, :].rearrange("p (n t) -> p t n", t=8), in_=yl[i][:],
                                                       func=AF.Gelu_apprx_tanh), reads=[r_yl[i]], writes=[r_y])
                cnt = 0
                for m in range(8):
                    wb, r_wb = ws.next()
                    for j in range(8):
                        pb = cnt % 4; k = cnt % 2; cnt += 1
                        tsl = slice(j * 512, (j + 1) * 512)
                        mm_acc(pb, wb, r_wb, yT, r_y, 8, tsl)
                        S.op("act", lambda e: e.activation(out=sgt[k][:], in_=PS[pb][:], func=AF.Sigmoid, bias=glb[:, m:m + 1], scale=1.0),
                             reads=[PSR[pb], r_gb], writes=[r_sg[k]])
                        S.op("dve", lambda e: e.tensor_tensor(out=zT[:, m, tsl], in0=yT[:, m, tsl], in1=sgt[k][:], op=ALU.mult),
                             reads=[r_sg[k], r_y], writes=[r_z])
                S.barrier()
            with ExitStack() as st2:
                oT = sb(st2, "mxo", [128, 8, NT], BF16); r_o = S.res(); s_o = S.dsem("mxso")
                tmp = sb(st2, "mxt", [128, NT], F32); r_t = S.res()
                gt = [sb(st2, f"mxg{i}", [128, 512], BF16) for i in range(4)]; r_g = [S.res() for _ in range(4)]
                s_g = [S.dsem(f"mxsg{i}") for i in range(4)]
                mo = [sb(st2, f"mxmo{i}", [128, 512], BF16) for i in range(2)]; r_mo = [S.res() for _ in range(2)]
                s_mo = [S.dsem(f"mxsmo{i}") for i in range(2)]
                t2 = [sb(st2, f"mxt2{i}", [128, 512], F32) for i in range(2)]; r_t2 = [S.res() for _ in range(2)]
                wav = w_ba.ap()[l].rearrange("(c p) n -> p c n", p=128)
                wsv = w_bs.ap()[l].rearrange("(c p) n -> p c n", p=128)
                its = []
                for m in range(16):
                    its.append((wav[:, :, m * 128:(m + 1) * 128], 8))
                    its.append((wsv[:, :, m * 128:(m + 1) * 128], 8))
                ws = WStream(st2, "mxb", 8, its)
                OTv = OT.ap().rearrange("c p n -> p c n")
                for j in range(8):
                    S.dma("sp", oT[:, :, j * 512:(j + 1) * 512], OTv[:, :, j * 512:(j + 1) * 512], s_o, writes=[r_o])
                cnt = 0; gc = 0
                for m in range(16):
                    wb, r_wb = ws.next()
                    for j in range(8):
                        pb = cnt % 4; cnt += 1
                        tsl = slice(j * 512, (j + 1) * 512)
                        gi = gc % 4; gc += 1
                        S.dma("sp", gt[gi][:], GA.ap()[m][:, tsl], s_g[gi], writes=[r_g[gi]])
                        mm_acc(pb, wb, r_wb, oT, r_o, 8, tsl)
                        S.op("dve", lambda e: e.tensor_tensor(out=tmp[:, tsl], in0=PS[pb][:], in1=gt[gi][:], op=ALU.mult),
                             reads=[PSR[pb], r_g[gi]], writes=[r_t])
                    wb, r_wb = ws.next()
                    for j in range(8):
                        pb = cnt % 4; k = cnt % 2; cnt += 1
                        tsl = slice(j * 512, (j + 1) * 512)
                        gi = gc % 4; gc += 1
                        S.dma("sp", gt[gi][:], GS.ap()[m][:, tsl], s_g[gi], writes=[r_g[gi]])
                        mm_acc(pb, wb, r_wb, zT, r_z, 8, tsl)
                        S.op("dve", lambda e: e.tensor_tensor(out=t2[k][:], in0=PS[pb][:], in1=gt[gi][:], op=ALU.mult),
                             reads=[PSR[pb], r_g[gi]], writes=[r_t2[k]])
                        S.op("pool", lambda e: e.tensor_tensor(out=mo[k][:], in0=t2[k][:], in1=tmp[:, tsl], op=ALU.add),
                             reads=[r_t2[k], r_t], writes=[r_mo[k]])
                        S.dma("act", H.ap()[m][:, tsl], mo[k][:], s_mo[k], reads=[r_mo[k]])
                S.barrier()
        with ExitStack() as st:
            Ms = sb(st, "mxM", [128, 16, NT], BF16); r_M = S.res(); s_M = S.dsem("mxsM")
            xt = [sb(st, f"mxx{i}", [128, 512], F32) for i in range(4)]; r_x = [S.res() for _ in range(4)]
            s_x = [S.dsem(f"mxsx{i}") for i in range(4)]
            wov = w_out.ap()[l].rearrange("(c p) n -> p c n", p=128)
            ws = WStream(st, "mxc", 16, [(wov[:, :, m * 128:(m + 1) * 128], 16) for m in range(16)])
            Hv = H.ap().rearrange("c p n -> p c n")
            for j in range(8):
                S.dma("sp", Ms[:, :, j * 512:(j + 1) * 512], Hv[:, :, j * 512:(j + 1) * 512], s_M, writes=[r_M])
            cnt = 0
            for m in range(16):
                wb, r_wb = ws.next()
                for j in range(8):
                    pb = cnt % 4; xi = cnt % 4; cnt += 1
                    tsl = slice(j * 512, (j + 1) * 512)
                    S.dma("sp", xt[xi][:], XT.ap()[m][:, tsl], s_x[xi], writes=[r_x[xi]])
                    mm_acc(pb, wb, r_wb, Ms, r_M, 16, tsl)
                    S.op("dve", lambda e: e.tensor_tensor(out=xt[xi][:], in0=PS[pb][:], in1=xt[xi][:], op=ALU.add),
                         reads=[PSR[pb], r_x[xi]], writes=[r_x[xi]])
                    S.dma("act", XT.ap()[m][:, tsl], xt[xi][:], s_x[xi], reads=[r_x[xi]])
            S.barrier()

    def phase_ffn_up(l):
        with ExitStack() as st:
            Hs = sb(st, "fuH", [128, 16, NT], BF16); r_H = S.res()
            phase_norm(norm2_g, l, f"n2{l}", Hs, r_H)
            cw = sb(st, "fucw", [128, 3, 86], F32); cb = sb(st, "fucb", [128, 86], F32); r_cw = S.res(); s_cw = S.dsem("fuscw")
            up = [[sb(st, f"fuu{a}{i}", [128, 514], F32) for i in range(2)] for a in range(2)]
            r_up = [[S.res() for i in range(2)] for a in range(2)]
            cc = [[sb(st, f"fuc{a}{i}", [128, 512], F32) for i in range(2)] for a in range(2)]
            r_cc = [[S.res() for i in range(2)] for a in range(2)]
            ga = [sb(st, f"fug{i}", [128, 512], F32) for i in range(2)]; r_ga = [S.res() for _ in range(2)]
            ao = [sb(st, f"fuo{i}", [128, 512], BF16) for i in range(2)]; r_ao = [S.res() for _ in range(2)]
            s_ao = [S.dsem(f"fusao{i}") for i in range(2)]
            wuv = w_up.ap()[l].rearrange("(c p) n -> p c n", p=128)
            its = []
            for m in range(NFT):
                its.append((wuv[:, :, m * 128:(m + 1) * 128], 16))
                its.append((wuv[:, :, (NFT + m) * 128:(NFT + m + 1) * 128], 16))
            ws = WStream(st, "fu", 16, its, nbf=4, dist=2)
            for k3 in range(3):
                S.dma("sp", cw[:, k3, :], conv_w.ap()[l][k3].rearrange("(m p) -> p m", p=128), s_cw, writes=[r_cw],
                      allow_slow_non_contiguous=True)
            S.dma("sp", cb[:], conv_b.ap()[l].rearrange("(m p) -> p m", p=128), s_cw, writes=[r_cw], allow_slow_non_contiguous=True)
            cnt = 0
            for m in range(NFT):
                wba, r_wba = ws.next()
                wbv, r_wbv = ws.next()
                for j in range(8):
                    k = cnt % 2; k4 = cnt % 4; cnt += 1
                    tsl = slice(j * 512, (j + 1) * 512)
                    pbs = (2 * k4, 2 * k4 + 1)
                    mm_acc(pbs[0], wba, r_wba, Hs, r_H, 16, tsl)
                    mm_acc(pbs[1], wbv, r_wbv, Hs, r_H, 16, tsl)
                    for a in range(2):
                        fm = m + a * NFT
                        pb = pbs[a]
                        u_, ru_ = up[a][k], r_up[a][k]
                        if j % 4 == 0:
                            S.op("pool", lambda e: e.memset(u_[:, 0:2], 0.0), writes=[ru_])
                        else:
                            S.op("pool", lambda e: e.tensor_copy(out=u_[:, 0:2], in_=up[a][1 - k][:, 512:514]),
                                 reads=[r_up[a][1 - k]], writes=[ru_])
                        S.op("act", lambda e: e.activation(out=u_[:, 2:514], in_=PS[pb][:], func=AF.Copy), reads=[PSR[pb]], writes=[ru_])
                        S.op("act", lambda e: e.activation(out=cc[a][k][:], in_=PS[pb][:], func=AF.Identity, bias=cb[:, fm:fm + 1],
                                                           scale=cw[:, 2, fm:fm + 1]), reads=[PSR[pb], r_cw], writes=[r_cc[a][k]])
                        S.op("dve", lambda e: e.scalar_tensor_tensor(out=cc[a][k][:], in0=u_[:, 1:513], scalar=cw[:, 1, fm:fm + 1],
                                                                     in1=cc[a][k][:], op0=ALU.mult, op1=ALU.add),
                             reads=[ru_, r_cw, r_cc[a][k]], writes=[r_cc[a][k]])
                        S.op("dve", lambda e: e.scalar_tensor_tensor(out=cc[a][k][:], in0=u_[:, 0:512], scalar=cw[:, 0, fm:fm + 1],
                                                                     in1=cc[a][k][:], op0=ALU.mult, op1=ALU.add),
                             reads=[ru_, r_cw, r_cc[a][k]], writes=[r_cc[a][k]])
                    S.op("act", lambda e: e.activation(out=ga[k][:], in_=cc[0][k][:], func=AF.Gelu_apprx_tanh),
                         reads=[r_cc[0][k]], writes=[r_ga[k]])
                    S.op("dve", lambda e: e.tensor_tensor(out=ao[k][:], in0=ga[k][:], in1=cc[1][k][:], op=ALU.mult),
                         reads=[r_ga[k], r_cc[1][k]], writes=[r_ao[k]])
                    S.dma("act", AT.ap()[m][:, tsl], ao[k][:], s_ao[k], reads=[r_ao[k]])
            S.barrier()

    def phase_ffn_down(l):
        for (k0, k1) in KGROUPS:
            nk = k1 - k0
            with ExitStack() as st:
                As = sb(st, "fdA", [128, 15, NT], BF16); r_A = S.res(); s_A = S.dsem(f"fdsA{k0}")
                xt = [sb(st, f"fdx{i}", [128, 512], F32) for i in range(4)]; r_x = [S.res() for _ in range(4)]
                s_x = [S.dsem(f"fdsx{k0}_{i}") for i in range(4)]
                wdv = w_down.ap()[l].rearrange("(c p) n -> p c n", p=128)
                ws = WStream(st, f"fd{k0}", 15, [(wdv[:, k0:k1, m * 128:(m + 1) * 128], nk) for m in range(16)])
                ATv = AT.ap().rearrange("c p n -> p c n")
                for j in range(8):
                    S.dma("sp", As[:, 0:nk, j * 512:(j + 1) * 512], ATv[:, k0:k1, j * 512:(j + 1) * 512], s_A, writes=[r_A])
                cnt = 0
                for m in range(16):
                    wb, r_wb = ws.next()
                    for j in range(8):
                        pb = cnt % 4; xi = cnt % 4; cnt += 1
                        tsl = slice(j * 512, (j + 1) * 512)
                        S.dma("sp", xt[xi][:], XT.ap()[m][:, tsl], s_x[xi], writes=[r_x[xi]])
                        mm_acc(pb, wb, r_wb, As, r_A, nk, tsl)
                        S.op("dve", lambda e: e.tensor_tensor(out=xt[xi][:], in0=PS[pb][:], in1=xt[xi][:], op=ALU.add),
                             reads=[PSR[pb], r_x[xi]], writes=[r_x[xi]])
                        S.dma("act", XT.ap()[m][:, tsl], xt[xi][:], s_x[xi], reads=[r_x[xi]])
                S.barrier()

    def phase_transpose_out():
        with ExitStack() as st:
            xs = [sb(st, f"pox{i}", [128, 16, 512], F32) for i in range(2)]
            yo = [sb(st, f"poy{i}", [128, 4, D], F32) for i in range(2)]
            r_x = [S.res() for _ in range(2)]; r_y = [S.res() for _ in range(2)]
            s_x = [S.dsem(f"posx{i}") for i in range(2)]; s_y = [S.dsem(f"posy{i}") for i in range(2)]
            XTv = XT.ap().rearrange("c p n -> p c n")
            yv = y_out.ap().rearrange("(n s p) d -> n p s d", s=4, p=128)
            cnt = 0
            for j in range(8):
                b = j % 2
                S.dma("sp", xs[b][:], XTv[:, :, j * 512:(j + 1) * 512], s_x[b], writes=[r_x[b]])
                for s4 in range(4):
                    for c4 in range(4):
                        pb = cnt % 8; cnt += 1
                        for cc_ in range(4):
                            c = c4 * 4 + cc_
                            S.op("pe", lambda e: e.transpose(out=PS[pb][:, cc_ * 128:(cc_ + 1) * 128],
                                                             in_=xs[b][:, c, s4 * 128:(s4 + 1) * 128], identity=ident[:]),
                                 reads=[r_x[b]], writes=[PSR[pb]], sig=(cc_ == 3))
                        if cnt % 2 == 0:
                            S.op("act", lambda e: e.activation(out=yo[b][:, s4, c4 * 512:(c4 + 1) * 512], in_=PS[pb][:], func=AF.Copy),
                                 reads=[PSR[pb]], writes=[r_y[b]])
                        else:
                            S.op("dve", lambda e: e.tensor_copy(out=yo[b][:, s4, c4 * 512:(c4 + 1) * 512], in_=PS[pb][:]),
                                 reads=[PSR[pb]], writes=[r_y[b]])
                S.dma("act", yv[j], yo[b][:], s_y[b], reads=[r_y[b]])
            S.barrier()

    phase_transpose_in()
    for l in range(nlayers):
        S.rotate()
        if "ip" in PH: phase_inproj(l)
        if "ss" in PH: phase_ssm(l, phase_attn)
        if "mx" in PH: phase_mix(l)
        if "fu" in PH: phase_ffn_up(l)
        if "fd" in PH: phase_ffn_down(l)
    phase_transpose_out()
    stack.close()
    return nc


def consts():
    ident = np.eye(128, dtype=np.float32)
    p = np.arange(128)[:, None].astype(np.float64)
    f = np.arange(512)[None, :].astype(np.float64)
    al = np.zeros((128, 5, 512), np.float32)
    al[:, 0, :] = (f - p)
    for m in range(4):
        d = f - p - 128.0 * m
        vis = (p + 128 * m) < (np.floor(f / 64) + 1) * 64
        al[:, 1 + m, :] = np.where(vis, np.abs(d), 1.0e9)
    blk = np.zeros((128, 128), np.float32)
    blk[:64, :64] = 1.0
    blk[64:, 64:] = 1.0
    pf = np.zeros((128, 513), np.float32)
    pf[:, 0] = np.arange(128)
    pf[:, 1:] = np.arange(512)[None, :]
    import ml_dtypes
    t = np.arange(SEQ) % 512
    aug = np.zeros((8, 2, SEQ), np.float32)
    for h in range(8):
        sl = 2.0 ** -(h + 1)
        aug[h, 0] = -8.0 * sl * 16.0 * (t // 16)
        aug[h, 1] = -8.0 * sl * (t % 16)
    return {"c_ident": ident, "c_alibi": al, "c_blk": blk, "c_pf": pf, "c_aug": aug.astype(ml_dtypes.bfloat16)}


PARAM_NAMES = ["norm1_g", "w_in", "q_norm_g", "k_norm_g", "lambda_q1", "lambda_k1", "lambda_q2", "lambda_k2",
               "subln_g", "ssm_a_re", "ssm_a_im", "ssm_log_dt", "ssm_b_re", "ssm_b_im", "ssm_c_re", "ssm_c_im",
               "ssm_d", "ssm_glu_w", "ssm_glu_b", "w_branch_attn", "w_branch_ssm", "w_out", "norm2_g",
               "ffn_w_up", "ffn_conv_w", "ffn_conv_b", "ffn_w_down"]


def make_in_maps(inputs):
    x = np.ascontiguousarray(np.asarray(inputs["x"], dtype=np.float32))
    shared = {k: np.ascontiguousarray(np.asarray(inputs[k], dtype=np.float32)) for k in PARAM_NAMES}
    shared.update(consts())
    maps = []
    for c in range(NCORES):
        m = dict(shared)
        m["x"] = x[2 * c:2 * c + 2].reshape(NT, D)
        maps.append(m)
    return maps


def kernel(**inputs):
    nc = build_program()
    res = run_bass_kernel_spmd(nc, make_in_maps(inputs), core_ids=list(range(NCORES)))
    out = np.stack([r["y"].reshape(2, SEQ, D) for r in res.results], axis=0).reshape(16, SEQ, D)
    return out.astype(np.float32)
```

```python
import math
from contextlib import ExitStack
import numpy as np
import concourse.bass as bass
import concourse.mybir as mybir
from concourse.bass_utils import run_bass_kernel_spmd

F32 = mybir.dt.float32
BF16 = mybir.dt.bfloat16
AF = mybir.ActivationFunctionType
ALU = mybir.AluOpType
AX = mybir.AxisListType

NCORES = 8
D = 2048
NT = 4096
SEQ = 2048
DEPTH = 4
DFF = 5504
EPS = 1e-6
NFT = DFF // 128
KGROUPS = [(0, 15), (15, 29), (29, 43)]


class Res:
    __slots__ = ("name", "w", "rs")

    def __init__(self, name):
        self.name = name
        self.w = None
        self.rs = []


class Sched:
    def __init__(self, nc, stack):
        self.nc = nc
        self.stack = stack
        self.eng = {"pe": nc.tensor, "act": nc.scalar, "dve": nc.vector, "pool": nc.gpsimd, "sp": nc.sync}
        self.sem = {}
        self.cnt = {}
        self.waited = {e: {} for e in self.eng}
        self.pending = {e: [] for e in self.eng}
        self.dma_sems = []
        self.free_dsems = []
        self.ndsem = 0
        self.cur = {}
        self.gen = 0
        self.rotate()
        self.nres = 0

    def rotate(self):
        self.gen += 1
        for e in ("pe", "act", "dve", "pool"):
            n = f"c_{e}_{self.gen}"
            self._newsem(n)
            self.cur[e] = n

    def _newsem(self, name):
        self.sem[name] = self.stack.enter_context(self.nc.semaphore(name))
        self.cnt[name] = 0

    def res(self, name=None):
        self.nres += 1
        return Res(name or f"r{self.nres}")

    def dsem(self, name=None):
        if self.free_dsems:
            n = self.free_dsems.pop()
        else:
            self.ndsem += 1
            n = f"d{self.ndsem}"
            self._newsem(n)
        self.dma_sems.append(n)
        return n

    def _wait(self, e, ev):
        if ev is None:
            return
        s, v = ev
        if e == "pe" and s.startswith("c_pe"):
            return
        if self.waited[e].get(s, 0) >= v:
            return
        self.eng[e].wait_ge(self.sem[s], v)
        self.waited[e][s] = v

    def _deps(self, e, reads, writes):
        for r in reads:
            self._wait(e, r.w)
        for w in writes:
            self._wait(e, w.w)
            for ev in w.rs:
                self._wait(e, ev)

    def op(self, e, fn, reads=(), writes=(), sig=True):
        self._deps(e, reads, writes)
        ins = fn(self.eng[e])
        if not sig:
            self.pending[e].append((tuple(reads), tuple(writes)))
            return None
        s = self.cur[e]
        self.cnt[s] += 1
        ins.then_inc(self.sem[s], 1)
        ev = (s, self.cnt[s])
        for (prs, pws) in self.pending[e]:
            for r in prs:
                r.rs.append(ev)
            for w in pws:
                w.w = ev
                w.rs = []
        self.pending[e] = []
        for r in reads:
            r.rs.append(ev)
        for w in writes:
            w.w = ev
            w.rs = []
        return ev

    def dma(self, q, out, in_, sem, reads=(), writes=(), **kw):
        self._deps(q, reads, writes)
        ins = self.eng[q].dma_start(out=out, in_=in_, **kw)
        self.cnt[sem] += 16
        ins.then_inc(self.sem[sem], 16)
        ev = (sem, self.cnt[sem])
        for r in reads:
            r.rs.append(ev)
        for w in writes:
            w.w = ev
            w.rs = []
        return ev

    def barrier(self):
        for s in self.dma_sems:
            if self.cnt[s] > self.waited["sp"].get(s, 0):
                self.eng["sp"].wait_ge(self.sem[s], self.cnt[s])
        self.nc.all_engine_barrier()
        for e in self.eng:
            for s in self.cnt:
                self.waited[e][s] = self.cnt[s]
        self.free_dsems.extend(self.dma_sems)
        self.dma_sems = []


def dap(t, off, pat):
    return bass.AP(t, off, [list(p) for p in pat])


def build_program(nlayers=DEPTH, dbg=None, PH=("n1", "ip", "at", "ss", "mx", "n2", "fu", "fd")):
    nc = bass.Bass("TRN2", target_bir_lowering=False)
    stack = ExitStack()
    S = Sched(nc, stack)
    L = DEPTH

    def din(name, shape, dt=F32):
        return nc.dram_tensor(name, list(shape), dt, kind="ExternalInput")

    def dscr(name, shape, dt, out=False):
        return nc.dram_tensor(name, list(shape), dt, kind=("ExternalOutput" if out else "Internal"))

    x_in = din("x", [NT, D])
    y_out = nc.dram_tensor("y", [NT, D], F32, kind="ExternalOutput")
    norm1_g = din("norm1_g", [L, D]); norm2_g = din("norm2_g", [L, D])
    w_in = din("w_in", [L, D, 8192])
    q_norm_g = din("q_norm_g", [L, 64]); k_norm_g = din("k_norm_g", [L, 64])
    lam_q1 = din("lambda_q1", [L, 64]); lam_k1 = din("lambda_k1", [L, 64])
    lam_q2 = din("lambda_q2", [L, 64]); lam_k2 = din("lambda_k2", [L, 64])
    subln_g = din("subln_g", [L, 128])
    a_re = din("ssm_a_re", [L, 64, 64]); a_im = din("ssm_a_im", [L, 64, 64])
    log_dt = din("ssm_log_dt", [L, 64])
    b_re = din("ssm_b_re", [L, 64, 64, 16]); b_im = din("ssm_b_im", [L, 64, 64, 16])
    c_re = din("ssm_c_re", [L, 64, 16, 64]); c_im = din("ssm_c_im", [L, 64, 16, 64])
    ssm_d = din("ssm_d", [L, 1024])
    glu_w = din("ssm_glu_w", [L, 1024, 1024]); glu_b = din("ssm_glu_b", [L, 1024])
    w_ba = din("w_branch_attn", [L, 1024, D]); w_bs = din("w_branch_ssm", [L, 1024, D])
    w_out = din("w_out", [L, D, D])
    w_up = din("ffn_w_up", [L, D, 2 * DFF]); conv_w = din("ffn_conv_w", [L, 3, 2 * DFF])
    conv_b = din("ffn_conv_b", [L, 2 * DFF]); w_down = din("ffn_w_down", [L, DFF, D])
    c_ident = din("c_ident", [128, 128])
    c_alibi = din("c_alibi", [128, 5, 512])
    c_blk = din("c_blk", [128, 128])
    c_pf = din("c_pf", [128, 513])
    c_aug = din("c_aug", [8, 2, SEQ], BF16)

    isdbg = lambda n: dbg is not None and n in dbg
    XT = dscr("XT", [16, 128, NT], F32, isdbg("XT"))
    H = dscr("H", [16, 128, NT], BF16, isdbg("H"))
    QT = dscr("QT", [8, 128, NT], BF16, isdbg("QT"))
    KT = dscr("KT", [8, 128, NT], BF16, isdbg("KT"))
    V = dscr("V", [NT, 1024], BF16, isdbg("V"))
    U2 = dscr("U2", [64, 8, 16, 512], BF16, isdbg("U2"))
    GA = dscr("GA", [16, 128, NT], BF16, isdbg("GA"))
    GS = dscr("GS", [16, 128, NT], BF16, isdbg("GS"))
    OT = dscr("OT", [8, 128, NT], BF16, isdbg("OT"))
    Y2 = dscr("Y2", [64, 8, 16, 512], BF16, isdbg("Y2"))
    AT = dscr("AT", [NFT, 128, NT], BF16, isdbg("AT"))

    PS = [nc.alloc_psum_tensor(f"ps{i}", [128, 512], F32) for i in range(8)]
    PSR = [S.res(f"ps{i}") for i in range(8)]

    _uid = [0]

    def sb(st, name, shape, dt):
        _uid[0] += 1
        return st.enter_context(nc.sbuf_tensor(f"{name}_{_uid[0]}", list(shape), dt))

    ident = sb(stack, "ident", [128, 128], F32)
    identb = sb(stack, "identb", [128, 128], BF16)
    onesb = sb(stack, "onesb", [128, 128], BF16)
    blkb = sb(stack, "blkb", [128, 128], BF16)
    blkf = sb(stack, "blkf", [128, 128], F32)
    epsc = sb(stack, "epsc", [128, 1], F32)
    r_const = S.res("const")
    s_const = S.dsem("d_const")
    S.dma("sp", ident[:], c_ident.ap(), s_const, writes=[r_const])
    S.dma("sp", blkf[:], c_blk.ap(), s_const, writes=[r_const])
    S.op("dve", lambda e: e.tensor_copy(out=identb[:], in_=ident[:]), reads=[r_const], writes=[r_const])
    S.op("dve", lambda e: e.tensor_copy(out=blkb[:], in_=blkf[:]), reads=[r_const], writes=[r_const])
    S.op("dve", lambda e: e.memset(onesb[:], 1.0), writes=[r_const])
    S.op("dve", lambda e: e.memset(epsc[:], EPS), writes=[r_const])
    S.barrier()

    def phase_transpose_in():
        with ExitStack() as st:
            xin = [sb(st, f"p0x{i}", [128, 4, D], F32) for i in range(2)]
            xo = [sb(st, f"p0o{i}", [128, 16, 512], F32) for i in range(2)]
            r_in = [S.res() for _ in range(2)]; r_o = [S.res() for _ in range(2)]
            s_in = [S.dsem(f"p0si{i}") for i in range(2)]; s_o = [S.dsem(f"p0so{i}") for i in range(2)]
            xv = x_in.ap().rearrange("(n s p) d -> n p s d", s=4, p=128)
            XTv = XT.ap().rearrange("c p n -> p c n")
            for j in range(8):
                b = j % 2
                S.dma("sp", xin[b][:], xv[j], s_in[b], writes=[r_in[b]])
                for c in range(16):
                    pb = c % 8
                    for s4 in range(4):
                        last = s4 == 3
                        S.op("pe", lambda e, s4=s4, c=c, pb=pb: e.transpose(
                            out=PS[pb][:, s4 * 128:(s4 + 1) * 128], in_=xin[b][:, s4, c * 128:(c + 1) * 128],
                            identity=ident[:]), reads=[r_in[b]], writes=[PSR[pb]], sig=last)
                    ee = "act" if c % 2 == 0 else "dve"
                    if ee == "act":
                        S.op("act", lambda e, c=c, pb=pb: e.activation(out=xo[b][:, c, :], in_=PS[pb][:], func=AF.Copy),
                             reads=[PSR[pb]], writes=[r_o[b]])
                    else:
                        S.op("dve", lambda e, c=c, pb=pb: e.tensor_copy(out=xo[b][:, c, :], in_=PS[pb][:]),
                             reads=[PSR[pb]], writes=[r_o[b]])
                S.dma("act", XTv[:, :, j * 512:(j + 1) * 512], xo[b][:], s_o[b], reads=[r_o[b]])
            S.barrier()

    def phase_norm(gain_dram, l, tag, Hs, r_H):
        with ExitStack() as st:
            g = sb(st, tag + "g", [128, 16], F32)
            xs = [sb(st, f"{tag}x{i}", [128, 16, 256], F32) for i in range(2)]
            sq = sb(st, tag + "sq", [128, 16, 256], BF16)
            rs = [sb(st, f"{tag}rs{i}", [128, 256], F32) for i in range(2)]
            r_g = S.res(); r_x = [S.res() for _ in range(2)]; r_sq = S.res(); r_rs = [S.res() for _ in range(2)]
            s_g = S.dsem(); s_x = [S.dsem() for i in range(2)]
            S.dma("sp", g[:], gain_dram.ap()[l].rearrange("(c p) -> p c", p=128), s_g, writes=[r_g],
                  allow_slow_non_contiguous=True)
            XTv = XT.ap().rearrange("c p n -> p c n")
            for j in range(16):
                b = j % 2
                tsl = slice(j * 256, (j + 1) * 256)
                S.dma("sp", xs[b][:], XTv[:, :, tsl], s_x[b], writes=[r_x[b]])
                S.op("act", lambda e: e.activation(out=sq[:], in_=xs[b][:], func=AF.Square), reads=[r_x[b]], writes=[r_sq])
                pb = j % 8
                for c in range(16):
                    S.op("pe", lambda e, c=c: e.matmul(PS[pb][:, 0:256], onesb[:], sq[:, c, :], start=(c == 0), stop=(c == 15)),
                         reads=[r_sq], writes=[PSR[pb]], sig=(c == 15))
                S.op("act", lambda e: e.activation(out=rs[b][:], in_=PS[pb][:, 0:256], func=AF.Ln, bias=epsc[:], scale=1.0 / D),
                     reads=[PSR[pb]], writes=[r_rs[b]])
                S.op("act", lambda e: e.activation(out=rs[b][:], in_=rs[b][:], func=AF.Exp, scale=-0.5), reads=[r_rs[b]], writes=[r_rs[b]])
                for c in range(16):
                    S.op("dve", lambda e, c=c: e.scalar_tensor_tensor(out=Hs[:, c, tsl], in0=xs[b][:, c, :], scalar=g[:, c:c + 1],
                                                                 in1=rs[b][:], op0=ALU.mult, op1=ALU.mult),
                         reads=[r_x[b], r_rs[b], r_g], writes=[r_H])
            S.barrier()

    def phase_inproj(l):
        with ExitStack() as st:
            Hs = sb(st, "p2H", [128, 16, NT], BF16)
            r_H = S.res()
            phase_norm(norm1_g, l, f"n1{l}", Hs, r_H)
            wv = w_in.ap()[l].rearrange("(c p) n -> p c n", p=128)
            ws = WStream(st, "p2", 16, [(wv[:, :, m * 128:(m + 1) * 128], 16) for m in range(64)])
            qg = sb(st, "p2qg", [128, 2], F32)
            sqb = [sb(st, f"p2sq{i}", [128, 512], BF16) for i in range(2)]
            rsb = [sb(st, f"p2rs{i}", [128, 512], F32) for i in range(2)]
            NO = 4
            ob = [sb(st, f"p2o{i}", [128, 512], BF16) for i in range(NO)]
            vb = [sb(st, f"p2v{i}", [128, 512], BF16) for i in range(2)]
            r_qg = S.res(); r_sq = [S.res() for _ in range(2)]; r_rs = [S.res() for _ in range(2)]
            r_o = [S.res() for _ in range(NO)]; r_v = [S.res() for _ in range(2)]
            s_H = S.dsem("p2sH")
            s_qg = S.dsem("p2sqg"); s_o = [S.dsem(f"p2so{i}") for i in range(NO)]
            for hh in range(2):
                S.dma("sp", qg[hh * 64:(hh + 1) * 64, 0:1], dap(q_norm_g, l * 64, [[1, 64], [1, 1]]), s_qg, writes=[r_qg])
                S.dma("sp", qg[hh * 64:(hh + 1) * 64, 1:2], dap(k_norm_g, l * 64, [[1, 64], [1, 1]]), s_qg, writes=[r_qg])
            Vv = V.ap().rearrange("(n s p) f -> n p s f", s=4, p=128)
            U2v = U2.ap()
            cnt = 0
            oc = 0
            pend = [None]
            for m in range(64):
                wbm, r_wbm = ws.next()
                for j in range(8):
                    pb = cnt % 6
                    cnt += 1
                    for c in range(16):
                        S.op("pe", lambda e, c=c: e.matmul(PS[pb][:], wbm[:, c, :], Hs[:, c, j * 512:(j + 1) * 512],
                                                           start=(c == 0), stop=(c == 15)),
                             reads=[r_wbm, r_H], writes=[PSR[pb]], sig=(c == 15))
                    if pend[0] is not None:
                        pend[0]()
                        pend[0] = None
                    oi = oc % NO
                    oc += 1
                    tsl = slice(j * 512, (j + 1) * 512)
                    if m < 16:
                        kq = 0 if m < 8 else 1
                        hh = m % 8
                        qi = cnt % 2
                        pb2 = 6 + (cnt % 2)
                        S.op("act", lambda e: e.activation(out=sqb[qi][:], in_=PS[pb][:], func=AF.Square),
                             reads=[PSR[pb]], writes=[r_sq[qi]])
                        def post(pb=pb, pb2=pb2, qi=qi, oi=oi, kq=kq, hh=hh, tsl=tsl):
                            S.op("pe", lambda e: e.matmul(PS[pb2][:], blkb[:], sqb[qi][:], start=True, stop=True),
                                 reads=[r_sq[qi]], writes=[PSR[pb2]])
                            S.op("act", lambda e: e.activation(out=rsb[qi][:], in_=PS[pb2][:], func=AF.Ln, bias=epsc[:], scale=1.0 / 64),
                                 reads=[PSR[pb2]], writes=[r_rs[qi]])
                            S.op("act", lambda e: e.activation(out=rsb[qi][:], in_=rsb[qi][:], func=AF.Exp, scale=-0.5),
                                 reads=[r_rs[qi]], writes=[r_rs[qi]])
                            S.op("dve", lambda e: e.scalar_tensor_tensor(out=ob[oi][:], in0=PS[pb][:], scalar=qg[:, kq:kq + 1],
                                                                         in1=rsb[qi][:], op0=ALU.mult, op1=ALU.mult),
                                 reads=[PSR[pb], r_rs[qi], r_qg], writes=[r_o[oi]])
                            dst = (QT if kq == 0 else KT).ap()[hh][:, tsl]
                            S.dma("act", dst, ob[oi][:], s_o[oi], reads=[r_o[oi]])
                        pend[0] = post
                    elif m < 24:
                        hh = m - 16
                        vi = cnt % 2
                        pb2 = 6 + (cnt % 2)
                        S.op("act", lambda e: e.activation(out=vb[vi][:], in_=PS[pb][:], func=AF.Copy),
                             reads=[PSR[pb]], writes=[r_v[vi]])
                        def postv(pb2=pb2, vi=vi, oi=oi, hh=hh, j=j):
                            pv = PS[pb2].bitcast(BF16)
                            for s4 in range(4):
                                S.op("pe", lambda e, s4=s4: e.transpose(out=pv[:, s4 * 128:(s4 + 1) * 128],
                                                                        in_=vb[vi][:, s4 * 128:(s4 + 1) * 128], identity=identb[:]),
                                     reads=[r_v[vi]], writes=[PSR[pb2]], sig=(s4 == 3))
                            S.op("dve", lambda e: e.tensor_copy(out=ob[oi][:], in_=pv[:, 0:512]), reads=[PSR[pb2]], writes=[r_o[oi]])
                            S.dma("act", Vv[j][:, :, hh * 128:(hh + 1) * 128], ob[oi][:].rearrange("p (s f) -> p s f", s=4),
                                  s_o[oi], reads=[r_o[oi]])
                        pend[0] = postv
                    elif m < 32:
                        mu = m - 24
                        S.op("act", lambda e: e.activation(out=ob[oi][:].rearrange("p (s c) -> p c s", s=8),
                                                           in_=PS[pb][:].rearrange("p (c s) -> p c s", s=8), func=AF.Copy),
                             reads=[PSR[pb]], writes=[r_o[oi]])
                        for gl in range(8):
                            gidx = mu * 8 + gl
                            S.dma("sp", U2v[gidx].rearrange("s i n -> i s n")[:, :, j * 64:(j + 1) * 64],
                                  ob[oi][gl * 16:(gl + 1) * 16, :].rearrange("p (s c) -> p s c", s=8),
                                  s_o[oi], reads=[r_o[oi]])
                    else:
                        mg = m - 32
                        S.op("act", lambda e: e.activation(out=ob[oi][:], in_=PS[pb][:], func=AF.Sigmoid),
                             reads=[PSR[pb]], writes=[r_o[oi]])
                        dst = (GA if mg < 16 else GS).ap()[mg % 16][:, tsl]
                        S.dma("act", dst, ob[oi][:], s_o[oi], reads=[r_o[oi]])
            if pend[0] is not None:
                pend[0]()
            S.barrier()


    def phase_attn(l, tick=None):
        lam_init = 0.8 - 0.6 * math.exp(-0.3 * l)
        with ExitStack() as st:
            alib = sb(st, "atal", [128, 5, 512], F32)
            lamt = sb(st, "atlam", [128, 4, 64], F32)
            ltmp = sb(st, "atlt", [128, 64], F32)
            lsc = sb(st, "atls", [128, 8], F32)
            sg = sb(st, "atsg", [128, 1], F32)
            btab = sb(st, "atbt", [128, 8, 13], F32)
            btab2 = sb(st, "atbt2", [128, 8, 13], F32)
            pcol = sb(st, "atpc", [128, 8], F32)
            cpf = sb(st, "atcpf", [128, 513], F32)
            qT = [sb(st, f"atq{i}", [128, 2, SEQ], BF16) for i in range(2)]
            kT = [sb(st, f"atk{i}", [128, 2, SEQ], BF16) for i in range(2)]
            vt = [sb(st, f"atv{i}", [128, 16, 128], BF16) for i in range(2)]
            NB = 6
            sbias = [sb(st, f"atsb{i}", [128, 512], F32) for i in range(NB)]
            Eb = [sb(st, f"atE{i}", [128, 512], BF16) for i in range(NB)]
            fr2 = sb(st, "atfr2", [128, 512], F32); r_fr2 = S.res()
            o1s = sb(st, "ato1s", [128, 512], F32); r_o1s = S.res()
            o2s = sb(st, "ato2s", [128, 512], F32); r_o2s = S.res()
            fr = sb(st, "atfr", [128, 512], F32)
            ft1 = sb(st, "atft1", [128, 512], F32)
            ft2 = sb(st, "atft2", [128, 512], F32)
            fo = sb(st, "atfo", [128, 512], F32)
            fsq = sb(st, "atfsq", [128, 512], BF16)
            frs = sb(st, "atfrs", [128, 512], F32)
            fon = [sb(st, f"atfon{i}", [128, 512], BF16) for i in range(2)]
            r_c = S.res(); r_q = [S.res() for _ in range(2)]
            r_sb = [S.res() for _ in range(NB)]; r_E = [S.res() for _ in range(NB)]
            r_fr = S.res(); r_ft1 = S.res(); r_ft2 = S.res(); r_fo = S.res(); r_fsq = S.res(); r_frs = S.res()
            r_fon = [S.res() for _ in range(2)]
            s_c = S.dsem("atsc"); s_q = [S.dsem(f"atsq{i}") for i in range(2)]
            s_on = [S.dsem(f"atson{i}") for i in range(2)]
            S.dma("sp", alib[:], c_alibi.ap(), s_c, writes=[r_c])
            for idx, t in enumerate([lam_q1, lam_k1, lam_q2, lam_k2]):
                S.dma("sp", lamt[:, idx, :], dap(t, l * 64, [[0, 128], [1, 64]]), s_c, writes=[r_c])
            S.dma("sp", sg[:], dap(subln_g, l * 128, [[1, 128], [1, 1]]), s_c, writes=[r_c])
            for k2 in range(2):
                S.op("dve", lambda e: e.tensor_tensor(out=ltmp[:], in0=lamt[:, 2 * k2, :], in1=lamt[:, 2 * k2 + 1, :], op=ALU.mult),
                     reads=[r_c], writes=[r_c])
                S.op("dve", lambda e: e.tensor_reduce(out=lsc[:, k2:k2 + 1], in_=ltmp[:], axis=AX.X, op=ALU.add),
                     reads=[r_c], writes=[r_c])
            S.op("act", lambda e: e.activation(out=lsc[:, 2:4], in_=lsc[:, 0:2], func=AF.Exp), reads=[r_c], writes=[r_c])
            S.op("dve", lambda e: e.tensor_tensor(out=lsc[:, 4:5], in0=lsc[:, 3:4], in1=lsc[:, 2:3], op=ALU.subtract),
                 reads=[r_c], writes=[r_c])
            S.op("dve", lambda e: e.tensor_scalar(out=lsc[:, 5:6], in0=lsc[:, 4:5], scalar1=-lam_init, scalar2=None, op0=ALU.add),
                 reads=[r_c], writes=[r_c])
            S.op("dve", lambda e: e.tensor_scalar(out=sg[:], in0=sg[:], scalar1=(1.0 - lam_init), scalar2=None, op0=ALU.mult),
                 reads=[r_c], writes=[r_c])
            for hh in range(8):
                for rel in range(13):
                    S.op("pool", lambda e: e.memset(btab[:, hh, rel:rel + 1], -(2.0 ** -(hh + 1)) * 128.0 * rel), writes=[r_c])
            S.dma("sp", cpf[:], c_pf.ap(), s_c, writes=[r_c])
            for hh in range(8):
                sl_ = 2.0 ** -(hh + 1)
                S.op("dve", lambda e: e.tensor_scalar(out=pcol[:, hh:hh + 1], in0=cpf[:, 0:1], scalar1=sl_, scalar2=None, op0=ALU.mult),
                     reads=[r_c], writes=[r_c])
                S.op("dve", lambda e: e.tensor_scalar(out=btab2[:, hh, :], in0=btab[:, hh, :], scalar1=pcol[:, hh:hh + 1], scalar2=None,
                                                      op0=ALU.add), reads=[r_c], writes=[r_c])
            nlam = lsc[:, 5:6]
            for qi in range(2):
                S.op("dve", lambda e: e.memset(kT[qi][64:66, :, :], 1.0), writes=[r_q[qi]])
            it = 0
            bh = 0
            pending = [None]
            for b in range(2):
                for hh in range(8):
                    qi = bh % 2
                    bh += 1
                    tok = slice(b * SEQ, (b + 1) * SEQ)
                    for c2 in range(2):
                        S.dma("sp", qT[qi][0:64, c2, :], QT.ap()[hh][c2 * 64:(c2 + 1) * 64, tok], s_q[qi], writes=[r_q[qi]])
                        S.dma("sp", kT[qi][0:64, c2, :], KT.ap()[hh][c2 * 64:(c2 + 1) * 64, tok], s_q[qi], writes=[r_q[qi]])
                        S.dma("sp", qT[qi][64:66, c2, :], c_aug.ap()[hh], s_q[qi], writes=[r_q[qi]])
                    S.dma("sp", vt[qi][:], V.ap()[tok, hh * 128:(hh + 1) * 128].rearrange("(t p) f -> p t f", p=128),
                          s_q[qi], writes=[r_q[qi]])
                    slope = 2.0 ** -(hh + 1)
                    for j in range(4):
                        nk = 4 * (j + 1)
                        slots = {}

                        def keep(i):
                            rel = 4 * j - i
                            return rel <= 0 or slope * (128 * rel - 127) <= 40.0
                        tiles = [i for i in range(nk) if keep(i)]
                        nkk = len(tiles)

                        def emit_S(n):
                            nonlocal it
                            i = tiles[n]
                            a = 2 * (n % 2)
                            ksl = slice(i * 128, (i + 1) * 128)
                            rel = 4 * j - i
                            c0 = 0 if rel >= 1 else -rel * 128
                            pidx = 0 if rel >= 1 else 1 - rel
                            cs_ = slice(c0, 512)
                            qs_ = slice(j * 512 + c0, (j + 1) * 512)
                            kk = 66 if rel >= 1 else 64
                            xs2 = []
                            for c2 in range(2):
                                S.op("pe", lambda e: e.matmul(PS[a + c2][:, cs_], kT[qi][0:kk, c2, ksl], qT[qi][0:kk, c2, qs_], start=True, stop=True),
                                     reads=[r_q[qi]], writes=[PSR[a + c2]])
                            for c2 in range(2):
                                x = it % NB
                                it += 1
                                xs2.append(x)
                                if rel >= 1:
                                    S.op("act", lambda e: e.activation(out=Eb[x][:], in_=PS[a + c2][:], func=AF.Exp,
                                                                       bias=btab2[:, hh, rel:rel + 1], scale=0.125),
                                         reads=[PSR[a + c2], r_c], writes=[r_E[x]])
                                else:
                                    S.op("dve", lambda e: e.scalar_tensor_tensor(out=sbias[x][:, cs_], in0=alib[:, pidx, cs_], scalar=-8.0 * slope,
                                                                                 in1=PS[a + c2][:, cs_], op0=ALU.mult, op1=ALU.add),
                                         reads=[r_c, PSR[a + c2]], writes=[r_sb[x]])
                                    S.op("act", lambda e: e.activation(out=Eb[x][:, cs_], in_=sbias[x][:, cs_], func=AF.Exp, scale=0.125),
                                         reads=[r_sb[x]], writes=[r_E[x]])
                            if tick is not None:
                                tick()
                            slots[n] = (xs2, cs_)

                        def emit_OZ(n):
                            i = tiles[n]
                            (x1, x2), cs_ = slots[n]
                            st_, sp_ = (n == 0), (n == nkk - 1)
                            S.op("pe", lambda e: e.matmul(PS[4][:, cs_], vt[qi][:, i, :], Eb[x1][:, cs_], start=st_, stop=sp_),
                                 reads=[r_q[qi], r_E[x1]], writes=[PSR[4]])
                            S.op("pe", lambda e: e.matmul(PS[5][:, cs_], onesb[:], Eb[x1][:, cs_], start=st_, stop=sp_),
                                 reads=[r_E[x1]], writes=[PSR[5]])
                            S.op("pe", lambda e: e.matmul(PS[6][:, cs_], vt[qi][:, i, :], Eb[x2][:, cs_], start=st_, stop=sp_),
                                 reads=[r_q[qi], r_E[x2]], writes=[PSR[6]])
                            S.op("pe", lambda e: e.matmul(PS[7][:, cs_], onesb[:], Eb[x2][:, cs_], start=st_, stop=sp_),
                                 reads=[r_E[x2]], writes=[PSR[7]])

                        for n in range(nkk + 2):
                            if n < nkk:
                                emit_S(n)
                            if n == 1 and pending[0] is not None:
                                pending[0]()
                                pending[0] = None
                            if n >= 2:
                                emit_OZ(n - 2)

                        S.op("act", lambda e: e.activation(out=fr[:], in_=PS[5][:], func=AF.Ln), reads=[PSR[5]], writes=[r_fr])
                        S.op("act", lambda e: e.activation(out=fr2[:], in_=PS[7][:], func=AF.Ln), reads=[PSR[7]], writes=[r_fr2])
                        S.op("dve", lambda e: e.tensor_copy(out=o1s[:], in_=PS[4][:]), reads=[PSR[4]], writes=[r_o1s])
                        S.op("dve", lambda e: e.tensor_copy(out=o2s[:], in_=PS[6][:]), reads=[PSR[6]], writes=[r_o2s])

                        def finalize(b=b, hh=hh, j=j, oi=(bh * 4 + j) % 2):
                            S.op("act", lambda e: e.activation(out=fr[:], in_=fr[:], func=AF.Exp, scale=-1.0), reads=[r_fr], writes=[r_fr])
                            S.op("dve", lambda e: e.tensor_tensor(out=ft1[:], in0=o1s[:], in1=fr[:], op=ALU.mult),
                                 reads=[r_o1s, r_fr], writes=[r_ft1])
                            S.op("act", lambda e: e.activation(out=fr2[:], in_=fr2[:], func=AF.Exp, scale=-1.0), reads=[r_fr2], writes=[r_fr2])
                            S.op("dve", lambda e: e.tensor_tensor(out=ft2[:], in0=o2s[:], in1=fr2[:], op=ALU.mult),
                                 reads=[r_o2s, r_fr2], writes=[r_ft2])
                            S.op("dve", lambda e: e.scalar_tensor_tensor(out=fo[:], in0=ft2[:], scalar=nlam, in1=ft1[:],
                                                                         op0=ALU.mult, op1=ALU.add),
                                 reads=[r_ft1, r_ft2, r_c], writes=[r_fo])
                            S.op("dve", lambda e: e.tensor_tensor(out=fsq[:], in0=fo[:], in1=fo[:], op=ALU.mult), reads=[r_fo], writes=[r_fsq])
                            S.op("pe", lambda e: e.matmul(PS[5][:], onesb[:], fsq[:], start=True, stop=True), reads=[r_fsq], writes=[PSR[5]])
                            S.op("act", lambda e: e.activation(out=frs[:], in_=PS[5][:], func=AF.Ln, bias=epsc[:], scale=1.0 / 128),
                                 reads=[PSR[5]], writes=[r_frs])
                            S.op("act", lambda e: e.activation(out=frs[:], in_=frs[:], func=AF.Exp, scale=-0.5), reads=[r_frs], writes=[r_frs])
                            S.op("dve", lambda e: e.scalar_tensor_tensor(out=fon[oi][:], in0=fo[:], scalar=sg[:, 0:1], in1=frs[:],
                                                                         op0=ALU.mult, op1=ALU.mult),
                                 reads=[r_fo, r_frs, r_c], writes=[r_fon[oi]])
                            S.dma("act", OT.ap()[hh][:, b * SEQ + j * 512:b * SEQ + (j + 1) * 512], fon[oi][:], s_on[oi], reads=[r_fon[oi]])
                        pending[0] = finalize
            pending[0]()
            S.barrier()

    def phase_ssm(l, attn_fn):
        def bc(t, off, rowlen, n1, n2):
            return bass.AP(t, off, [[rowlen, 128], [1, n1], [0, n2]])
        with ExitStack() as st:
            W1b = sb(st, "ssW1", [128, 64, 128], BF16)
            W4re = sb(st, "ssW4r", [128, 32, 128], BF16); W4im = sb(st, "ssW4i", [128, 32, 128], BF16)
            AR2 = sb(st, "ssAR2", [128, 2, 32], F32); NAI = sb(st, "ssNAI", [128, 2, 32], F32)
            W2re = sb(st, "ssW2r", [128, 64, 64], BF16); W2im = sb(st, "ssW2i", [128, 64, 64], BF16)
            r_W = S.res()
            r_SH = [S.res() for _ in range(2)]; r_H = [S.res() for _ in range(2)]
            with ExitStack() as st2:
                An = sb(st2, "ssAn", [32, 2, 128], F32)
                Are = sb(st2, "ssAre", [128, 32], F32); Aim = sb(st2, "ssAim", [128, 32], F32)
                Ldt = sb(st2, "ssLdt", [128, 32], F32)
                dre = sb(st2, "ssdre", [128, 32], F32); dim = sb(st2, "ssdim", [128, 32], F32)
                mag = sb(st2, "ssmag", [128, 32], F32)
                cs = sb(st2, "sscs", [128, 2, 32], F32)
                t1 = sb(st2, "sst1", [128, 32], F32); t2 = sb(st2, "sst2", [128, 32], F32)
                PW = sb(st2, "ssPW", [128, 9, 2, 32], F32)
                FF = sb(st2, "ssFF", [128, 2, 32], F32)
                FP = sb(st2, "ssFP", [128, 8, 2, 32], F32)
                hpi = sb(st2, "sshpi", [128, 1], F32)
                Bre = sb(st2, "ssBre", [128, 32, 16], F32); Bim = sb(st2, "ssBim", [128, 32, 16], F32)
                T1 = sb(st2, "ssT1", [128, 32, 16], F32); T2 = sb(st2, "ssT2", [128, 32, 16], F32)
                ABr = [sb(st2, f"ssAB{i}", [128, 32, 240], F32) for i in range(2)]
                ABrb = [sb(st2, f"ssABb{i}", [128, 32, 240], BF16) for i in range(2)]
                CTb = [sb(st2, f"ssCTb{i}", [128, 32, 16], BF16) for i in range(2)]
                Cn = sb(st2, "ssCn", [128, 2, 64], F32)
                CT = [sb(st2, f"ssCT{i}", [128, 32, 16], F32) for i in range(3)]
                dgi = sb(st2, "ssdgi", [64, 8, 16], F32)
                Drep = sb(st2, "ssDrep", [128, 64], F32)
                r_p = S.res(); s_p = S.dsem(); r_cn = S.res(); s_cn = S.dsem()
                D_ = lambda fn, rd=(), wr=(): S.op("dve", fn, reads=[r_p] + list(rd), writes=[r_p] + list(wr))
                A_ = lambda fn, rd=(), wr=(): S.op("act", fn, reads=[r_p] + list(rd), writes=[r_p] + list(wr))
                TT = lambda o, a, b_, op: D_(lambda e: e.tensor_tensor(out=o, in0=a, in1=b_, op=op))
                for k2, src in enumerate([a_re, a_im]):
                    for gh in range(2):
                        S.dma("sp", An[:, k2, gh * 64:(gh + 1) * 64], src.ap()[l][gh * 32:(gh + 1) * 32, :], s_p, writes=[r_p])
                for gh in range(2):
                    S.dma("sp", Ldt[gh * 64:(gh + 1) * 64, :], dap(log_dt, l * 64 + gh * 32, [[0, 64], [1, 32]]), s_p, writes=[r_p])
                    S.dma("sp", Bre[gh * 64:(gh + 1) * 64, :, :], b_re.ap()[l][gh * 32:(gh + 1) * 32].rearrange("g p i -> p g i"), s_p, writes=[r_p])
                    S.dma("sp", Bim[gh * 64:(gh + 1) * 64, :, :], b_im.ap()[l][gh * 32:(gh + 1) * 32].rearrange("g p i -> p g i"), s_p, writes=[r_p])
                S.dma("sp", dgi[:], dap(ssm_d, l * 1024, [[16, 64], [0, 8], [1, 16]]), s_p, writes=[r_p])
                D_(lambda e: e.memset(hpi[:], math.pi / 2))
                for k2, dst in enumerate([Are, Aim]):
                    S.op("pe", lambda e: e.transpose(out=PS[k2][:, 0:32], in_=An[:, k2, :], identity=ident[0:32, 0:32]),
                         reads=[r_p], writes=[PSR[k2]])
                    D_(lambda e: e.tensor_copy(out=dst[:], in_=PS[k2][:, 0:32]), rd=[PSR[k2]])
                A_(lambda e: e.activation(out=Ldt[:], in_=Ldt[:], func=AF.Exp))
                TT(dre[:], Ldt[:], Are[:], ALU.mult)
                TT(dim[:], Ldt[:], Aim[:], ALU.mult)
                A_(lambda e: e.activation(out=mag[:], in_=dre[:], func=AF.Exp))
                A_(lambda e: e.activation(out=cs[:, 1, :], in_=dim[:], func=AF.Sin, scale=1.0 / 16))
                A_(lambda e: e.activation(out=cs[:, 0, :], in_=dim[:], func=AF.Sin, bias=hpi[:], scale=-1.0 / 16))
                for _ in range(4):
                    TT(t1[:], cs[:, 0, :], cs[:, 0, :], ALU.mult)
                    TT(t2[:], cs[:, 1, :], cs[:, 1, :], ALU.mult)
                    D_(lambda e: e.scalar_tensor_tensor(out=cs[:, 1, :], in0=cs[:, 0, :], scalar=2.0, in1=cs[:, 1, :],
                                                        op0=ALU.mult, op1=ALU.mult))
                    TT(cs[:, 0, :], t1[:], t2[:], ALU.subtract)
                D_(lambda e: e.memset(PW[:, 0, 0, :], 1.0))
                D_(lambda e: e.memset(PW[:, 0, 1, :], 0.0))
                TT(PW[:, 1, 0, :], mag[:], cs[:, 0, :], ALU.mult)
                TT(PW[:, 1, 1, :], mag[:], cs[:, 1, :], ALU.mult)
                ar, ai = PW[:, 1, 0, :], PW[:, 1, 1, :]

                def cmul(o_re, o_im, x_re, x_im, y_re, y_im):
                    TT(t1[:], x_re, y_re, ALU.mult); TT(t2[:], x_im, y_im, ALU.mult)
                    TT(o_re, t1[:], t2[:], ALU.subtract)
                    TT(t1[:], x_re, y_im, ALU.mult); TT(t2[:], x_im, y_re, ALU.mult)
                    TT(o_im, t1[:], t2[:], ALU.add)
                for k in range(2, 9):
                    cmul(PW[:, k, 0, :], PW[:, k, 1, :], PW[:, k - 1, 0, :], PW[:, k - 1, 1, :], ar, ai)
                TT(t1[:], Are[:], Are[:], ALU.mult); TT(t2[:], Aim[:], Aim[:], ALU.mult)
                TT(mag[:], t1[:], t2[:], ALU.add)
                D_(lambda e: e.reciprocal(out=mag[:], in_=mag[:]))
                D_(lambda e: e.tensor_scalar(out=dre[:], in0=ar, scalar1=-1.0, scalar2=None, op0=ALU.add))
                TT(t1[:], dre[:], Are[:], ALU.mult); TT(t2[:], ai, Aim[:], ALU.mult)
                TT(t1[:], t1[:], t2[:], ALU.add); TT(FF[:, 0, :], t1[:], mag[:], ALU.mult)
                TT(t1[:], ai, Are[:], ALU.mult); TT(t2[:], dre[:], Aim[:], ALU.mult)
                TT(t1[:], t1[:], t2[:], ALU.subtract); TT(FF[:, 1, :], t1[:], mag[:], ALU.mult)
                for k in range(8):
                    cmul(FP[:, k, 0, :], FP[:, k, 1, :], PW[:, k, 0, :], PW[:, k, 1, :], FF[:, 0, :], FF[:, 1, :])
                D_(lambda e: e.memset(ABr[0][:], 0.0)); D_(lambda e: e.memset(ABr[1][:], 0.0))
                for tau in range(8):
                    bk = 7 - tau
                    fr_ = bc(FP, tau * 64, 512, 32, 16); fi_ = bc(FP, tau * 64 + 32, 512, 32, 16)
                    osl = slice(bk * 16, (bk + 1) * 16)
                    TT(T1[:], Bre[:], fr_, ALU.mult); TT(T2[:], Bim[:], fi_, ALU.mult)
                    TT(ABr[0][:, :, osl], T1[:], T2[:], ALU.subtract)
                    TT(T1[:], Bim[:], fr_, ALU.mult); TT(T2[:], Bre[:], fi_, ALU.mult)
                    TT(ABr[1][:, :, osl], T1[:], T2[:], ALU.add)
                for k2, src in enumerate([c_re, c_im]):
                    for blk in range(4):
                        for gh in range(2):
                            g0 = gh * 32 + blk * 8
                            S.dma("sp", Cn[:, gh, :], src.ap()[l][g0:g0 + 8].rearrange("g o p -> (g o) p"), s_cn, writes=[r_cn])
                        pb = 2 + (k2 * 4 + blk) % 2
                        S.op("pe", lambda e: e.transpose(out=PS[pb][:, 0:128], in_=Cn[:].rearrange("p a b -> p (a b)"), identity=ident[:]),
                             reads=[r_cn], writes=[PSR[pb]])
                        D_(lambda e: e.tensor_copy(out=CT[k2][:, blk * 8:(blk + 1) * 8, :].rearrange("p a b -> p (a b)"), in_=PS[pb][:, 0:128]),
                           rd=[PSR[pb]])
                D_(lambda e: e.tensor_scalar(out=CT[2][:], in0=CT[1][:], scalar1=-1.0, scalar2=None, op0=ALU.mult))
                D_(lambda e: e.tensor_copy(out=ABrb[0][:], in_=ABr[0][:]))
                A_(lambda e: e.activation(out=ABrb[1][:], in_=ABr[1][:], func=AF.Copy))
                D_(lambda e: e.tensor_copy(out=CTb[0][:], in_=CT[0][:]))
                D_(lambda e: e.tensor_copy(out=CTb[1][:], in_=CT[2][:]))
                S.op("pe", lambda e: e.matmul(PS[4][:, 0:64], dgi[:].rearrange("p a b -> p (a b)"), ident[0:64, 0:64], start=True, stop=True),
                     reads=[r_p], writes=[PSR[4]])
                D_(lambda e: e.tensor_copy(out=Drep[:], in_=PS[4][:, 0:64]), rd=[PSR[4]])
                for g4 in range(16):
                    pb = 5 + g4 % 2
                    for q in range(4):
                        g = g4 * 4 + q
                        gh, gl = g // 32, g % 32
                        ps_ = slice(gh * 64, (gh + 1) * 64)
                        for t in range(8):
                            wsl = slice((7 - t) * 16, (7 - t) * 16 + 128)
                            osl = slice(q * 128 + t * 16, q * 128 + (t + 1) * 16)
                            S.op("pe", lambda e: e.matmul(PS[pb][:, osl], ABrb[0][ps_, gl, wsl], CTb[0][ps_, gl, :], start=True, stop=False),
                                 reads=[r_p], writes=[PSR[pb]], sig=False)
                            S.op("pe", lambda e: e.matmul(PS[pb][:, osl], ABrb[1][ps_, gl, wsl], CTb[1][ps_, gl, :], start=False, stop=True),
                                 reads=[r_p], writes=[PSR[pb]], sig=(t == 7 and q == 3))
                    for q in range(4):
                        g = g4 * 4 + q
                        S.op("dve", lambda e: e.scalar_tensor_tensor(out=W1b[:, g, :], in0=ident[:], scalar=Drep[:, g:g + 1],
                                                                     in1=PS[pb][:, q * 128:(q + 1) * 128], op0=ALU.mult, op1=ALU.add),
                             reads=[r_p, PSR[pb]], writes=[r_W])
                for k2, dst in enumerate([W2re, W2im]):
                    for g8 in range(8):
                        pb = (k2 * 8 + g8) % 2
                        for q in range(8):
                            g = g8 * 8 + q
                            gh, gl = g // 32, g % 32
                            ps_ = slice(gh * 64, (gh + 1) * 64)
                            S.op("pe", lambda e: e.transpose(out=PS[pb][:, q * 64:(q + 1) * 64], in_=ABr[k2][ps_, gl, 0:128],
                                                             identity=ident[ps_, ps_]), reads=[r_p], writes=[PSR[pb]], sig=(q == 7))
                        S.op("act", lambda e: e.activation(out=dst[:, g8 * 8:(g8 + 1) * 8, :].rearrange("p a b -> p (a b)"), in_=PS[pb][:],
                                                           func=AF.Copy), reads=[PSR[pb]], writes=[r_W])
                for t in range(8):
                    k = t + 1
                    pr_ = bc(PW, k * 64, 576, 32, 16); pi_ = bc(PW, k * 64 + 32, 576, 32, 16)
                    osl = slice(t * 16, (t + 1) * 16)
                    TT(T1[:], CT[0][:], pr_, ALU.mult); TT(T2[:], CT[1][:], pi_, ALU.mult)
                    D_(lambda e: e.tensor_tensor(out=W4re[:, :, osl], in0=T1[:], in1=T2[:], op=ALU.subtract), wr=[r_W])
                    TT(T1[:], CT[2][:], pr_, ALU.mult); TT(T2[:], CT[0][:], pi_, ALU.mult)
                    D_(lambda e: e.tensor_tensor(out=W4im[:, :, osl], in0=T1[:], in1=T2[:], op=ALU.subtract), wr=[r_W])
                for r2 in range(2):
                    D_(lambda e: e.tensor_copy(out=AR2[:, r2, :], in_=PW[:, 8, 0, :]), wr=[r_W])
                D_(lambda e: e.tensor_copy(out=NAI[:, 1, :], in_=PW[:, 8, 1, :]), wr=[r_W])
                D_(lambda e: e.tensor_scalar(out=NAI[:, 0, :], in0=PW[:, 8, 1, :], scalar1=-1.0, scalar2=None, op0=ALU.mult), wr=[r_W])
                S.barrier()
            SH = sb(st, "ssSH", [128, 2, 32, 2, 257], BF16)
            Hst = [sb(st, f"ssH{b}", [128, 2, 32, 2], F32) for b in range(2)]
            Pt = sb(st, "ssP", [128, 2, 32, 2], F32)
            Qt = sb(st, "ssQ", [128, 2, 32, 2], F32)
            r_P = S.res(); r_Q = S.res(); r_SHs = S.res()
            with ExitStack() as st3:
                ub = [sb(st3, f"ssub{i}", [128, 512], BF16) for i in range(4)]
                r_ub = [S.res() for _ in range(4)]; s_ub = [S.dsem() for _ in range(4)]
                for b in range(2):
                    S.op("dve", lambda e: e.memset(SH[:, :, :, b, 0:1], 0.0), writes=[r_SH[b]])
                    S.op("dve", lambda e: e.memset(Hst[b][:], 0.0), writes=[r_H[b]])
                uc = 0
                for gl in range(32):
                    a = 2 * (gl % 2)
                    for gh in range(2):
                        g = gh * 32 + gl
                        ui = uc % 4; uc += 1
                        S.dma("sp", ub[ui][:], U2.ap()[g].rearrange("s i n -> (s i) n"), s_ub[ui], writes=[r_ub[ui]])
                        ps_ = slice(gh * 64, (gh + 1) * 64)
                        S.op("pe", lambda e: e.matmul(PS[a][ps_, :], W2re[:, g, :], ub[ui][:], start=True, stop=True),
                             reads=[r_W, r_ub[ui]], writes=[PSR[a]])
                        S.op("pe", lambda e: e.matmul(PS[a + 1][ps_, :], W2im[:, g, :], ub[ui][:], start=True, stop=True),
                             reads=[r_W, r_ub[ui]], writes=[PSR[a + 1]])
                    S.op("act", lambda e: e.activation(out=SH[:, 0, gl, :, 1:257], in_=PS[a][:].rearrange("p (b c) -> p b c", b=2), func=AF.Copy),
                         reads=[PSR[a]], writes=r_SH)
                    S.op("dve", lambda e: e.tensor_copy(out=SH[:, 1, gl, :, 1:257], in_=PS[a + 1][:].rearrange("p (b c) -> p b c", b=2)),
                         reads=[PSR[a + 1]], writes=r_SH)
                S.barrier()
            state = {"c": 0}

            ARb = bass.AP(AR2.tensor if hasattr(AR2, "tensor") else AR2, 0, [[64, 128], [1, 64], [0, 2]])
            NAb = [bass.AP(NAI.tensor if hasattr(NAI, "tensor") else NAI, r2 * 32, [[64, 128], [1, 32], [0, 2]]) for r2 in range(2)]

            def tick():
                c = state["c"]
                if c >= 256:
                    return
                state["c"] = c + 1
                Hc, Hn = Hst[c % 2], Hst[(c + 1) % 2]
                rHc, rHn = r_H[c % 2], r_H[(c + 1) % 2]
                S.op("pool", lambda e: e.tensor_tensor(out=Pt[:].rearrange("p r g b -> p (r g) b"), in0=ARb,
                                                      in1=Hc[:].rearrange("p r g b -> p (r g) b"), op=ALU.mult),
                     reads=[rHc, r_W], writes=[r_P])
                S.op("pool", lambda e: e.tensor_tensor(out=Qt[:, 0, :, :], in0=NAb[0], in1=Hc[:, 1, :, :], op=ALU.mult),
                     reads=[rHc, r_W], writes=[r_Q])
                S.op("pool", lambda e: e.tensor_tensor(out=Qt[:, 1, :, :], in0=NAb[1], in1=Hc[:, 0, :, :], op=ALU.mult),
                     reads=[rHc, r_W], writes=[r_Q])
                S.op("pool", lambda e: e.tensor_tensor(out=Pt[:], in0=Pt[:], in1=Qt[:], op=ALU.add), reads=[r_P, r_Q], writes=[r_P])
                S.op("pool", lambda e: e.tensor_tensor(out=Hn[:], in0=Pt[:], in1=SH[:, :, :, :, c + 1], op=ALU.add),
                     reads=[r_P] + r_SH, writes=[rHn])
                S.op("pool", lambda e: e.tensor_copy(out=SH[:, :, :, :, c + 1], in_=Hn[:]), reads=[rHn], writes=r_SH)

            attn_fn(l, tick)
            while state["c"] < 256:
                tick()
            with ExitStack() as st4:
                ub = [sb(st4, f"ssub{i}", [128, 512], BF16) for i in range(4)]
                r_ub = [S.res() for _ in range(4)]; s_ub = [S.dsem() for _ in range(4)]
                yo = [sb(st4, f"ssyo{i}", [128, 512], BF16) for i in range(2)]
                r_yo = [S.res() for _ in range(2)]; s_yo = [S.dsem() for _ in range(2)]
                yc = 0; uc = 0
                for g in range(64):
                    gh, gl = g // 32, g % 32
                    ps_ = slice(gh * 64, (gh + 1) * 64)
                    ui = uc % 4; uc += 1
                    S.dma("sp", ub[ui][:], U2.ap()[g].rearrange("s i n -> (s i) n"), s_ub[ui], writes=[r_ub[ui]])
                    pb = 4 + g % 4
                    S.op("pe", lambda e: e.matmul(PS[pb][:], W1b[:, g, :], ub[ui][:], start=True, stop=False),
                         reads=[r_W, r_ub[ui]], writes=[PSR[pb]], sig=False)
                    S.op("pe", lambda e: e.matmul(PS[pb][:], W4re[ps_, gl, :], SH[ps_, 0, gl, :, 0:256], start=False, stop=False),
                         reads=[r_W] + r_SH, writes=[PSR[pb]], sig=False)
                    S.op("pe", lambda e: e.matmul(PS[pb][:], W4im[ps_, gl, :], SH[ps_, 1, gl, :, 0:256], start=False, stop=True),
                         reads=[r_W] + r_SH, writes=[PSR[pb]])
                    yi = yc % 2; yc += 1
                    if g % 2 == 0:
                        S.op("act", lambda e: e.activation(out=yo[yi][:], in_=PS[pb][:], func=AF.Copy), reads=[PSR[pb]], writes=[r_yo[yi]])
                    else:
                        S.op("dve", lambda e: e.tensor_copy(out=yo[yi][:], in_=PS[pb][:]), reads=[PSR[pb]], writes=[r_yo[yi]])
                    S.dma("act", Y2.ap()[g].rearrange("t o n -> (t o) n"), yo[yi][:], s_yo[yi], reads=[r_yo[yi]])
                S.barrier()

    class WStream:
        def __init__(self, st, tag, nkmax, items, nbf=2, dist=1):
            self.items = items
            self.nbf = nbf
            self.dist = dist
            self.wst = [sb(st, f"{tag}ws{i}", [128, nkmax, 128], F32) for i in range(2)]
            self.wbf = [sb(st, f"{tag}wb{i}", [128, nkmax, 128], BF16) for i in range(nbf)]
            self.r_ws = [S.res() for _ in range(2)]; self.r_wb = [S.res() for _ in range(nbf)]
            self.s_ws = [S.dsem() for i in range(2)]
            self.issued = 0
            self.idx = 0

        def _issue(self, k):
            view, nk = self.items[k]
            i = k % 2
            j = k % self.nbf
            S.dma("sp", self.wst[i][:, 0:nk, :], view, self.s_ws[i], writes=[self.r_ws[i]])
            S.op("pool", lambda e: e.tensor_copy(out=self.wbf[j][:, 0:nk, :], in_=self.wst[i][:, 0:nk, :]),
                 reads=[self.r_ws[i]], writes=[self.r_wb[j]])

        def next(self):
            while self.issued < min(len(self.items), self.idx + self.dist + 1):
                self._issue(self.issued)
                self.issued += 1
            j = self.idx % self.nbf
            self.idx += 1
            return self.wbf[j], self.r_wb[j]

    def mm_acc(pb, wb, r_wb, X, r_X, nk, tsl, first=True, last=True):
        if isinstance(r_X, list):
            r_X = r_X[tsl.start // 512]
        for c in range(nk):
            S.op("pe", lambda e: e.matmul(PS[pb][:], wb[:, c, :], X[:, c, tsl], start=(first and c == 0), stop=(last and c == nk - 1)),
                 reads=[r_wb, r_X], writes=[PSR[pb]], sig=(c == nk - 1))

    def phase_mix(l):
        with ExitStack() as st:
            zT = sb(st, "mxz", [128, 8, NT], BF16)
            r_z = S.res()
            glb = sb(st, "mxgb", [128, 8], F32); r_gb = S.res(); s_gb = S.dsem("mxsgb")
            S.dma("sp", glb[:], glu_b.ap()[l].rearrange("(c p) -> p c", p=128), s_gb, writes=[r_gb], allow_slow_non_contiguous=True)
            with ExitStack() as st2:
                yT = sb(st2, "mxy", [128, 8, NT], BF16); r_y = S.res()
                yl = [sb(st2, f"mxyl{i}", [128, 8, 512], BF16) for i in range(2)]
                r_yl = [S.res() for _ in range(2)]; s_yl = [S.dsem(f"mxsyl{i}") for i in range(2)]
                sgt = [sb(st2, f"mxsg{i}", [128, 512], BF16) for i in range(2)]; r_sg = [S.res() for _ in range(2)]
                wgv = glu_w.ap()[l].rearrange("(c p) n -> p c n", p=128)
                ws = WStream(st2, "mxa", 8, [(wgv[:, :, m * 128:(m + 1) * 128], 8) for m in range(8)])
                for mu in range(8):
                    i = mu % 2
                    for gl in range(8):
                        S.dma("sp", yl[i][gl * 16:(gl + 1) * 16, :, :], Y2.ap()[mu * 8 + gl].rearrange("t o n -> o t n"),
                              s_yl[i], writes=[r_yl[i]])
                    S.op("act", lambda e: e.activation(out=yT[:, mu, :].rearrange("p (n t) -> p t n", t=8), in_=yl[i][:],
                                                       func=AF.Gelu_apprx_tanh), reads=[r_yl[i]], writes=[r_y])
                cnt = 0
                for m in range(8):
                    wb, r_wb = ws.next()
                    for j in range(8):
                        pb = cnt % 4; k = cnt % 2; cnt += 1
                        tsl = slice(j * 512, (j + 1) * 512)
                        mm_acc(pb, wb, r_wb, yT, r_y, 8, tsl)
                        S.op("act", lambda e: e.activation(out=sgt[k][:], in_=PS[pb][:], func=AF.Sigmoid, bias=glb[:, m:m + 1], scale=1.0),
                             reads=[PSR[pb], r_gb], writes=[r_sg[k]])
                        S.op("dve", lambda e: e.tensor_tensor(out=zT[:, m, tsl], in0=yT[:, m, tsl], in1=sgt[k][:], op=ALU.mult),
                             reads=[r_sg[k], r_y], writes=[r_z])
                S.barrier()
            with ExitStack() as st2:
                oT = sb(st2, "mxo", [128, 8, NT], BF16); r_o = [S.res() for _ in range(8)]; s_o = [S.dsem() for _ in range(8)]
                tmp = sb(st2, "mxt", [128, NT], F32); r_t = S.res()
                gt = [sb(st2, f"mxg{i}", [128, 512], BF16) for i in range(4)]; r_g = [S.res() for _ in range(4)]
                s_g = [S.dsem(f"mxsg{i}") for i in range(4)]
                mo = [sb(st2, f"mxmo{i}", [128, 512], BF16) for i in range(2)]; r_mo = [S.res() for _ in range(2)]
                s_mo = [S.dsem(f"mxsmo{i}") for i in range(2)]
                t2 = [sb(st2, f"mxt2{i}", [128, 512], F32) for i in range(2)]; r_t2 = [S.res() for _ in range(2)]
                wav = w_ba.ap()[l].rearrange("(c p) n -> p c n", p=128)
                wsv = w_bs.ap()[l].rearrange("(c p) n -> p c n", p=128)
                its = []
                for m in range(16):
                    its.append((wav[:, :, m * 128:(m + 1) * 128], 8))
                    its.append((wsv[:, :, m * 128:(m + 1) * 128], 8))
                ws = WStream(st2, "mxb", 8, its)
                OTv = OT.ap().rearrange("c p n -> p c n")
                for j in range(8):
                    S.dma("sp", oT[:, :, j * 512:(j + 1) * 512], OTv[:, :, j * 512:(j + 1) * 512], s_o[j], writes=[r_o[j]])
                cnt = 0; gc = 0
                for m in range(16):
                    wb, r_wb = ws.next()
                    for j in range(8):
                        pb = cnt % 4; cnt += 1
                        tsl = slice(j * 512, (j + 1) * 512)
                        gi = gc % 4; gc += 1
                        S.dma("sp", gt[gi][:], GA.ap()[m][:, tsl], s_g[gi], writes=[r_g[gi]])
                        mm_acc(pb, wb, r_wb, oT, r_o, 8, tsl)
                        S.op("dve", lambda e: e.tensor_tensor(out=tmp[:, tsl], in0=PS[pb][:], in1=gt[gi][:], op=ALU.mult),
                             reads=[PSR[pb], r_g[gi]], writes=[r_t])
                    wb, r_wb = ws.next()
                    for j in range(8):
                        pb = cnt % 4; k = cnt % 2; cnt += 1
                        tsl = slice(j * 512, (j + 1) * 512)
                        gi = gc % 4; gc += 1
                        S.dma("sp", gt[gi][:], GS.ap()[m][:, tsl], s_g[gi], writes=[r_g[gi]])
                        mm_acc(pb, wb, r_wb, zT, r_z, 8, tsl)
                        S.op("dve", lambda e: e.tensor_tensor(out=t2[k][:], in0=PS[pb][:], in1=gt[gi][:], op=ALU.mult),
                             reads=[PSR[pb], r_g[gi]], writes=[r_t2[k]])
                        S.op("pool", lambda e: e.tensor_tensor(out=mo[k][:], in0=t2[k][:], in1=tmp[:, tsl], op=ALU.add),
                             reads=[r_t2[k], r_t], writes=[r_mo[k]])
                        S.dma("act", H.ap()[m][:, tsl], mo[k][:], s_mo[k], reads=[r_mo[k]])
                S.barrier()
        with ExitStack() as st:
            Ms = sb(st, "mxM", [128, 16, NT], BF16); r_M = [S.res() for _ in range(8)]; s_M = [S.dsem() for _ in range(8)]
            xt = [sb(st, f"mxx{i}", [128, 512], F32) for i in range(4)]; r_x = [S.res() for _ in range(4)]
            s_x = [S.dsem(f"mxsx{i}") for i in range(4)]
            wov = w_out.ap()[l].rearrange("(c p) n -> p c n", p=128)
            ws = WStream(st, "mxc", 16, [(wov[:, :, m * 128:(m + 1) * 128], 16) for m in range(16)])
            Hv = H.ap().rearrange("c p n -> p c n")
            for j in range(8):
                S.dma("sp", Ms[:, :, j * 512:(j + 1) * 512], Hv[:, :, j * 512:(j + 1) * 512], s_M[j], writes=[r_M[j]])
            cnt = 0
            for m in range(16):
                wb, r_wb = ws.next()
                for j in range(8):
                    pb = cnt % 4; xi = cnt % 4; cnt += 1
                    tsl = slice(j * 512, (j + 1) * 512)
                    S.dma("sp", xt[xi][:], XT.ap()[m][:, tsl], s_x[xi], writes=[r_x[xi]])
                    mm_acc(pb, wb, r_wb, Ms, r_M, 16, tsl)
                    S.op("dve", lambda e: e.tensor_tensor(out=xt[xi][:], in0=PS[pb][:], in1=xt[xi][:], op=ALU.add),
                         reads=[PSR[pb], r_x[xi]], writes=[r_x[xi]])
                    S.dma("act", XT.ap()[m][:, tsl], xt[xi][:], s_x[xi], reads=[r_x[xi]])
            S.barrier()

    def phase_ffn_up(l):
        with ExitStack() as st:
            Hs = sb(st, "fuH", [128, 16, NT], BF16); r_H = S.res()
            phase_norm(norm2_g, l, f"n2{l}", Hs, r_H)
            cw = sb(st, "fucw", [128, 3, 86], F32); cb = sb(st, "fucb", [128, 86], F32); r_cw = S.res(); s_cw = S.dsem("fuscw")
            up = [[sb(st, f"fuu{a}{i}", [128, 514], F32) for i in range(2)] for a in range(2)]
            r_up = [[S.res() for i in range(2)] for a in range(2)]
            cc = [[sb(st, f"fuc{a}{i}", [128, 512], F32) for i in range(2)] for a in range(2)]
            r_cc = [[S.res() for i in range(2)] for a in range(2)]
            ga = [sb(st, f"fug{i}", [128, 512], F32) for i in range(2)]; r_ga = [S.res() for _ in range(2)]
            ao = [sb(st, f"fuo{i}", [128, 512], BF16) for i in range(2)]; r_ao = [S.res() for _ in range(2)]
            s_ao = [S.dsem(f"fusao{i}") for i in range(2)]
            wuv = w_up.ap()[l].rearrange("(c p) n -> p c n", p=128)
            its = []
            for m in range(NFT):
                its.append((wuv[:, :, m * 128:(m + 1) * 128], 16))
                its.append((wuv[:, :, (NFT + m) * 128:(NFT + m + 1) * 128], 16))
            ws = WStream(st, "fu", 16, its, nbf=4, dist=2)
            for k3 in range(3):
                S.dma("sp", cw[:, k3, :], conv_w.ap()[l][k3].rearrange("(m p) -> p m", p=128), s_cw, writes=[r_cw],
                      allow_slow_non_contiguous=True)
            S.dma("sp", cb[:], conv_b.ap()[l].rearrange("(m p) -> p m", p=128), s_cw, writes=[r_cw], allow_slow_non_contiguous=True)
            cnt = 0
            for m in range(NFT):
                wba, r_wba = ws.next()
                wbv, r_wbv = ws.next()
                for j in range(8):
                    k = cnt % 2; k4 = cnt % 4; cnt += 1
                    tsl = slice(j * 512, (j + 1) * 512)
                    pbs = (2 * k4, 2 * k4 + 1)
                    mm_acc(pbs[0], wba, r_wba, Hs, r_H, 16, tsl)
                    mm_acc(pbs[1], wbv, r_wbv, Hs, r_H, 16, tsl)
                    for a in range(2):
                        fm = m + a * NFT
                        pb = pbs[a]
                        u_, ru_ = up[a][k], r_up[a][k]
                        if j % 4 == 0:
                            S.op("pool", lambda e: e.memset(u_[:, 0:2], 0.0), writes=[ru_])
                        else:
                            S.op("pool", lambda e: e.tensor_copy(out=u_[:, 0:2], in_=up[a][1 - k][:, 512:514]),
                                 reads=[r_up[a][1 - k]], writes=[ru_])
                        S.op("act", lambda e: e.activation(out=u_[:, 2:514], in_=PS[pb][:], func=AF.Copy), reads=[PSR[pb]], writes=[ru_])
                        S.op("act", lambda e: e.activation(out=cc[a][k][:], in_=PS[pb][:], func=AF.Identity, bias=cb[:, fm:fm + 1],
                                                           scale=cw[:, 2, fm:fm + 1]), reads=[PSR[pb], r_cw], writes=[r_cc[a][k]])
                        S.op("dve", lambda e: e.scalar_tensor_tensor(out=cc[a][k][:], in0=u_[:, 1:513], scalar=cw[:, 1, fm:fm + 1],
                                                                     in1=cc[a][k][:], op0=ALU.mult, op1=ALU.add),
                             reads=[ru_, r_cw, r_cc[a][k]], writes=[r_cc[a][k]])
                        S.op("dve", lambda e: e.scalar_tensor_tensor(out=cc[a][k][:], in0=u_[:, 0:512], scalar=cw[:, 0, fm:fm + 1],
                                                                     in1=cc[a][k][:], op0=ALU.mult, op1=ALU.add),
                             reads=[ru_, r_cw, r_cc[a][k]], writes=[r_cc[a][k]])
                    S.op("act", lambda e: e.activation(out=ga[k][:], in_=cc[0][k][:], func=AF.Gelu_apprx_tanh),
                         reads=[r_cc[0][k]], writes=[r_ga[k]])
                    S.op("dve", lambda e: e.tensor_tensor(out=ao[k][:], in0=ga[k][:], in1=cc[1][k][:], op=ALU.mult),
                         reads=[r_ga[k], r_cc[1][k]], writes=[r_ao[k]])
                    S.dma("act", AT.ap()[m][:, tsl], ao[k][:], s_ao[k], reads=[r_ao[k]])
            S.barrier()

    def phase_ffn_down(l):
        for (k0, k1) in KGROUPS:
            nk = k1 - k0
            with ExitStack() as st:
                As = sb(st, "fdA", [128, 15, NT], BF16); r_A = [S.res() for _ in range(8)]; s_A = [S.dsem() for _ in range(8)]
                xt = [sb(st, f"fdx{i}", [128, 512], F32) for i in range(4)]; r_x = [S.res() for _ in range(4)]
                s_x = [S.dsem(f"fdsx{k0}_{i}") for i in range(4)]
                wdv = w_down.ap()[l].rearrange("(c p) n -> p c n", p=128)
                ws = WStream(st, f"fd{k0}", 15, [(wdv[:, k0:k1, m * 128:(m + 1) * 128], nk) for m in range(16)])
                ATv = AT.ap().rearrange("c p n -> p c n")
                for j in range(8):
                    S.dma("sp", As[:, 0:nk, j * 512:(j + 1) * 512], ATv[:, k0:k1, j * 512:(j + 1) * 512], s_A[j], writes=[r_A[j]])
                cnt = 0
                for m in range(16):
                    wb, r_wb = ws.next()
                    for j in range(8):
                        pb = cnt % 4; xi = cnt % 4; cnt += 1
                        tsl = slice(j * 512, (j + 1) * 512)
                        S.dma("sp", xt[xi][:], XT.ap()[m][:, tsl], s_x[xi], writes=[r_x[xi]])
                        mm_acc(pb, wb, r_wb, As, r_A, nk, tsl)
                        S.op("dve", lambda e: e.tensor_tensor(out=xt[xi][:], in0=PS[pb][:], in1=xt[xi][:], op=ALU.add),
                             reads=[PSR[pb], r_x[xi]], writes=[r_x[xi]])
                        S.dma("act", XT.ap()[m][:, tsl], xt[xi][:], s_x[xi], reads=[r_x[xi]])
                S.barrier()

    def phase_transpose_out():
        with ExitStack() as st:
            xs = [sb(st, f"pox{i}", [128, 16, 512], F32) for i in range(2)]
            yo = [sb(st, f"poy{i}", [128, 4, D], F32) for i in range(2)]
            r_x = [S.res() for _ in range(2)]; r_y = [S.res() for _ in range(2)]
            s_x = [S.dsem(f"posx{i}") for i in range(2)]; s_y = [S.dsem(f"posy{i}") for i in range(2)]
            XTv = XT.ap().rearrange("c p n -> p c n")
            yv = y_out.ap().rearrange("(n s p) d -> n p s d", s=4, p=128)
            cnt = 0
            for j in range(8):
                b = j % 2
                S.dma("sp", xs[b][:], XTv[:, :, j * 512:(j + 1) * 512], s_x[b], writes=[r_x[b]])
                for s4 in range(4):
                    for c4 in range(4):
                        pb = cnt % 8; cnt += 1
                        for cc_ in range(4):
                            c = c4 * 4 + cc_
                            S.op("pe", lambda e: e.transpose(out=PS[pb][:, cc_ * 128:(cc_ + 1) * 128],
                                                             in_=xs[b][:, c, s4 * 128:(s4 + 1) * 128], identity=ident[:]),
                                 reads=[r_x[b]], writes=[PSR[pb]], sig=(cc_ == 3))
                        if cnt % 2 == 0:
                            S.op("act", lambda e: e.activation(out=yo[b][:, s4, c4 * 512:(c4 + 1) * 512], in_=PS[pb][:], func=AF.Copy),
                                 reads=[PSR[pb]], writes=[r_y[b]])
                        else:
                            S.op("dve", lambda e: e.tensor_copy(out=yo[b][:, s4, c4 * 512:(c4 + 1) * 512], in_=PS[pb][:]),
                                 reads=[PSR[pb]], writes=[r_y[b]])
                S.dma("act", yv[j], yo[b][:], s_y[b], reads=[r_y[b]])
            S.barrier()

    phase_transpose_in()
    for l in range(nlayers):
        S.rotate()
        if "ip" in PH: phase_inproj(l)
        if "ss" in PH: phase_ssm(l, phase_attn)
        if "mx" in PH: phase_mix(l)
        if "fu" in PH: phase_ffn_up(l)
        if "fd" in PH: phase_ffn_down(l)
    phase_transpose_out()
    stack.close()
    return nc


def consts():
    ident = np.eye(128, dtype=np.float32)
    p = np.arange(128)[:, None].astype(np.float64)
    f = np.arange(512)[None, :].astype(np.float64)
    al = np.zeros((128, 5, 512), np.float32)
    al[:, 0, :] = (f - p)
    for m in range(4):
        d = f - p - 128.0 * m
        vis = (p + 128 * m) < (np.floor(f / 64) + 1) * 64
        al[:, 1 + m, :] = np.where(vis, np.abs(d), 1.0e9)
    blk = np.zeros((128, 128), np.float32)
    blk[:64, :64] = 1.0
    blk[64:, 64:] = 1.0
    pf = np.zeros((128, 513), np.float32)
    pf[:, 0] = np.arange(128)
    pf[:, 1:] = np.arange(512)[None, :]
    import ml_dtypes
    t = np.arange(SEQ) % 512
    aug = np.zeros((8, 2, SEQ), np.float32)
    for h in range(8):
        sl = 2.0 ** -(h + 1)
        aug[h, 0] = -8.0 * sl * 16.0 * (t // 16)
        aug[h, 1] = -8.0 * sl * (t % 16)
    return {"c_ident": ident, "c_alibi": al, "c_blk": blk, "c_pf": pf, "c_aug": aug.astype(ml_dtypes.bfloat16)}


PARAM_NAMES = ["norm1_g", "w_in", "q_norm_g", "k_norm_g", "lambda_q1", "lambda_k1", "lambda_q2", "lambda_k2",
               "subln_g", "ssm_a_re", "ssm_a_im", "ssm_log_dt", "ssm_b_re", "ssm_b_im", "ssm_c_re", "ssm_c_im",
               "ssm_d", "ssm_glu_w", "ssm_glu_b", "w_branch_attn", "w_branch_ssm", "w_out", "norm2_g",
               "ffn_w_up", "ffn_conv_w", "ffn_conv_b", "ffn_w_down"]


def make_in_maps(inputs):
    x = np.ascontiguousarray(np.asarray(inputs["x"], dtype=np.float32))
    shared = {k: np.ascontiguousarray(np.asarray(inputs[k], dtype=np.float32)) for k in PARAM_NAMES}
    shared.update(consts())
    maps = []
    for c in range(NCORES):
        m = dict(shared)
        m["x"] = x[2 * c:2 * c + 2].reshape(NT, D)
        maps.append(m)
    return maps


def kernel(**inputs):
    nc = build_program()
    res = run_bass_kernel_spmd(nc, make_in_maps(inputs), core_ids=list(range(NCORES)))
    out = np.stack([r["y"].reshape(2, SEQ, D) for r in res.results], axis=0).reshape(16, SEQ, D)
    return out.astype(np.float32)
```
